# Optimizing a Trainium2 kernel written in Bass

```python
import math
import jax, jax.numpy as jnp
from jax import lax
import numpy as np

D_MODEL = 1024
BATCH = 32
SEQ = 2048
DEPTH = 1

CHUNK = 64
Q_BLOCK = 128
A_HEADS = 8
A_HEAD_DIM = 64
A_Q = A_HEADS * A_HEAD_DIM
A_LATENT = 128
IDX_HEADS = 8
IDX_DIM = 64
TOPK_MAX = 256
REL_BUCKETS = 32
REL_MAX_DIST = 128
B_HEADS = 4
B_KEY_DIM = 128
B_VAL_DIM = 128
B_K = B_HEADS * B_KEY_DIM
B_V = B_HEADS * B_VAL_DIM
PEER_HEADS = 8
PEER_KEYS = 128
PEER_EXPERTS = PEER_KEYS * PEER_KEYS
PEER_QDIM = 256
PEER_TOPK = 16
PEER_TOKEN_BLOCK = 512
ALPHA = (2 * DEPTH) ** 0.25
BETA = (8 * DEPTH) ** -0.25
LN_EPS = 1e-5
RMS_EPS = 1e-6

IN_SPLITS = (A_Q, A_LATENT, IDX_HEADS * IDX_DIM, IDX_DIM, IDX_HEADS, B_K, B_K, B_V, B_V, D_MODEL, D_MODEL)
IN_COLS = A_Q + A_LATENT + IDX_HEADS * IDX_DIM + IDX_DIM + IDX_HEADS + 2 * B_K + 2 * B_V + 2 * D_MODEL

kernel_name = "hybrid_dsa_hgrn2_peer_deepnorm"


def layer_norm(x, g, b):
    xf = x.astype(jnp.float32)
    mu = jnp.mean(xf, -1, keepdims=True)
    var = jnp.mean(jnp.square(xf - mu), -1, keepdims=True)
    return ((xf - mu) * lax.rsqrt(var + LN_EPS) * g + b).astype(x.dtype)


def rms_norm(x, g):
    xf = x.astype(jnp.float32)
    return (xf * lax.rsqrt(jnp.mean(xf * xf, -1, keepdims=True) + RMS_EPS) * g).astype(x.dtype)


def t5_bucket(rel):
    half = REL_BUCKETS // 2
    max_exact = half // 2
    base = jnp.where(rel > 0, half, 0)
    n = jnp.abs(rel)
    nf = jnp.maximum(n, 1).astype(jnp.float32)
    large = max_exact + (jnp.log(nf / max_exact) / math.log(REL_MAX_DIST / max_exact) * (half - max_exact)).astype(jnp.int32)
    large = jnp.minimum(large, half - 1)
    return base + jnp.where(n < max_exact, n, large)


def dsa_mixer(q, c_kv, q_idx, k_idx, w_idx, w_uk, w_uv, rel_bias):
    B, S = q.shape[0], q.shape[1]
    n_sel = min(TOPK_MAX, S // 4)
    nblk = S // Q_BLOCK
    q_lat = jnp.einsum('bshd,chd->bshc', q, w_uk)
    key_chunk = jnp.arange(S) // CHUNK
    scale = A_HEAD_DIM ** -0.5

    def to_blocks(t):
        return jnp.moveaxis(t.reshape((B, nblk, Q_BLOCK) + t.shape[2:]), 1, 0)

    def block(args):
        ql, qi, wi, qpos = args
        q_chunk = qpos // CHUNK
        logits = jnp.einsum('bthd,bsd->bths', qi, k_idx)
        score = jnp.einsum('bths,bth->bts', jax.nn.relu(logits), wi).astype(jnp.float32)
        visible = key_chunk[None, :] <= q_chunk[:, None]
        score = jnp.where(visible[None], score, -jnp.inf)
        _, idx = lax.top_k(score, n_sel)
        valid = (idx // CHUNK) <= q_chunk[None, :, None]
        c_sel = jax.vmap(lambda c, i: c[i])(c_kv, idx.reshape(B, -1)).reshape(B, Q_BLOCK, n_sel, A_LATENT)
        bias = rel_bias[t5_bucket(idx - qpos[None, :, None])]
        s = jnp.einsum('bthc,btkc->bthk', ql, c_sel).astype(jnp.float32) * scale + jnp.moveaxis(bias, -1, 2)
        s = jnp.where(valid[:, :, None, :], s, -jnp.inf)
        p = jax.nn.softmax(s, axis=-1).astype(c_sel.dtype)
        return jnp.einsum('bthk,btkc->bthc', p, c_sel)

    pos = jnp.arange(S).reshape(nblk, Q_BLOCK)
    o_lat = lax.map(block, (to_blocks(q_lat), to_blocks(q_idx), to_blocks(w_idx), pos))
    o_lat = jnp.moveaxis(o_lat, 0, 1).reshape(B, S, A_HEADS, A_LATENT)
    o = jnp.einsum('bshc,chd->bshd', o_lat, w_uv)
    return o.reshape(B, S, A_Q)


def hgrn2_mixer(q, f_logit, i, g, lb, norm_g):
    B, S, H, dk = q.shape
    dv = i.shape[-1]
    n = S // CHUNK
    f32 = jnp.float32
    lbh = lb.reshape(H, dk).astype(f32)
    z = f_logit.astype(f32)
    log_f = jnp.log(lbh + (1.0 - lbh) * jax.nn.sigmoid(z))
    k = (1.0 - lbh) * jax.nn.sigmoid(-z)

    def chunks(t):
        return t.astype(f32).reshape(B, n, CHUNK, H, t.shape[-1]).transpose(0, 3, 1, 2, 4)

    qc, kc, vc = chunks(q), chunks(k), chunks(i)
    bc = jnp.cumsum(chunks(log_f), axis=3)
    b_ref = bc[..., CHUNK // 2 - 1:CHUNK // 2, :]
    q_in = qc * jnp.exp(bc - b_ref)
    k_in = kc * jnp.exp(b_ref - bc)
    causal = jnp.tril(jnp.ones((CHUNK, CHUNK), dtype=bool))
    attn = jnp.where(causal, jnp.einsum('bhntd,bhnsd->bhnts', q_in, k_in), 0.0)
    o_intra = jnp.einsum('bhnts,bhnsv->bhntv', attn, vc)
    b_last = bc[..., -1:, :]
    dS = jnp.einsum('bhnsd,bhnsv->bhndv', kc * jnp.exp(b_last - bc), vc)
    decay = jnp.exp(b_last[..., 0, :])

    def step(state, xs):
        dS_n, dec_n = xs
        return dec_n[..., None] * state + dS_n, state

    _, S_before = lax.scan(step, jnp.zeros((B, H, dk, dv), f32), (jnp.moveaxis(dS, 2, 0), jnp.moveaxis(decay, 2, 0)))
    S_before = jnp.moveaxis(S_before, 0, 2)
    o_inter = jnp.einsum('bhntd,bhndv->bhntv', qc * jnp.exp(bc), S_before)
    o = (o_intra + o_inter).transpose(0, 2, 3, 1, 4).reshape(B, S, H, dv)
    gf = g.astype(f32)
    o = rms_norm(o, norm_g) * gf * jax.nn.sigmoid(gf)
    return o.reshape(B, S, H * dv).astype(q.dtype)


def mixer_sublayer(x, w_in, kv_norm_g, w_uk, w_uv, rel_bias, lb, b_norm_g, w_br_a, w_br_b, w_o):
    B, S, _ = x.shape
    proj = x @ w_in
    offs = np.cumsum(np.array(IN_SPLITS))[:-1].tolist()
    qa, ckv, qi, ki, wi, qb, fb, ib, gb, gate_a, gate_b = jnp.split(proj, offs, axis=-1)
    ya = dsa_mixer(qa.reshape(B, S, A_HEADS, A_HEAD_DIM), rms_norm(ckv, kv_norm_g),
                   qi.reshape(B, S, IDX_HEADS, IDX_DIM), ki, wi * (IDX_HEADS * IDX_DIM) ** -0.5,
                   w_uk, w_uv, rel_bias) @ w_br_a
    yb = hgrn2_mixer(qb.reshape(B, S, B_HEADS, B_KEY_DIM), fb.reshape(B, S, B_HEADS, B_KEY_DIM),
                     ib.reshape(B, S, B_HEADS, B_VAL_DIM), gb.reshape(B, S, B_HEADS, B_VAL_DIM),
                     lb, b_norm_g) @ w_br_b
    merged = jax.nn.sigmoid(gate_a) * ya + jax.nn.sigmoid(gate_b) * yb
    return merged @ w_o


def peer_sublayer(x, w_pq, sub_keys1, sub_keys2, u_table, v_table):
    B, S, D = x.shape
    T = B * S
    blk = math.gcd(T, PEER_TOKEN_BLOCK)

    def block(xb):
        q = (xb @ w_pq).reshape(blk, PEER_HEADS, 2, PEER_QDIM // 2)
        s1 = jnp.einsum('thd,nd->thn', q[:, :, 0], sub_keys1).astype(jnp.float32)
        s2 = jnp.einsum('thd,nd->thn', q[:, :, 1], sub_keys2).astype(jnp.float32)
        v1, i1 = lax.top_k(s1, PEER_TOPK)
        v2, i2 = lax.top_k(s2, PEER_TOPK)
        cand = (v1[..., :, None] + v2[..., None, :]).reshape(blk, PEER_HEADS, PEER_TOPK * PEER_TOPK)
        cand_id = (i1[..., :, None] * PEER_KEYS + i2[..., None, :]).reshape(blk, PEER_HEADS, PEER_TOPK * PEER_TOPK)
        top_s, pos = lax.top_k(cand, PEER_TOPK)
        ids = jnp.take_along_axis(cand_id, pos, axis=-1).reshape(blk, PEER_HEADS * PEER_TOPK)
        gate = jax.nn.softmax(top_s, axis=-1).reshape(blk, PEER_HEADS * PEER_TOPK).astype(xb.dtype)
        u = jnp.take(u_table, ids, axis=0)
        h = jax.nn.gelu(jnp.einsum('tkd,td->tk', u, xb), approximate=False)
        v = jnp.take(v_table, ids, axis=0)
        return jnp.einsum('tk,tkd->td', gate * h, v)

    out = lax.map(block, x.reshape(T // blk, blk, D))
    return out.reshape(B, S, D)


def setup_inputs(seed: int = 0) -> dict:
    key = jax.random.key(seed)
    ks = jax.random.split(key, 21)
    L = DEPTH

    def nrm(k, shape, scale):
        return jax.random.normal(k, shape, jnp.float32) * scale

    return {
        "x": nrm(ks[0], (BATCH, SEQ, D_MODEL), 1.0),
        "w_in": nrm(ks[1], (L, D_MODEL, IN_COLS), D_MODEL ** -0.5),
        "kv_norm_g": 1.0 + nrm(ks[2], (L, A_LATENT), 0.02),
        "w_uk": nrm(ks[3], (L, A_LATENT, A_HEADS, A_HEAD_DIM), A_LATENT ** -0.5),
        "w_uv": nrm(ks[4], (L, A_LATENT, A_HEADS, A_HEAD_DIM), A_LATENT ** -0.5),
        "rel_bias": nrm(ks[5], (REL_BUCKETS, A_HEADS), 0.5),
        "lb_params": 1.0 + nrm(ks[6], (DEPTH + 1, B_K), 0.1),
        "b_norm_g": 1.0 + nrm(ks[7], (L, B_VAL_DIM), 0.02),
        "w_br_a": nrm(ks[8], (L, A_Q, D_MODEL), A_Q ** -0.5 * BETA),
        "w_br_b": nrm(ks[9], (L, B_V, D_MODEL), B_V ** -0.5 * BETA),
        "w_o": nrm(ks[10], (L, D_MODEL, D_MODEL), D_MODEL ** -0.5 * BETA),
        "ln1_g": 1.0 + nrm(ks[11], (L, D_MODEL), 0.02),
        "ln1_b": nrm(ks[12], (L, D_MODEL), 0.01),
        "w_pq": nrm(ks[13], (L, D_MODEL, PEER_HEADS * PEER_QDIM), D_MODEL ** -0.5),
        "sub_keys1": nrm(ks[14], (L, PEER_KEYS, PEER_QDIM // 2), (PEER_QDIM // 2) ** -0.5),
        "sub_keys2": nrm(ks[15], (L, PEER_KEYS, PEER_QDIM // 2), (PEER_QDIM // 2) ** -0.5),
        "u_table": nrm(ks[16], (L, PEER_EXPERTS, D_MODEL), D_MODEL ** -0.5),
        "v_table": nrm(ks[17], (L, PEER_EXPERTS, D_MODEL), BETA * PEER_HEADS ** -0.5),
        "ln2_g": 1.0 + nrm(ks[18], (L, D_MODEL), 0.02),
        "ln2_b": nrm(ks[19], (L, D_MODEL), 0.01),
    }


def reference(x, w_in, kv_norm_g, w_uk, w_uv, rel_bias, lb_params, b_norm_g, w_br_a, w_br_b, w_o,
              ln1_g, ln1_b, w_pq, sub_keys1, sub_keys2, u_table, v_table, ln2_g, ln2_b):
    lower_bounds = jnp.cumsum(jax.nn.softmax(lb_params.astype(jnp.float32), axis=0), axis=0)
    for l in range(DEPTH):
        mix = mixer_sublayer(x, w_in[l], kv_norm_g[l], w_uk[l], w_uv[l], rel_bias, lower_bounds[l],
                             b_norm_g[l], w_br_a[l], w_br_b[l], w_o[l])
        x = layer_norm(ALPHA * x + mix, ln1_g[l], ln1_b[l])
        ffn = peer_sublayer(x, w_pq[l], sub_keys1[l], sub_keys2[l], u_table[l], v_table[l])
        x = layer_norm(ALPHA * x + ffn, ln2_g[l], ln2_b[l])
    return x
```

```python
import math
from contextlib import ExitStack

import numpy as np
import jax
import jax.numpy as jnp

import concourse.bass as bass
import concourse.mybir as mybir
from concourse.bass_utils import run_bass_kernel_spmd

F32 = mybir.dt.float32
BF16 = mybir.dt.bfloat16
I32 = mybir.dt.int32
U32 = mybir.dt.uint32
AF = mybir.ActivationFunctionType
ALU = mybir.AluOpType
AX = mybir.AxisListType

NCORES = 8
D = 1024
S = 2048
NT = S // 128
QA0, CKV0, QI0, KI0, WI0, QB0, FB0, IB0, GB0, GA0, GG0 = 0, 512, 640, 1152, 1216, 1224, 1736, 2248, 2760, 3272, 4296
INCOLS = 5320
ALPHA = 2.0 ** 0.25
LN_EPS = 1e-5
RMS_EPS = 1e-6
NEG = -1.0e30
NBIS = 16
KC = 2


class Prog:
    ENG = ("pe", "act", "dve", "pool", "sp")
    CAP = 30000

    def __init__(self, nc, stack):
        self.nc = nc
        self.stack = stack
        self.q = {e: [] for e in self.ENG}
        self.cur = {e: None for e in self.ENG}
        self.cnt = {e: 0 for e in self.ENG}
        self.seen = {e: {} for e in self.ENG}
        self.lastw = {}
        self.readers = {}
        self.dsem = {}
        self.nsem = 0
        self.sems = {}
        self.cap = None

    def _newsem(self, name):
        s = self.stack.enter_context(self.nc.semaphore(f"s{self.nsem}_{name}"))
        self.nsem += 1
        self.sems[id(s)] = s
        return s

    def _deps(self, eng, reads, writes):
        toks = []
        for r in reads:
            t = self.lastw.get(r)
            if t is not None:
                toks.append(t)
        for w in writes:
            t = self.lastw.get(w)
            if t is not None:
                toks.append(t)
            toks.extend(self.readers.get(w, ()))
        waits = {}
        seen = self.seen[eng]
        for (sem, val, src) in toks:
            if src == "pe" and eng == "pe":
                continue
            k = id(sem)
            if seen.get(k, 0) >= val:
                continue
            if waits.get(k, 0) < val:
                waits[k] = val
        for k, v in waits.items():
            seen[k] = v
        return [(self.sems[k], v) for k, v in waits.items()]

    def _commit(self, tok, reads, writes):
        for w in writes:
            self.lastw[w] = tok
            self.readers[w] = []
        for r in reads:
            self.readers.setdefault(r, []).append(tok)

    def replay(self, items):
        for it in items:
            if it[0] == "op":
                self.op(*it[1:])
            else:
                self.dma(*it[1:])

    def op(self, eng, fn, reads=(), writes=()):
        if self.cap is not None:
            self.cap.append(("op", eng, fn, tuple(reads), tuple(writes)))
            return None
        waits = self._deps(eng, reads, writes)
        if self.cur[eng] is None or self.cnt[eng] >= self.CAP:
            self.cur[eng] = self._newsem(eng)
            self.cnt[eng] = 0
        self.cnt[eng] += 1
        tok = (self.cur[eng], self.cnt[eng], eng)
        self.q[eng].append((fn, waits, tok[0], 1))
        self._commit(tok, reads, writes)
        return tok

    def dma(self, queue, slot, fn, reads=(), writes=()):
        if self.cap is not None:
            self.cap.append(("dma", queue, slot, fn, tuple(reads), tuple(writes)))
            return None
        waits = self._deps(queue, reads, writes)
        if slot not in self.dsem:
            self.dsem[slot] = [self._newsem("d" + str(slot)), 0]
        ent = self.dsem[slot]
        ent[1] += 16
        tok = (ent[0], ent[1], "dma")
        self.q[queue].append((fn, waits, tok[0], 16))
        self._commit(tok, reads, writes)
        return tok

    def barrier(self):
        toks = []
        for e in self.ENG:
            if self.cur[e] is not None:
                toks.append((self.cur[e], self.cnt[e], e))
        for slot, ent in self.dsem.items():
            toks.append((ent[0], ent[1], "dma"))
        for e in self.ENG:
            waits = {}
            seen = self.seen[e]
            for (sem, val, src) in toks:
                k = id(sem)
                if seen.get(k, 0) >= val:
                    continue
                waits[k] = val
                seen[k] = val
            self.q[e].append((None, [(self.sems[k], v) for k, v in waits.items()], None, 0))
        self.lastw.clear()
        self.readers.clear()

    def emit(self):
        nc = self.nc
        with nc.Block() as block:
            def run(engname):
                def f(engine):
                    for (fn, waits, sem, inc) in self.q[engname]:
                        for (s, v) in waits:
                            engine.wait_ge(s, v)
                        if fn is not None:
                            fn(engine).then_inc(sem, inc)
                return f
            block.tensor(run("pe"))
            block.scalar(run("act"))
            block.vector(run("dve"))
            block.gpsimd(run("pool"))
            block.sync(run("sp"))


def build(nseq, dbg=None):
    nc = bass.Bass("TRN2", target_bir_lowering=False)
    ntok = nseq * S

    def din(name, shape, dt=F32):
        return nc.dram_tensor(name, list(shape), dt, kind="ExternalInput").ap()

    x = din("x", [ntok, D])
    w_in = din("w_in", [D, INCOLS])
    w_br_a = din("w_br_a", [512, D])
    w_br_b = din("w_br_b", [512, D])
    w_o = din("w_o", [D, D])
    w_pq = din("w_pq", [D, 2048])
    u_table = din("u_table", [16384, D])
    v_table = din("v_table", [16384, D])
    c_ident = din("c_ident", [128, 128])
    c_cum = din("c_cum", [128, 386])
    c_negvis = din("c_negvis", [128, 128])
    c_bnear = din("c_bnear", [128, 16, 128])
    c_cb = din("c_cb", [128, 8])
    c_wukT = din("c_wukT", [128, 4, 128])
    c_wuvp = din("c_wuvp", [128, 8, 128])
    c_lbbc = din("c_lbbc", [128, 2, 512])
    c_lbfm = din("c_lbfm", [128, 2, 4])
    c_kvg = din("c_kvg", [128, 128])
    c_bng = din("c_bng", [128, 1])
    c_ln = din("c_ln", [128, 4, D])
    c_skT = din("c_skT", [128, 2, 128])
    c_iota = din("c_iota", [128, 32])
    y = nc.dram_tensor("y", [ntok, D], F32, kind="ExternalOutput").ap()
    dbg_out = None
    if dbg == "dsa" or dbg == "hgrn":
        dbg_out = nc.dram_tensor("dbg", [512, S], F32, kind="ExternalOutput").ap()
    if dbg == "h1":
        dbg_out = nc.dram_tensor("dbg", [S, D], F32, kind="ExternalOutput").ap()

    with ExitStack() as top:
        P = Prog(nc, top)

        tcount = [0]

        def T(st, name, shape, dt):
            tcount[0] += 1
            return st.enter_context(nc.sbuf_tensor(f"{name}_{tcount[0]}", list(shape), dt))

        def mm(out, lhsT, rhs, start=True, stop=True, r=(), w=()):
            P.op("pe", lambda e: e.matmul(out, lhsT=lhsT, rhs=rhs, start=start, stop=stop), r, w)

        def tr(out, in_, ident, r=(), w=()):
            P.op("pe", lambda e: e.transpose(out=out, in_=in_, identity=ident), r, w)

        def act(out, in_, func, r=(), w=(), bias=None, scale=None, accum=None):
            kw = {}
            if bias is not None:
                kw["bias"] = bias
            if scale is not None:
                kw["scale"] = scale
            if accum is not None:
                kw["accum_out"] = accum
            P.op("act", lambda e: e.activation(out=out, in_=in_, func=func, **kw), r, w)

        def ts(eng, out, in0, s1, s2, op0, op1=None, r=(), w=(), accum=None):
            kw = {}
            if op1 is not None:
                kw["op1"] = op1
            if accum is not None:
                kw["accum_out"] = accum
            P.op(eng, lambda e: e.tensor_scalar(out, in0, s1, s2, op0, **kw), r, w)

        def tt(eng, out, in0, in1, op, r=(), w=()):
            P.op(eng, lambda e: e.tensor_tensor(out, in0, in1, op), r, w)

        def stt(out, in0, scalar, in1, op0, op1, r=(), w=(), accum=None):
            kw = {}
            if accum is not None:
                kw["accum_out"] = accum
            P.op("dve", lambda e: e.scalar_tensor_tensor(out=out, in0=in0, scalar=scalar, in1=in1, op0=op0, op1=op1, **kw), r, w)

        def cp(eng, out, in_, r=(), w=()):
            if eng == "act":
                P.op("act", lambda e: e.activation(out=out, in_=in_, func=AF.Copy), r, w)
            else:
                P.op(eng, lambda e: e.tensor_copy(out=out, in_=in_), r, w)

        def red(out, in_, op, r=(), w=()):
            P.op("dve", lambda e: e.tensor_reduce(out=out, in_=in_, axis=AX.X, op=op), r, w)

        def dma(queue, slot, out, in_, r=(), w=()):
            P.dma(queue, slot, lambda e: e.dma_start(out=out, in_=in_), r, w)

        def gather(slot, out, table, idx_ap, r=(), w=()):
            P.dma("pool", slot, lambda e: e.indirect_dma_start(
                out=out, out_offset=None, in_=table,
                in_offset=bass.IndirectOffsetOnAxis(ap=idx_ap, axis=0)), r, w)

        psb = [top.enter_context(nc.psum_tensor(f"psb{i}", [128, 512], F32)) for i in range(6)]
        pst = [top.enter_context(nc.psum_tensor(f"pst{i}", [128, 1024], BF16)) for i in range(2)]
        rot = {"pool": [0, 1, 2, 3, 4, 5], "i": 0, "t": 0}

        def nb():
            b = rot["pool"][rot["i"] % len(rot["pool"])]
            rot["i"] += 1
            return psb[b], f"ps{b}"

        def nbt():
            b = rot["t"] % 2
            rot["t"] += 1
            return pst[b], f"pst{b}"

        idf = T(top, "idf", [128, 128], F32)
        idb = T(top, "idb", [128, 128], BF16)
        onesb = T(top, "onesb", [128, 128], BF16)
        cum = T(top, "cum", [128, 386], F32)
        negvis = T(top, "negvis", [128, 128], F32)
        cb = T(top, "cb", [128, 8], F32)
        wukT = T(top, "wukT", [128, 4, 128], BF16)
        wuvp = T(top, "wuvp", [128, 8, 128], BF16)
        lbt = T(top, "lbt", [128, 512], F32)
        omlt = T(top, "omlt", [128, 512], F32)
        lbf = T(top, "lbf", [128, 4], F32)
        omlf = T(top, "omlf", [128, 4], F32)
        nomlf = T(top, "nomlf", [128, 4], F32)
        kvg = T(top, "kvg", [128, 128], F32)
        bng = T(top, "bng", [128, 1], F32)
        skT = T(top, "skT", [128, 2, 128], BF16)
        iot = T(top, "iot", [128, 32], F32)
        thrc = T(top, "thrc", [128, 1], F32)
        wstg = [T(top, f"wstg{i}", [128, 8, 128], F32) for i in range(3)]
        wst = {"i": 0}

        with ExitStack() as st0:
            tmp1 = T(st0, "tmp1", [128, 16, 128], F32)
            tmp2 = T(st0, "tmp2", [128, 2, 512], F32)
            tmp3 = T(st0, "tmp3", [128, 2, 4], F32)
            dma("sp", "c0", idf[:], c_ident, w=["idf"])
            cp("dve", idb[:], idf[:], r=["idf"], w=["idb"])
            P.op("dve", lambda e: e.memset(onesb[:], 1.0), (), ["onesb"])
            P.op("dve", lambda e: e.memset(thrc[:], -1.0e29), (), ["thrc"])
            dma("sp", "c1", cum[:], c_cum, w=["cum"])
            dma("sp", "c2", negvis[:], c_negvis, w=["negvis"])
            dma("sp", "c4", cb[:], c_cb, w=["cb"])
            dma("sp", "c5", tmp1[:, 0:4, :], c_wukT, w=["tmp1"])
            cp("dve", wukT[:], tmp1[:, 0:4, :], r=["tmp1"], w=["wukT"])
            dma("sp", "c6", tmp1[:, 0:8, :], c_wuvp, w=["tmp1"])
            cp("dve", wuvp[:], tmp1[:, 0:8, :], r=["tmp1"], w=["wuvp"])
            dma("sp", "c7", tmp2[:], c_lbbc, w=["tmp2"])
            tt("dve", lbt[:], tmp2[:, 0, :], tmp2[:, 1, :], ALU.subtract, r=["tmp2"], w=["lbt"])
            act(lbt[:], lbt[:], AF.Sigmoid, r=["lbt"], w=["lbt"])
            ts("dve", omlt[:], lbt[:], -1.0, 1.0, ALU.mult, ALU.add, r=["lbt"], w=["omlt"])
            dma("sp", "c8", tmp3[:], c_lbfm, w=["tmp3"])
            tt("dve", lbf[:], tmp3[:, 0, :], tmp3[:, 1, :], ALU.subtract, r=["tmp3"], w=["lbf"])
            act(lbf[:], lbf[:], AF.Sigmoid, r=["lbf"], w=["lbf"])
            ts("dve", omlf[:], lbf[:], -1.0, 1.0, ALU.mult, ALU.add, r=["lbf"], w=["omlf"])
            ts("dve", nomlf[:], omlf[:], -1.0, None, ALU.mult, r=["omlf"], w=["nomlf"])
            dma("sp", "c9", kvg[:], c_kvg, w=["kvg"])
            dma("sp", "c10", bng[:], c_bng, w=["bng"])
            dma("sp", "c11", tmp1[:, 0:2, :], c_skT, w=["tmp1"])
            cp("dve", skT[:], tmp1[:, 0:2, :], r=["tmp1"], w=["skT"])
            dma("sp", "c12", iot[:], c_iota, w=["iot"])
            P.barrier()

        uv = nc.dram_tensor("uv_scratch", [16384, 2048], BF16).ap()

        def wload(dst, src2d, nkc, wd, key, eng="pool"):
            i = wst["i"] % 3
            wst["i"] += 1
            stg = wstg[i]
            dma("sp", f"wstg{i}", stg[:, 0:nkc, 0:wd], src2d.rearrange("(kc p) c -> p kc c", p=128), w=[f"wstg{i}"])
            cp(eng, dst, stg[:, 0:nkc, 0:wd], r=[f"wstg{i}"], w=[key])

        for b in range(nseq):
            t0 = b * S
            with ExitStack() as sq:
                xT = T(sq, "xT", [128, 8, S], BF16)
                oTa = T(sq, "oTa", [128, 4, S], BF16)
                oTb = T(sq, "oTb", [128, 4, S], BF16)
                wt = [T(sq, f"wt{i}", [128, 8, 128], BF16) for i in range(3)]
                wti = {"i": 0}

                def nwt():
                    i = wti["i"] % 3
                    wti["i"] += 1
                    return wt[i], f"wt{i}"

                with ExitStack() as sx:
                    xs = [T(sx, f"xs{i}", [128, D], F32) for i in range(2)]
                    xb = [T(sx, f"xb{i}", [128, D], BF16) for i in range(2)]
                    for i in range(NT):
                        p = i % 2
                        dma("sp", f"xs{p}", xs[p][:], x[t0 + 128 * i:t0 + 128 * (i + 1), :], w=[f"xs{p}"])
                        cp("dve" if i % 2 == 0 else "act", xb[p][:], xs[p][:], r=[f"xs{p}"], w=[f"xb{p}"])
                        pt, ptk = nbt()
                        for c in range(8):
                            tr(pt[:, c * 128:(c + 1) * 128], xb[p][:, c * 128:(c + 1) * 128], idb[:], r=[f"xb{p}", "idb"], w=[ptk])
                        cp("act" if i % 2 == 0 else "dve", xT[:, :, 128 * i:128 * (i + 1)], pt[:].rearrange("p (c t) -> p c t", c=8), r=[ptk], w=[f"xT{i // 4}"])
                    P.barrier()

                def proj_fm(col0, evac, wtile=None, wkey=None):
                    if wtile is None:
                        wtile, wkey = nwt()
                        wload(wtile[:], w_in[:, col0:col0 + 128], 8, 128, wkey)
                    for tb in range(4):
                        ps, psk = nb()
                        for kc in range(8):
                            mm(ps[:, :], wtile[:, kc, :], xT[:, kc, tb * 512:(tb + 1) * 512], start=(kc == 0), stop=(kc == 7), r=[wkey, f"xT{tb}"], w=[psk])
                        evac(tb, ps, psk)

                with ExitStack() as sa:
                    qlat = T(sa, "qlat", [128, 8, S], BF16)
                    qiT = T(sa, "qiT", [128, 4, S], BF16)
                    qaT = [T(sa, f"qaT{i}", [128, S], BF16) for i in range(2)]
                    kiT = T(sa, "kiT", [128, S], BF16)
                    ckv = T(sa, "ckv", [128, NT, 128], BF16)
                    ckvT = T(sa, "ckvT", [128, S], BF16)
                    wia = T(sa, "wia", [128, NT, 8], F32)
                    wck = T(sa, "wck", [128, 8, 136], BF16)
                    sc = T(sa, "sc", [128, S], F32)
                    rl = [T(sa, f"rl{i}", [128, 512], F32) for i in range(2)]
                    nm = [T(sa, f"nm{i}", [128, S], BF16) for i in range(2)]
                    pT = [T(sa, f"pT{i}", [128, S], BF16) for i in range(2)]
                    olat = [T(sa, f"olat{i}", [128, 8, 128], BF16) for i in range(2)]
                    sm = T(sa, "sm", [128, 16], F32)
                    rzs = [T(sa, f"rzs{i}", [128, 128], F32) for i in range(2)]
                    ssq = T(sa, "ssq", [128, 2], F32)
                    cjunk = T(sa, "cjunk", [128, 128], F32)
                    P.op("dve", lambda e: e.memset(ssq[:], 0.0), (), ["ssq0", "ssq1"])
                    P.op("dve", lambda e: e.memset(sm[:], 0.0), (), ["sm0", "sm1", "sm2", "sm3", "sm4", "sm5"])
                    bpp = T(sa, "bpp", [128, 16, 128], F32)
                    dma("sp", "c3", bpp[:], c_bnear, w=["bpp"])
                    for dh in range(16):
                        ts("dve", bpp[:, dh, :], bpp[:, dh, :], cb[:, dh % 8:dh % 8 + 1], 8.0, ALU.subtract, ALU.mult, r=["bpp", "cb"], w=["bpp"])

                    for m in range(4):
                        qa_t = qaT[m % 2]
                        qk = f"qaT{m % 2}"

                        def ev_qa(tb, ps, psk, qa_t=qa_t, qk=qk):
                            cp("act" if tb % 2 == 0 else "dve", qa_t[:, tb * 512:(tb + 1) * 512], ps[:, :], r=[psk], w=[qk])
                        proj_fm(QA0 + m * 128, ev_qa)
                        for hl in range(2):
                            h = 2 * m + hl
                            for tb in range(4):
                                ps, psk = nb()
                                mm(ps[:, :], wukT[64 * hl:64 * hl + 64, m, :], qa_t[64 * hl:64 * hl + 64, tb * 512:(tb + 1) * 512], r=["wukT", qk], w=[psk])
                                cp("act" if tb % 2 == 1 else "dve", qlat[:, h, tb * 512:(tb + 1) * 512], ps[:, :], r=[psk], w=[f"qlat{h}"])
                    for m in range(4):
                        def ev_qi(tb, ps, psk, m=m):
                            cp("act" if tb % 2 == 0 else "dve", qiT[:, m, tb * 512:(tb + 1) * 512], ps[:, :], r=[psk], w=[f"qiT{m}"])
                        proj_fm(QI0 + m * 128, ev_qi)
                    wk_t, wk_k = nwt()
                    wload(wk_t[:, :, 0:64], w_in[:, KI0:KI0 + 64], 8, 64, wk_k)
                    wload(wk_t[:, :, 64:128], w_in[:, KI0:KI0 + 64], 8, 64, wk_k)

                    def ev_ki(tb, ps, psk):
                        cp("act" if tb % 2 == 0 else "dve", kiT[:, tb * 512:(tb + 1) * 512], ps[:, :], r=[psk], w=["kiT"])
                    proj_fm(None, ev_ki, wtile=wk_t, wkey=wk_k)
                    wload(wck[:, :, 0:128], w_in[:, CKV0:CKV0 + 128], 8, 128, "wck")
                    wload(wck[:, :, 128:136], w_in[:, WI0:WI0 + 8], 8, 8, "wck")
                    for i in range(NT):
                        ps, psk = nb()
                        for kc in range(8):
                            mm(ps[:, 0:136], xT[:, kc, 128 * i:128 * (i + 1)], wck[:, kc, :], start=(kc == 0), stop=(kc == 7), r=["wck", f"xT{i // 4}"], w=[psk])
                        act(cjunk[:], ps[:, 0:128], AF.Square, r=[psk], w=["ssq0"], accum=ssq[:, 0:1])
                        act(ssq[:, 1:2], ssq[:, 0:1], AF.Ln, r=["ssq0"], w=["ssq1"], bias=RMS_EPS, scale=1.0 / 128.0)
                        act(ssq[:, 1:2], ssq[:, 1:2], AF.Exp, r=["ssq1"], w=["ssq1"], scale=-0.5)
                        stt(ckv[:, i, :], ps[:, 0:128], ssq[:, 1:2], kvg[:], ALU.mult, ALU.mult, r=[psk, "ssq1", "kvg"], w=[f"ckv{i}"])
                        cp("act", wia[:, i, :], ps[:, 128:136], r=[psk], w=[f"wia{i}"])
                        pt, ptk = nbt()
                        tr(pt[:, 0:128], ckv[:, i, :], idb[:], r=[f"ckv{i}", "idb"], w=[ptk])
                        cp("dve", ckvT[:, 128 * i:128 * (i + 1)], pt[:, 0:128], r=[ptk], w=[f"ckvT{i}"])

                    sck = [f"sc{kb}" for kb in range(4)]

                    def IDX(j):
                        nv = 128 * (j + 1)
                        nlo = 128 * j + 64
                        for h in range(8):
                            hl, m = h % 2, h // 2
                            for kb in range((nv + 511) // 512):
                                c0 = kb * 512
                                cw = min(512, nv - c0)
                                ps, psk = nb()
                                mm(ps[:, 0:cw], qiT[64 * hl:64 * hl + 64, m, 128 * j:128 * (j + 1)], kiT[64 * hl:64 * hl + 64, c0:c0 + cw], r=[f"qiT{m}", "kiT"], w=[psk])
                                ri = (h * 4 + kb) % 2
                                act(rl[ri][:, 0:cw], ps[:, 0:cw], AF.Relu, r=[psk], w=[f"rl{ri}"])
                                if h == 0:
                                    ts("dve", sc[:, c0:c0 + cw], rl[ri][:, 0:cw], wia[:, j, 0:1], None, ALU.mult, r=[f"rl{ri}", f"wia{j}"], w=[f"sc{kb}"])
                                else:
                                    stt(sc[:, c0:c0 + cw], rl[ri][:, 0:cw], wia[:, j, h:h + 1], sc[:, c0:c0 + cw], ALU.mult, ALU.add, r=[f"rl{ri}", f"wia{j}", f"sc{kb}"], w=[f"sc{kb}"])
                        tt("dve", sc[:, 128 * j:128 * (j + 1)], sc[:, 128 * j:128 * (j + 1)], negvis[:], ALU.add, r=sck + ["negvis"], w=sck)
                        if j >= 2:
                            red(sm[:, 0:1], sc[:, 0:nlo], ALU.min, r=sck, w=["sm0"])
                            red(sm[:, 1:2], sc[:, 0:nv], ALU.max, r=sck, w=["sm1"])
                            tt("dve", sm[:, 2:3], sm[:, 1:2], sm[:, 0:1], ALU.subtract, r=["sm0", "sm1"], w=["sm2"])
                            stt(sm[:, 3:4], sm[:, 2:3], -0.5, sm[:, 0:1], ALU.mult, ALU.subtract, r=["sm2", "sm0"], w=["sm3"])

                    def BIS(j, k):
                        nv = 128 * (j + 1)
                        f = 2.0 ** -(k + 1)
                        nmn, nmnk = nm[j % 2], f"nm{j % 2}"
                        act(nmn[:, 0:nv], sc[:, 0:nv], AF.Sign, r=sck + ["sm3"], w=[nmnk, "sm4"], bias=sm[:, 3:4], accum=sm[:, 4:5])
                        stt(sm[:, 5:6], sm[:, 4:5], 510.5 - nv, sm[:, 2:3], ALU.is_ge, ALU.mult, r=["sm4", "sm2"], w=["sm5"])
                        stt(sm[:, 0:1], sm[:, 5:6], f, sm[:, 0:1], ALU.mult, ALU.add, r=["sm5", "sm0"], w=["sm0"])
                        if k + 1 < NBIS:
                            stt(sm[:, 3:4], sm[:, 2:3], -0.5 * f, sm[:, 0:1], ALU.mult, ALU.subtract, r=["sm2", "sm0"], w=["sm3"])

                    def NM(j):
                        nv = 128 * (j + 1)
                        thr, thrk = (sm[:, 0:1], "sm0") if j >= 2 else (thrc[:, 0:1], "thrc")
                        ts("dve", nm[j % 2][:, 0:nv], sc[:, 0:nv], thr, -32768.0, ALU.is_lt, ALU.mult, r=sck + [thrk], w=[f"nm{j % 2}"])

                    def ATT_qk(j, h):
                        nmj, nmk = nm[j % 2], f"nm{j % 2}"
                        pj = pT[h % 2]
                        pk = f"pT{h % 2}"
                        for g in range((j + 4) // 4):
                            ps, psk = nb()
                            tiles = list(range(4 * g, min(4 * g + 4, j + 1)))
                            for ii_, i in enumerate(tiles):
                                o_ = ps[:, ii_ * 128:(ii_ + 1) * 128]
                                near = (i >= j - 1)
                                mm(o_, ckvT[:, 128 * i:128 * (i + 1)], qlat[:, h, 128 * j:128 * (j + 1)], start=True, stop=False, r=[f"ckvT{i}", f"qlat{h}"], w=[psk])
                                mm(o_, nmj[:, 128 * i:128 * (i + 1)], idb[:], start=False, stop=(not near), r=[nmk, "idb"], w=[psk])
                                if near:
                                    mm(o_, idf[:], bpp[:, (j - i) * 8 + h, :], start=False, stop=True, r=["idf", "bpp"], w=[psk])
                            ncol = 128 * len(tiles)
                            act(pj[:, 512 * g:512 * g + ncol], ps[:, 0:ncol], AF.Exp, r=[psk, "cb"], w=[pk], bias=cb[:, h:h + 1], scale=0.125)

                    def ATT_pv(j, h):
                        ol, olk = olat[j % 2], f"olat{j % 2}"
                        pj = pT[h % 2]
                        pk = f"pT{h % 2}"
                        ps, psk = nb()
                        for i in range(j + 1):
                            mm(ps[:, 0:128], ckv[:, i, :], pj[:, 128 * i:128 * (i + 1)], start=(i == 0), stop=(i == j), r=[f"ckv{i}", pk], w=[psk])
                        for i in range(j + 1):
                            mm(ps[:, 128:256], onesb[:], pj[:, 128 * i:128 * (i + 1)], start=(i == 0), stop=(i == j), r=["onesb", pk], w=[psk])
                        rz = rzs[h % 2]
                        rzk = f"rzs{h % 2}"
                        P.op("dve", lambda e, rz=rz, ps=ps: e.reciprocal(out=rz[:], in_=ps[:, 128:256]), [psk], [rzk])
                        tt("dve", ol[:, h, :], ps[:, 0:128], rz[:], ALU.mult, r=[psk, rzk], w=[olk])

                    def ATT_tail(j):
                        ol, olk = olat[j % 2], f"olat{j % 2}"
                        for m in range(4):
                            ps, psk = nb()
                            mm(ps[:, 0:128], wuvp[:, 2 * m, :], ol[:, 2 * m, :], start=True, stop=False, r=["wuvp", olk], w=[psk])
                            mm(ps[:, 0:128], wuvp[:, 2 * m + 1, :], ol[:, 2 * m + 1, :], start=False, stop=True, r=["wuvp", olk], w=[psk])
                            cp("act", oTa[:, m, 128 * j:128 * (j + 1)], ps[:, 0:128], r=[psk], w=[f"oTa{j // 4}"])

                    IDX(0)
                    NM(0)
                    for j in range(NT):
                        nxt = j + 1
                        if nxt < NT:
                            IDX(nxt)
                        ATT_qk(j, 0)
                        for h in range(8):
                            if h + 1 < 8:
                                ATT_qk(j, h + 1)
                            ATT_pv(j, h)
                            if nxt < NT and nxt >= 2:
                                for k in range(NBIS * h // 8, NBIS * (h + 1) // 8):
                                    BIS(nxt, k)
                        if nxt < NT:
                            NM(nxt)
                        ATT_tail(j)
                    P.barrier()

                if dbg == "dsa":
                    with ExitStack() as sd:
                        dtmp = T(sd, "dtmp", [128, 4, S], F32)
                        cp("dve", dtmp[:], oTa[:], w=["dtmp"])
                        dma("sp", "dbg", dbg_out.rearrange("(m p) t -> p m t", p=128), dtmp[:], r=["dtmp"])
                        P.barrier()
                    break

                with ExitStack() as sb:
                    qbT = T(sb, "qbT", [128, 2, S], BF16)
                    kT = T(sb, "kT", [128, 2, S], BF16)
                    sgT = T(sb, "sgT", [128, 2, S], BF16)
                    logf = T(sb, "logf", [128, NT, 256], F32)
                    kk = T(sb, "kk", [128, NT, 256], BF16)
                    iib = T(sb, "iib", [128, NT, 256], BF16)
                    w5 = [T(sb, f"w5{i}", [128, 8, 256], BF16) for i in range(2)]
                    sgtmp = [T(sb, f"sgtmp{i}", [128, 512], F32) for i in range(2)]
                    ftmp = [T(sb, f"ftmp{i}", [128, 256], F32) for i in range(2)]
                    e14 = [T(sb, f"e14{i}", [128, 256], F32) for i in range(4)]
                    e2 = [T(sb, f"e2{i}", [128, 128], F32) for i in range(4)]
                    e3 = [T(sb, f"e3{i}", [128, 128], F32) for i in range(4)]
                    dec = [T(sb, f"dec{i}", [128, 2], F32) for i in range(4)]
                    qin = [T(sb, f"qin{i}", [128, 128], BF16) for i in range(4)]
                    q2 = [T(sb, f"q2{i}", [128, 128], F32) for i in range(4)]
                    kin = [T(sb, f"kin{i}", [128, 128], BF16) for i in range(4)]
                    k3 = [T(sb, f"k3{i}", [128, 128], BF16) for i in range(4)]
                    am = [T(sb, f"am{i}", [128, 128], BF16) for i in range(4)]
                    s32 = [[T(sb, f"s32{i}{k}", [128, 128], F32) for k in range(2)] for i in range(2)]
                    sbf = [[T(sb, f"sbf{i}{k}", [128, 128], BF16) for k in range(2)] for i in range(2)]
                    sqb = [T(sb, f"sq{i}", [128, 128], BF16) for i in range(2)]
                    sd_ = [T(sb, f"sd{i}", [128, 128], F32) for i in range(2)]
                    o1 = [T(sb, f"o1{i}", [128, 128], F32) for i in range(2)]
                    tbl = {"r": 0}
                    if b == 0:
                        ust = [T(sb, f"ust{i}", [128, 2, D], F32) for i in range(2)]
                        ubt = [T(sb, f"ubt{i}", [128, 2048], BF16) for i in range(2)]

                    def tbl_step():
                        if b != 0 or tbl["r"] >= 128:
                            return
                        r_ = tbl["r"]
                        tbl["r"] += 1
                        p = r_ % 2
                        dma("sp", f"ustu{p}", ust[p][:, 0, :], u_table[128 * r_:128 * (r_ + 1), :], w=[f"ustu{p}"])
                        dma("sp", f"ustv{p}", ust[p][:, 1, :], v_table[128 * r_:128 * (r_ + 1), :], w=[f"ustv{p}"])
                        cp("act", ubt[p][:, 0:1024], ust[p][:, 0, :], r=[f"ustu{p}"], w=[f"ubtu{p}"])
                        cp("dve", ubt[p][:, 1024:2048], ust[p][:, 1, :], r=[f"ustv{p}"], w=[f"ubtv{p}"])
                        dma("sp", f"ubt{p}", uv[128 * r_:128 * (r_ + 1), :], ubt[p][:], r=[f"ubtu{p}", f"ubtv{p}"], w=["uvtab"])
                    for hp in range(2):
                        for hl in range(2):
                            h = 2 * hp + hl

                            def ev_q(tb, ps, psk, hl=hl):
                                cp("act" if tb % 2 == 0 else "dve", qbT[:, hl, tb * 512:(tb + 1) * 512], ps[:, :], r=[psk], w=[f"qbT{hl}"])
                            proj_fm(QB0 + h * 128, ev_q)

                            def ev_f(tb, ps, psk, hl=hl, h=h):
                                sg = sgtmp[tb % 2]
                                sgk = f"sgtmp{tb % 2}"
                                act(sg[:], ps[:, :], AF.Sigmoid, r=[psk], w=[sgk])
                                ts("dve", kT[:, hl, tb * 512:(tb + 1) * 512], sg[:], nomlf[:, h:h + 1], omlf[:, h:h + 1], ALU.mult, ALU.add, r=[sgk], w=[f"kT{hl}"])
                            proj_fm(FB0 + h * 128, ev_f)

                            def ev_g(tb, ps, psk, hl=hl):
                                act(sgT[:, hl, tb * 512:(tb + 1) * 512], ps[:, :], AF.Silu, r=[psk], w=[f"sgT{hl}"])
                            proj_fm(GB0 + h * 128, ev_g)
                        c2 = 2 * hp * 128
                        wload(w5[0][:, :, 0:128], w_in[:, FB0 + c2:FB0 + c2 + 128], 8, 128, "w50")
                        wload(w5[0][:, :, 128:256], w_in[:, FB0 + c2 + 128:FB0 + c2 + 256], 8, 128, "w50")
                        wload(w5[1][:, :, 0:128], w_in[:, IB0 + c2:IB0 + c2 + 128], 8, 128, "w51")
                        wload(w5[1][:, :, 128:256], w_in[:, IB0 + c2 + 128:IB0 + c2 + 256], 8, 128, "w51")
                        for i in range(NT):
                            ps, psk = nb()
                            for kc in range(8):
                                mm(ps[:, 0:256], xT[:, kc, 128 * i:128 * (i + 1)], w5[0][:, kc, :], start=(kc == 0), stop=(kc == 7), r=["w50", f"xT{i // 4}"], w=[psk])
                            for kc in range(8):
                                mm(ps[:, 256:512], xT[:, kc, 128 * i:128 * (i + 1)], w5[1][:, kc, :], start=(kc == 0), stop=(kc == 7), r=["w51", f"xT{i // 4}"], w=[psk])
                            ft = ftmp[i % 2]
                            fk = f"ftmp{i % 2}"
                            act(ft[:], ps[:, 0:256], AF.Sigmoid, r=[psk], w=[fk])
                            tt("dve", ft[:], ft[:], omlt[:, c2:c2 + 256], ALU.mult, r=[fk, "omlt"], w=[fk])
                            tt("dve", logf[:, i, :], ft[:], lbt[:, c2:c2 + 256], ALU.add, r=[fk, "lbt"], w=[f"logf{i}"])
                            ts("dve", kk[:, i, :], logf[:, i, :], -1.0, 1.0, ALU.mult, ALU.add, r=[f"logf{i}"], w=[f"kk{i}"])
                            cp("act", iib[:, i, :], ps[:, 256:512], r=[psk], w=[f"iib{i}"])
                        lfk = [f"logf{i}" for i in range(NT)]
                        for q4 in range(4):
                            act(logf[:, 4 * q4:4 * q4 + 4, :], logf[:, 4 * q4:4 * q4 + 4, :], AF.Ln, r=lfk[4 * q4:4 * q4 + 4], w=lfk[4 * q4:4 * q4 + 4])
                        for hl in range(2):
                            P.op("dve", lambda e, hl=hl: e.memset(s32[hl][0][:], 0.0), (), [f"s32{hl}0"])
                        rot["pool"] = [0, 1, 2, 3]
                        pob = {(0, 0): (psb[4], "ps4"), (0, 1): (psb[5], "ps5"),
                               (1, 0): (pst[0][:].bitcast(F32), "pst0"), (1, 1): (pst[1][:].bitcast(F32), "pst1")}

                        def HA(i, hl):
                            q = 2 * hl + (i % 2)
                            sfx = f"{hl}{i % 2}"
                            tsl = slice(128 * i, 128 * (i + 1))
                            lf = logf[:, i, hl * 128:(hl + 1) * 128]
                            pe_, pek = nb()
                            mm(pe_[:, 0:128], lf, cum[:, 0:128], r=[f"logf{i}", "cum"], w=[pek])
                            mm(pe_[:, 128:256], lf, cum[:, 128:256], r=[f"logf{i}", "cum"], w=[pek])
                            mm(pe_[:, 256:258], lf, cum[:, 256:258], r=[f"logf{i}", "cum"], w=[pek])
                            mm(pe_[:, 384:512], cum[:, 258:386], lf, r=[f"logf{i}", "cum"], w=[pek])
                            act(e14[q][:], pe_[:, 0:256], AF.Exp, r=[pek], w=[f"e14{sfx}"])
                            act(e2[q][:], pe_[:, 0:128], AF.Exp, r=[pek], w=[f"e2{sfx}"], scale=-1.0)
                            act(dec[q][:], pe_[:, 256:258], AF.Exp, r=[pek], w=[f"dec{sfx}"])
                            act(e3[q][:], pe_[:, 384:512], AF.Exp, r=[pek], w=[f"e3{sfx}"])
                            tt("dve", qin[q][:], qbT[:, hl, tsl], e14[q][:, 0:128], ALU.mult, r=[f"qbT{hl}", f"e14{sfx}"], w=[f"qin{sfx}"])
                            tt("dve", q2[q][:], qbT[:, hl, tsl], e14[q][:, 128:256], ALU.mult, r=[f"qbT{hl}", f"e14{sfx}"], w=[f"q2{sfx}"])
                            tt("dve", kin[q][:], kT[:, hl, tsl], e2[q][:], ALU.mult, r=[f"kT{hl}", f"e2{sfx}"], w=[f"kin{sfx}"])
                            tt("dve", k3[q][:], kk[:, i, hl * 128:(hl + 1) * 128], e3[q][:], ALU.mult, r=[f"kk{i}", f"e3{sfx}"], w=[f"k3{sfx}"])
                            pa, pak = nb()
                            mm(pa[:, 0:128], kin[q][:], qin[q][:], r=[f"kin{sfx}", f"qin{sfx}"], w=[pak])
                            tt("dve", am[q][:], pa[:, 0:128], cum[:, 128:256], ALU.mult, r=[pak, "cum"], w=[f"am{sfx}"])
                            po, pok = pob[(hl, i % 2)]
                            iv = iib[:, i, hl * 128:(hl + 1) * 128]
                            mm(po[:, 0:128], iv, am[q][:], start=True, stop=False, r=[f"iib{i}", f"am{sfx}"], w=[pok])

                        def HB(i):
                            tsl = slice(128 * i, 128 * (i + 1))
                            for _ in range(4):
                                tbl_step()
                            for c in range(2):
                                cur, nxt = c, 1 - c
                                for hl in range(2):
                                    q = 2 * hl + (i % 2)
                                    sfx = f"{hl}{i % 2}"
                                    po, pok = pob[(hl, i % 2)]
                                    mm(po[:, 64 * c:64 * c + 64], s32[hl][cur][:], q2[q][:, 64 * c:64 * c + 64], start=False, stop=(c == 1), r=[f"s32{hl}{cur}", f"q2{sfx}"], w=[pok])
                                    pss, pssk = nb()
                                    mm(pss[:, 0:128], k3[q][64 * c:64 * c + 64, :], iib[64 * c:64 * c + 64, i, hl * 128:(hl + 1) * 128], r=[f"k3{sfx}", f"iib{i}"], w=[pssk])
                                    stt(s32[hl][nxt][:], s32[hl][cur][:], dec[q][:, c:c + 1], pss[:, 0:128], ALU.mult, ALU.add, r=[f"s32{hl}{cur}", f"dec{sfx}", pssk], w=[f"s32{hl}{nxt}"])
                            for hl in range(2):
                                h = 2 * hp + hl
                                po, pok = pob[(hl, i % 2)]
                                act(sqb[hl][:], po[:, 0:128], AF.Square, r=[pok], w=[f"sq{hl}"])
                                mm(po[:, 128:256], onesb[:], sqb[hl][:], r=["onesb", f"sq{hl}"], w=[pok])
                                act(sd_[hl][:], po[:, 128:256], AF.Ln, r=[pok], w=[f"sd{hl}"], bias=RMS_EPS, scale=1.0 / 128.0)
                                act(sd_[hl][:], sd_[hl][:], AF.Exp, r=[f"sd{hl}"], w=[f"sd{hl}"], scale=-0.5)
                                tt("dve", o1[hl][:], po[:, 0:128], sd_[hl][:], ALU.mult, r=[pok, f"sd{hl}"], w=[f"o1{hl}"])
                                stt(oTb[:, h, tsl], o1[hl][:], bng[:, 0:1], sgT[:, hl, tsl], ALU.mult, ALU.mult, r=[f"o1{hl}", "bng", f"sgT{hl}"], w=[f"oTb{i // 4}"])

                        HA(0, 0)
                        HA(0, 1)
                        for i in range(NT):
                            if i + 1 < NT:
                                HA(i + 1, 0)
                                HA(i + 1, 1)
                            HB(i)
                        rot["pool"] = [0, 1, 2, 3, 4, 5]
                        P.barrier()

                if dbg == "hgrn":
                    with ExitStack() as sd:
                        dtmp = T(sd, "dtmp", [128, 4, S], F32)
                        cp("dve", dtmp[:], oTb[:], w=["dtmp"])
                        dma("sp", "dbg", dbg_out.rearrange("(m p) t -> p m t", p=128), dtmp[:], r=["dtmp"])
                        P.barrier()
                    break

                mgT = T(sq, "mgT", [128, 8, S], BF16)
                wpq = T(sq, "wpq", [128, 8, 2048], BF16)
                wo = T(sq, "wo", [128, 8, D], BF16)
                pre = [("wo", wo, w_o, c) for c in range(8)] + [("wpq", wpq, w_pq, c) for c in range(16)]
                with ExitStack() as sc_:
                    wbr = [T(sc_, f"wbr{i}", [128, 4, 128], BF16) for i in range(2)]
                    sga = [T(sc_, f"sga{i}", [128, 512], F32) for i in range(2)]
                    t1 = [T(sc_, f"t1{i}", [128, 512], F32) for i in range(2)]
                    for m in range(8):
                        wa, wak = nwt()
                        wload(wa[:], w_in[:, GA0 + m * 128:GA0 + (m + 1) * 128], 8, 128, wak)
                        wg, wgk = nwt()
                        wload(wg[:], w_in[:, GG0 + m * 128:GG0 + (m + 1) * 128], 8, 128, wgk)
                        wload(wbr[0][:], w_br_a[:, m * 128:(m + 1) * 128], 4, 128, "wbr0")
                        wload(wbr[1][:], w_br_b[:, m * 128:(m + 1) * 128], 4, 128, "wbr1")
                        for tb in range(4):
                            cs = slice(tb * 512, (tb + 1) * 512)
                            for br in range(2):
                                wgt, wgtk = (wa, wak) if br == 0 else (wg, wgk)
                                src = oTa if br == 0 else oTb
                                srck = f"oTa{tb}" if br == 0 else f"oTb{tb}"
                                ps, psk = nb()
                                for kc in range(8):
                                    mm(ps[:, :], wgt[:, kc, :], xT[:, kc, cs], start=(kc == 0), stop=(kc == 7), r=[wgtk, f"xT{tb}"], w=[psk])
                                act(sga[br][:], ps[:, :], AF.Sigmoid, r=[psk], w=[f"sga{br}"])
                                ps2, ps2k = nb()
                                for kc in range(4):
                                    mm(ps2[:, :], wbr[br][:, kc, :], src[:, kc, cs], start=(kc == 0), stop=(kc == 3), r=[f"wbr{br}", srck], w=[ps2k])
                                tt("dve", t1[br][:], ps2[:, :], sga[br][:], ALU.mult, r=[ps2k, f"sga{br}"], w=[f"t1{br}"])
                            tt("dve", mgT[:, m, cs], t1[0][:], t1[1][:], ALU.add, r=["t10", "t11"], w=[f"mgT{tb}"])
                        for _ in range(3):
                            key_, dst_, src_, c_ = pre.pop(0)
                            wload(dst_[:, :, c_ * 128:(c_ + 1) * 128], src_[:, c_ * 128:(c_ + 1) * 128], 8, 128, key_, eng=("act", "dve")[len(pre) % 2])
                    P.barrier()

                with ExitStack() as sd:
                    lnp = oTa[:].bitcast(F32)
                    xr, xrk = wstg[0][:].rearrange("p a b -> p (a b)"), "wstg0"
                    ot, otk = wstg[1][:].rearrange("p a b -> p (a b)"), "wstg1"
                    rr2, rr2k = wstg[2][:].rearrange("p a b -> p (a b)"), "wstg2"
                    pjunk = wt[0][:].rearrange("p a b -> p (a b)")
                    h1T, h1Tk = wt[1], "wt1"
                    h1 = [xT[:, 6 + i, :].bitcast(F32) for i in range(2)]
                    h1b = [T(sd, "h1b0", [128, D], BF16)[:], wt[2][:].rearrange("p a b -> p (a b)")]
                    h1bk = ["h1b0", "wt2"]
                    ssb = T(sd, "ssb", [128, 16, 128], F32)
                    rr = ssb[:, 0:8, :].rearrange("p a b -> p (a b)")
                    RRK = ["ssb0", "ssb1"]
                    m8 = T(sd, "m8", [128, 16, 16], F32)
                    ix = T(sd, "ix", [128, 16, 16], U32)
                    ixf = T(sd, "ixf", [128, 16, 16], F32)
                    cand = T(sd, "cand", [128, 8, 256], F32)
                    qpT = cand[:].bitcast(BF16)[:, 0:4, :].rearrange("p a (c t) -> p (a c) t", t=128)
                    CANDK = [f"cand{h}" for h in range(8)]
                    big = ssb[:].rearrange("p (h a) b -> p h (a b)", h=8)
                    BIGK = ["ssb0", "ssb1", "ssb2", "ssb3"]
                    t8 = T(sd, "t8", [128, 8, 16], F32)
                    px = T(sd, "px", [128, 8, 16], U32)
                    pf = T(sd, "pf", [128, 8, 16], F32)
                    pa_ = T(sd, "pa", [128, 8, 16], F32)
                    pb_ = T(sd, "pb", [128, 8, 16], F32)
                    i1s = T(sd, "i1s", [128, 8, 16], F32)
                    i2s = T(sd, "i2s", [128, 8, 16], F32)
                    idsf = T(sd, "idsf", [128, 128], F32)
                    ids = [T(sd, f"ids{i}", [128, 128], I32) for i in range(2)]
                    gate = [T(sd, f"gate{i}", [128, 8, 16], F32) for i in range(2)]
                    zz = T(sd, "zz", [128, 8], F32)
                    hd = T(sd, "hd", [128, 128], F32)
                    gh = T(sd, "gh", [128, 128], F32)
                    st6 = T(sd, "st6", [128, 2, 12], F32)
                    mv = T(sd, "mv", [128, 2, 4], F32)
                    gg = T(sd, "gg", [128, 128], F32)
                    uvb = [oTb[:, s_, :] for s_ in range(4)] + [xT[:, s_, :] for s_ in range(6)]
                    NSL = len(uvb)
                    dk = [T(sd, f"dk{i}", [128, 128], BF16) for i in range(4)]
                    P.op("dve", lambda e: e.memset(hd[:], 0.0), (), ["hdinit"])
                    P.op("dve", lambda e: e.memset(mv[:], 0.0), (), ["mv0", "mv20", "mv30", "mv1", "mv21", "mv31"])
                    dma("sp", "lnp", lnp[:], c_ln, w=["lnp"])
                    P.barrier()
                    rot["pool"] = [0, 1, 2, 3]
                    pf0, pf1 = psb[4], psb[5]

                    def layer_norm(dst, src, srck, gi, dstk):
                        q_ = gi // 2
                        s6, m4 = st6[:, q_, :], mv[:, q_, :]
                        ka, kb_, km, km2, km3 = f"st6a{q_}", f"st6b{q_}", f"mv{q_}", f"mv2{q_}", f"mv3{q_}"
                        srcl = list(srck) if isinstance(srck, (list, tuple)) else [srck]
                        P.op("dve", lambda e: e.bn_stats(out=s6[:, 0:6], in_=src[:, 0:512]), srcl, [ka])
                        P.op("dve", lambda e: e.bn_stats(out=s6[:, 6:12], in_=src[:, 512:1024]), srcl, [kb_])
                        P.op("dve", lambda e: e.bn_aggr(out=m4[:, 0:2], in_=s6), [ka, kb_], [km])
                        act(m4[:, 2:3], m4[:, 1:2], AF.Sqrt, r=[km], w=[km2], bias=LN_EPS)
                        P.op("dve", lambda e: e.reciprocal(out=m4[:, 3:4], in_=m4[:, 2:3]), [km2], [km3])
                        ts("dve", dst, src, m4[:, 0:1], m4[:, 3:4], ALU.subtract, ALU.mult, r=srcl + [km, km3], w=[dstk])
                        tt("dve", dst, dst, lnp[:, gi, :], ALU.mult, r=[dstk, "lnp"], w=[dstk])
                        tt("dve", dst, dst, lnp[:, gi + 1, :], ALU.add, r=[dstk, "lnp"], w=[dstk])

                    def front(i):
                        p = i % 2
                        tsl = slice(128 * i, 128 * (i + 1))
                        hk, hbk, idk, gk = f"h1{p}", h1bk[p], f"ids{p}", f"gate{p}"
                        dma("sp", xrk, xr, x[t0 + 128 * i:t0 + 128 * (i + 1), :], w=[xrk])
                        for nbk in range(2):
                            ps, psk = nb()
                            for kc in range(8):
                                mm(ps[:, :], mgT[:, kc, tsl], wo[:, kc, nbk * 512:(nbk + 1) * 512], start=(kc == 0), stop=(kc == 7), r=[f"mgT{i // 4}", "wo"], w=[psk])
                            stt(rr[:, nbk * 512:(nbk + 1) * 512], xr[:, nbk * 512:(nbk + 1) * 512], ALPHA, ps[:, :], ALU.mult, ALU.add, r=[xrk, psk], w=RRK)
                        layer_norm(h1[p], rr, RRK, 0, hk)
                        if dbg == "h1":
                            dma("sp", "dbgh1", dbg_out[128 * i:128 * (i + 1), :], h1[p], r=[hk])
                        cp("act", h1b[p], h1[p], r=[hk], w=[hbk])
                        pt, ptk = nbt()
                        for c in range(8):
                            tr(pt[:, c * 128:(c + 1) * 128], h1b[p][:, c * 128:(c + 1) * 128], idb[:], r=[hbk, "idb"], w=[ptk])
                        cp("act", h1T[:], pt[:].rearrange("p (c t) -> p c t", c=8), r=[ptk], w=[h1Tk])
                        for g in range(4):
                            ps, psk = nb()
                            for q_ in range(4):
                                ct = 4 * g + q_
                                for kc in range(8):
                                    mm(ps[:, q_ * 128:(q_ + 1) * 128], wpq[:, kc, ct * 128:(ct + 1) * 128], h1T[:, kc, :], start=(kc == 0), stop=(kc == 7), r=["wpq", h1Tk], w=[psk])
                            cp("act", qpT[:, 4 * g:4 * g + 4, :], ps[:, :].rearrange("p (c t) -> p c t", c=4), r=[psk], w=CANDK)
                        for g in range(4):
                            ps, psk = nb()
                            for q_ in range(4):
                                ct = 4 * g + q_
                                mm(ps[:, q_ * 128:(q_ + 1) * 128], qpT[:, ct, :], skT[:, ct % 2, :], r=CANDK + ["skT"], w=[psk])
                            cp("act", ssb[:, 4 * g:4 * g + 4, :], ps[:, :].rearrange("p (c t) -> p c t", c=4), r=[psk], w=[f"ssb{g}"])
                        for ph in range(5):
                            for ct in range(16):
                                g = ct // 4
                                if ph == 0:
                                    P.op("dve", lambda e, ct=ct: e.max(out=m8[:, ct, 0:8], in_=ssb[:, ct, :]), [f"ssb{g}"], [f"m8a{ct}"])
                                elif ph == 1:
                                    P.op("dve", lambda e, ct=ct: e.max_index(out=ix[:, ct, 0:8], in_max=m8[:, ct, 0:8], in_values=ssb[:, ct, :]), [f"ssb{g}", f"m8a{ct}"], [f"ixa{ct}"])
                                elif ph == 2:
                                    P.op("dve", lambda e, ct=ct: e.match_replace(out=ssb[:, ct, :], in_to_replace=m8[:, ct, 0:8], in_values=ssb[:, ct, :], imm_value=NEG), [f"ssb{g}", f"m8a{ct}", f"ixa{ct}"], [f"ssb{g}"])
                                elif ph == 3:
                                    P.op("dve", lambda e, ct=ct: e.max(out=m8[:, ct, 8:16], in_=ssb[:, ct, :]), [f"ssb{g}"], [f"m8b{ct}"])
                                else:
                                    P.op("dve", lambda e, ct=ct: e.max_index(out=ix[:, ct, 8:16], in_max=m8[:, ct, 8:16], in_values=ssb[:, ct, :]), [f"ssb{g}", f"m8b{ct}"], [f"ixb{ct}"])
                        m8k = [f"m8a{ct}" for ct in range(16)] + [f"m8b{ct}" for ct in range(16)]
                        ixk = [f"ixa{ct}" for ct in range(16)] + [f"ixb{ct}" for ct in range(16)]
                        cp("dve", ixf[:], ix[:], r=ixk, w=["ixf"])
                        m8v = m8[:].rearrange("p (h two) a -> p h two a", two=2)
                        ixv = ixf[:].rearrange("p (h two) a -> p h two a", two=2)
                        candv = cand[:].rearrange("p h (a b) -> p h a b", a=16)
                        bigv = big[:].rearrange("p h (a b) -> p h a b", a=16)
                        tt("dve", candv, m8v[:, :, 0, :].unsqueeze(3).to_broadcast([128, 8, 16, 16]), m8v[:, :, 1, :].unsqueeze(2).to_broadcast([128, 8, 16, 16]), ALU.add, r=m8k, w=[f"cand{h}" for h in range(8)])
                        for ph in range(5):
                            for h in range(8):
                                if ph == 0:
                                    P.op("dve", lambda e, h=h: e.max(out=t8[:, h, 0:8], in_=cand[:, h, :]), [f"cand{h}"], [f"t8a{h}"])
                                elif ph == 1:
                                    P.op("dve", lambda e, h=h: e.max_index(out=px[:, h, 0:8], in_max=t8[:, h, 0:8], in_values=cand[:, h, :]), [f"cand{h}", f"t8a{h}"], [f"pxa{h}"])
                                elif ph == 2:
                                    P.op("dve", lambda e, h=h: e.match_replace(out=cand[:, h, :], in_to_replace=t8[:, h, 0:8], in_values=cand[:, h, :], imm_value=NEG), [f"cand{h}", f"t8a{h}", f"pxa{h}"], [f"cand{h}"])
                                elif ph == 3:
                                    P.op("dve", lambda e, h=h: e.max(out=t8[:, h, 8:16], in_=cand[:, h, :]), [f"cand{h}"], [f"t8b{h}"])
                                else:
                                    P.op("dve", lambda e, h=h: e.max_index(out=px[:, h, 8:16], in_max=t8[:, h, 8:16], in_values=cand[:, h, :]), [f"cand{h}", f"t8b{h}"], [f"pxb{h}"])
                        t8k = [f"t8a{h}" for h in range(8)] + [f"t8b{h}" for h in range(8)]
                        pxk = [f"pxa{h}" for h in range(8)] + [f"pxb{h}" for h in range(8)]
                        cp("dve", pf[:], px[:], r=pxk, w=["pf"])
                        tt("dve", bigv, pf[:].unsqueeze(3).to_broadcast([128, 8, 16, 16]), iot[:, 16:32].unsqueeze(1).unsqueeze(1).to_broadcast([128, 8, 16, 16]), ALU.is_ge, r=["pf", "iot"], w=BIGK)
                        red(pa_[:], bigv, ALU.add, r=BIGK, w=["pa"])
                        stt(pb_[:], pa_[:], -16.0, pf[:], ALU.mult, ALU.add, r=["pa", "pf"], w=["pb"])
                        io16 = iot[:, 0:16].unsqueeze(1).unsqueeze(1).to_broadcast([128, 8, 16, 16])
                        for (src_, sel, selk, part) in ((pa_, i1s, "i1s", 0), (pb_, i2s, "i2s", 1)):
                            tt("dve", bigv, src_[:].unsqueeze(3).to_broadcast([128, 8, 16, 16]), io16, ALU.is_equal, r=["pa", "pb", "iot"], w=BIGK)
                            tt("dve", bigv, bigv, ixv[:, :, part, :].unsqueeze(2).to_broadcast([128, 8, 16, 16]), ALU.mult, r=BIGK + ["ixf"], w=BIGK)
                            red(sel[:], bigv, ALU.add, r=BIGK, w=[selk])
                        stt(idsf[:].rearrange("p (h j) -> p h j", h=8), i1s[:], 128.0, i2s[:], ALU.mult, ALU.add, r=["i1s", "i2s"], w=["idsf"])
                        ts("dve", idsf[:], idsf[:], 0.0, 16383.0, ALU.max, ALU.min, r=["idsf"], w=["idsf"])
                        cp("dve", ids[p][:], idsf[:], r=["idsf"], w=[idk])
                        tt("dve", gate[p][:], t8[:], t8[:, :, 0:1].to_broadcast([128, 8, 16]), ALU.subtract, r=t8k, w=[gk])
                        act(gate[p][:], gate[p][:], AF.Exp, r=[gk], w=[gk])
                        red(zz[:], gate[p][:], ALU.add, r=[gk], w=["zz"])
                        P.op("dve", lambda e: e.reciprocal(out=zz[:], in_=zz[:]), ["zz"], ["zz"])
                        tt("dve", gate[p][:], gate[p][:], zz[:].unsqueeze(2).to_broadcast([128, 8, 16]), ALU.mult, r=[gk, "zz"], w=[gk])

                    def back(i):
                        p = i % 2
                        hk, hbk, idk, gk = f"h1{p}", h1bk[p], f"ids{p}", f"gate{p}"
                        gatef = gate[p][:].rearrange("p h j -> p (h j)")
                        for k in range(129):
                            if k < 128:
                                s_ = k % NSL
                                gather(f"uv{s_}", uvb[s_], uv[:, :], ids[p][:, k:k + 1], r=[idk], w=[f"uv{s_}"])
                                stt(pjunk, uvb[s_][:, 0:1024], 1.0, h1b[p], ALU.mult, ALU.mult, r=[f"uv{s_}", hbk, "hdinit"], w=[f"hd{k % 8}"], accum=hd[:, k:k + 1])
                                act(gh[:, k:k + 1], hd[:, k:k + 1], AF.Gelu, r=[f"hd{k % 8}"], w=[f"gh{k % 8}"])
                            if k >= 1:
                                k1 = k - 1
                                s_ = k1 % NSL
                                di = k1 % 4
                                act(gg[:, k1:k1 + 1], gh[:, k1:k1 + 1], AF.Copy, r=[f"gh{k1 % 8}", gk], w=[f"gg{k1 % 8}"], scale=gatef[:, k1:k1 + 1])
                                act(dk[di][:], idb[:], AF.Copy, r=["idb", f"gg{k1 % 8}"], w=[f"dk{di}"], scale=gg[:, k1:k1 + 1])
                                mm(pf0[:, :], dk[di][:], uvb[s_][:, 1024:1536], start=(k1 == 0), stop=(k1 == 127), r=[f"dk{di}", f"uv{s_}"], w=["pf0"])
                                mm(pf1[:, :], dk[di][:], uvb[s_][:, 1536:2048], start=(k1 == 0), stop=(k1 == 127), r=[f"dk{di}", f"uv{s_}"], w=["pf1"])
                        stt(rr2[:, 0:512], h1[p][:, 0:512], ALPHA, pf0[:, :], ALU.mult, ALU.add, r=[hk, "pf0"], w=[rr2k])
                        stt(rr2[:, 512:1024], h1[p][:, 512:1024], ALPHA, pf1[:, :], ALU.mult, ALU.add, r=[hk, "pf1"], w=[rr2k])
                        layer_norm(ot, rr2, rr2k, 2, otk)
                        dma("sp", otk, y[t0 + 128 * i:t0 + 128 * (i + 1), :], ot, r=[otk])

                    def capture(fn_, *a):
                        P.cap = []
                        fn_(*a)
                        lst = P.cap
                        P.cap = None
                        return lst

                    P.replay(capture(front, 0))
                    for i in range(NT):
                        bl = capture(back, i)
                        fl = capture(front, i + 1) if i + 1 < NT else []
                        merged = []
                        fi = 0
                        nb_ = max(1, len(bl) - 60)
                        for bi, it in enumerate(bl):
                            merged.append(it)
                            want = min(len(fl), (len(fl) * (bi + 1)) // nb_)
                            while fi < want:
                                merged.append(fl[fi])
                                fi += 1
                        merged.extend(fl[fi:])
                        P.replay(merged)
                    rot["pool"] = [0, 1, 2, 3, 4, 5]
                    P.barrier()
        P.barrier()
        P.emit()
    return nc


def _t5_bucket_np(rel):
    half = 16
    max_exact = 8
    rel = jnp.asarray(rel, jnp.int32)
    base = jnp.where(rel > 0, half, 0)
    n = jnp.abs(rel)
    nf = jnp.maximum(n, 1).astype(jnp.float32)
    large = max_exact + (jnp.log(nf / max_exact) / math.log(128 / max_exact) * (half - max_exact)).astype(jnp.int32)
    large = jnp.minimum(large, half - 1)
    return np.asarray(base + jnp.where(n < max_exact, n, large))


def host_consts(inp):
    f32 = np.float32
    c = {}
    c["c_ident"] = np.eye(128, dtype=f32)
    s = np.arange(128)[:, None]
    t = np.arange(128)[None, :]
    same = (s // 64) == (t // 64)
    a3 = (same & (s <= t)).astype(f32)
    ref = (same & ((s % 64) <= 31)).astype(f32)
    a1 = a3 - ref
    a2 = same.astype(f32) - a3
    ind = np.stack([(np.arange(128) // 64 == 0), (np.arange(128) // 64 == 1)], axis=1).astype(f32)
    c["c_cum"] = np.concatenate([a1, a3, ind, a2], axis=1).astype(f32)
    tq = np.arange(128)[:, None]
    sk = np.arange(128)[None, :]
    c["c_negvis"] = np.where((tq < 64) & (sk >= 64), NEG, 0.0).astype(f32)
    with jax.default_device(jax.devices("cpu")[0]):
        kk_ = np.arange(128)[:, None]
        tt_ = np.arange(128)[None, :]
        bk = [_t5_bucket_np(kk_ - tt_ - 128 * d) for d in range(2)]
    rb = np.asarray(inp["rel_bias"], f32)
    bn = np.zeros((128, 16, 128), f32)
    for d in range(2):
        g = rb[bk[d]]
        bn[:, d * 8:(d + 1) * 8, :] = np.transpose(g, (0, 2, 1))
    c["c_bnear"] = bn
    c["c_cb"] = np.broadcast_to(rb[15][None, :], (128, 8)).astype(f32).copy()
    wuk = np.asarray(inp["w_uk"][0], f32)
    c["c_wukT"] = np.ascontiguousarray(np.transpose(wuk, (1, 2, 0)).reshape(4, 128, 128).transpose(1, 0, 2))
    wuv = np.asarray(inp["w_uv"][0], f32)
    wp = np.zeros((128, 8, 128), f32)
    for h in range(8):
        wp[:, h, (h % 2) * 64:(h % 2) * 64 + 64] = wuv[:, h, :]
    c["c_wuvp"] = wp
    lbp = np.asarray(inp["lb_params"], f32)
    c["c_lbbc"] = np.broadcast_to(lbp[None], (128, 2, 512)).astype(f32).copy()
    c["c_lbfm"] = np.ascontiguousarray(lbp.reshape(2, 4, 128).transpose(2, 0, 1))
    c["c_kvg"] = np.broadcast_to(np.asarray(inp["kv_norm_g"][0], f32)[None, :], (128, 128)).copy()
    c["c_bng"] = np.asarray(inp["b_norm_g"][0], f32).reshape(128, 1).copy()
    ln = np.stack([inp["ln1_g"][0], inp["ln1_b"][0], inp["ln2_g"][0], inp["ln2_b"][0]], axis=0).astype(f32)
    c["c_ln"] = np.broadcast_to(ln[None], (128, 4, D)).copy()
    c["c_skT"] = np.ascontiguousarray(np.stack([np.asarray(inp["sub_keys1"][0], f32).T, np.asarray(inp["sub_keys2"][0], f32).T], axis=1))
    io = np.concatenate([np.arange(16), 16 * (np.arange(16) + 1)]).astype(f32)
    c["c_iota"] = np.broadcast_to(io[None, :], (128, 32)).copy()
    return c


def make_in_maps(inp, ncores, nseq):
    c = host_consts(inp)
    shared = dict(c)
    shared["w_in"] = np.ascontiguousarray(inp["w_in"][0], dtype=np.float32)
    shared["w_br_a"] = np.ascontiguousarray(inp["w_br_a"][0], dtype=np.float32)
    shared["w_br_b"] = np.ascontiguousarray(inp["w_br_b"][0], dtype=np.float32)
    shared["w_o"] = np.ascontiguousarray(inp["w_o"][0], dtype=np.float32)
    shared["w_pq"] = np.ascontiguousarray(inp["w_pq"][0], dtype=np.float32)
    shared["u_table"] = np.ascontiguousarray(inp["u_table"][0], dtype=np.float32)
    shared["v_table"] = np.ascontiguousarray(inp["v_table"][0], dtype=np.float32)
    maps = []
    xx = np.asarray(inp["x"], dtype=np.float32)
    for ci in range(ncores):
        m = dict(shared)
        m["x"] = np.ascontiguousarray(xx[ci * nseq:(ci + 1) * nseq].reshape(nseq * S, D))
        maps.append(m)
    return maps


def kernel(**inputs):
    nseq = 32 // NCORES
    nc = build(nseq)
    maps = make_in_maps(inputs, NCORES, nseq)
    res = run_bass_kernel_spmd(nc, maps, core_ids=list(range(NCORES)))
    out = np.concatenate([r["y"].reshape(nseq, S, D) for r in res.results], axis=0)
    return out.astype(np.float32)
```

```python
import math
from contextlib import ExitStack

import numpy as np
import jax
import jax.numpy as jnp

import concourse.bass as bass
import concourse.mybir as mybir
from concourse.bass_utils import run_bass_kernel_spmd

F32 = mybir.dt.float32
BF16 = mybir.dt.bfloat16
I32 = mybir.dt.int32
U32 = mybir.dt.uint32
AF = mybir.ActivationFunctionType
ALU = mybir.AluOpType
AX = mybir.AxisListType

NCORES = 8
D = 1024
S = 2048
NT = S // 128
QA0, CKV0, QI0, KI0, WI0, QB0, FB0, IB0, GB0, GA0, GG0 = 0, 512, 640, 1152, 1216, 1224, 1736, 2248, 2760, 3272, 4296
INCOLS = 5320
ALPHA = 2.0 ** 0.25
LN_EPS = 1e-5
RMS_EPS = 1e-6
NEG = -1.0e30
NBIS = 16
KC = 2


class Prog:
    ENG = ("pe", "act", "dve", "pool", "sp")
    CAP = 30000

    def __init__(self, nc, stack):
        self.nc = nc
        self.stack = stack
        self.q = {e: [] for e in self.ENG}
        self.cur = {e: None for e in self.ENG}
        self.cnt = {e: 0 for e in self.ENG}
        self.seen = {e: {} for e in self.ENG}
        self.lastw = {}
        self.readers = {}
        self.dsem = {}
        self.nsem = 0
        self.sems = {}
        self.cap = None

    def _newsem(self, name):
        s = self.stack.enter_context(self.nc.semaphore(f"s{self.nsem}_{name}"))
        self.nsem += 1
        self.sems[id(s)] = s
        return s

    def _deps(self, eng, reads, writes):
        toks = []
        for r in reads:
            t = self.lastw.get(r)
            if t is not None:
                toks.append(t)
        for w in writes:
            t = self.lastw.get(w)
            if t is not None:
                toks.append(t)
            toks.extend(self.readers.get(w, ()))
        waits = {}
        seen = self.seen[eng]
        for (sem, val, src) in toks:
            if src == "pe" and eng == "pe":
                continue
            k = id(sem)
            if seen.get(k, 0) >= val:
                continue
            if waits.get(k, 0) < val:
                waits[k] = val
        for k, v in waits.items():
            seen[k] = v
        return [(self.sems[k], v) for k, v in waits.items()]

    def _commit(self, tok, reads, writes):
        for w in writes:
            self.lastw[w] = tok
            self.readers[w] = []
        for r in reads:
            self.readers.setdefault(r, []).append(tok)

    def replay(self, items):
        for it in items:
            if it[0] == "op":
                self.op(*it[1:])
            else:
                self.dma(*it[1:])

    def op(self, eng, fn, reads=(), writes=()):
        if self.cap is not None:
            self.cap.append(("op", eng, fn, tuple(reads), tuple(writes)))
            return None
        waits = self._deps(eng, reads, writes)
        if self.cur[eng] is None or self.cnt[eng] >= self.CAP:
            self.cur[eng] = self._newsem(eng)
            self.cnt[eng] = 0
        self.cnt[eng] += 1
        tok = (self.cur[eng], self.cnt[eng], eng)
        self.q[eng].append((fn, waits, tok[0], 1))
        self._commit(tok, reads, writes)
        return tok

    def dma(self, queue, slot, fn, reads=(), writes=()):
        if self.cap is not None:
            self.cap.append(("dma", queue, slot, fn, tuple(reads), tuple(writes)))
            return None
        waits = self._deps(queue, reads, writes)
        if slot not in self.dsem:
            self.dsem[slot] = [self._newsem("d" + str(slot)), 0]
        ent = self.dsem[slot]
        ent[1] += 16
        tok = (ent[0], ent[1], "dma")
        self.q[queue].append((fn, waits, tok[0], 16))
        self._commit(tok, reads, writes)
        return tok

    def barrier(self):
        toks = []
        for e in self.ENG:
            if self.cur[e] is not None:
                toks.append((self.cur[e], self.cnt[e], e))
        for slot, ent in self.dsem.items():
            toks.append((ent[0], ent[1], "dma"))
        for e in self.ENG:
            waits = {}
            seen = self.seen[e]
            for (sem, val, src) in toks:
                k = id(sem)
                if seen.get(k, 0) >= val:
                    continue
                waits[k] = val
                seen[k] = val
            self.q[e].append((None, [(self.sems[k], v) for k, v in waits.items()], None, 0))
        self.lastw.clear()
        self.readers.clear()

    def emit(self):
        nc = self.nc
        with nc.Block() as block:
            def run(engname):
                def f(engine):
                    for (fn, waits, sem, inc) in self.q[engname]:
                        for (s, v) in waits:
                            engine.wait_ge(s, v)
                        if fn is not None:
                            fn(engine).then_inc(sem, inc)
                return f
            block.tensor(run("pe"))
            block.scalar(run("act"))
            block.vector(run("dve"))
            block.gpsimd(run("pool"))
            block.sync(run("sp"))


def build(nseq, dbg=None):
    nc = bass.Bass("TRN2", target_bir_lowering=False)
    ntok = nseq * S

    def din(name, shape, dt=F32):
        return nc.dram_tensor(name, list(shape), dt, kind="ExternalInput").ap()

    x = din("x", [ntok, D])
    w_in = din("w_in", [D, INCOLS])
    w_br_a = din("w_br_a", [512, D])
    w_br_b = din("w_br_b", [512, D])
    w_o = din("w_o", [D, D])
    w_pq = din("w_pq", [D, 2048])
    u_table = din("u_table", [16384, D])
    v_table = din("v_table", [16384, D])
    c_ident = din("c_ident", [128, 128])
    c_cum = din("c_cum", [128, 386])
    c_negvis = din("c_negvis", [128, 128])
    c_bnear = din("c_bnear", [128, 16, 128])
    c_cb = din("c_cb", [128, 8])
    c_wukT = din("c_wukT", [128, 4, 128])
    c_wuvp = din("c_wuvp", [128, 8, 128])
    c_lbbc = din("c_lbbc", [128, 2, 512])
    c_lbfm = din("c_lbfm", [128, 2, 4])
    c_kvg = din("c_kvg", [128, 128])
    c_bng = din("c_bng", [128, 1])
    c_ln = din("c_ln", [128, 4, D])
    c_skT = din("c_skT", [128, 2, 128])
    c_iota = din("c_iota", [128, 32])
    y = nc.dram_tensor("y", [ntok, D], F32, kind="ExternalOutput").ap()
    dbg_out = None
    if dbg == "dsa" or dbg == "hgrn":
        dbg_out = nc.dram_tensor("dbg", [512, S], F32, kind="ExternalOutput").ap()
    if dbg == "h1":
        dbg_out = nc.dram_tensor("dbg", [S, D], F32, kind="ExternalOutput").ap()

    with ExitStack() as top:
        P = Prog(nc, top)

        tcount = [0]

        def T(st, name, shape, dt):
            tcount[0] += 1
            return st.enter_context(nc.sbuf_tensor(f"{name}_{tcount[0]}", list(shape), dt))

        def mm(out, lhsT, rhs, start=True, stop=True, r=(), w=()):
            P.op("pe", lambda e: e.matmul(out, lhsT=lhsT, rhs=rhs, start=start, stop=stop), r, w)

        def tr(out, in_, ident, r=(), w=()):
            P.op("pe", lambda e: e.transpose(out=out, in_=in_, identity=ident), r, w)

        def act(out, in_, func, r=(), w=(), bias=None, scale=None, accum=None):
            kw = {}
            if bias is not None:
                kw["bias"] = bias
            if scale is not None:
                kw["scale"] = scale
            if accum is not None:
                kw["accum_out"] = accum
            P.op("act", lambda e: e.activation(out=out, in_=in_, func=func, **kw), r, w)

        def ts(eng, out, in0, s1, s2, op0, op1=None, r=(), w=(), accum=None):
            kw = {}
            if op1 is not None:
                kw["op1"] = op1
            if accum is not None:
                kw["accum_out"] = accum
            P.op(eng, lambda e: e.tensor_scalar(out, in0, s1, s2, op0, **kw), r, w)

        def tt(eng, out, in0, in1, op, r=(), w=()):
            P.op(eng, lambda e: e.tensor_tensor(out, in0, in1, op), r, w)

        def stt(out, in0, scalar, in1, op0, op1, r=(), w=(), accum=None):
            kw = {}
            if accum is not None:
                kw["accum_out"] = accum
            P.op("dve", lambda e: e.scalar_tensor_tensor(out=out, in0=in0, scalar=scalar, in1=in1, op0=op0, op1=op1, **kw), r, w)

        def cp(eng, out, in_, r=(), w=()):
            if eng == "act":
                P.op("act", lambda e: e.activation(out=out, in_=in_, func=AF.Copy), r, w)
            else:
                P.op(eng, lambda e: e.tensor_copy(out=out, in_=in_), r, w)

        def red(out, in_, op, r=(), w=()):
            P.op("dve", lambda e: e.tensor_reduce(out=out, in_=in_, axis=AX.X, op=op), r, w)

        def dma(queue, slot, out, in_, r=(), w=()):
            P.dma(queue, slot, lambda e: e.dma_start(out=out, in_=in_), r, w)

        def gather(slot, out, table, idx_ap, r=(), w=()):
            P.dma("pool", slot, lambda e: e.indirect_dma_start(
                out=out, out_offset=None, in_=table,
                in_offset=bass.IndirectOffsetOnAxis(ap=idx_ap, axis=0)), r, w)

        psb = [top.enter_context(nc.psum_tensor(f"psb{i}", [128, 512], F32)) for i in range(6)]
        pst = [top.enter_context(nc.psum_tensor(f"pst{i}", [128, 1024], BF16)) for i in range(2)]
        rot = {"pool": [0, 1, 2, 3, 4, 5], "i": 0, "t": 0}

        def nb():
            b = rot["pool"][rot["i"] % len(rot["pool"])]
            rot["i"] += 1
            return psb[b], f"ps{b}"

        def nbt():
            b = rot["t"] % 2
            rot["t"] += 1
            return pst[b], f"pst{b}"

        idf = T(top, "idf", [128, 128], F32)
        idb = T(top, "idb", [128, 128], BF16)
        onesb = T(top, "onesb", [128, 128], BF16)
        cum = T(top, "cum", [128, 386], F32)
        negvis = T(top, "negvis", [128, 128], F32)
        cb = T(top, "cb", [128, 8], F32)
        wukT = T(top, "wukT", [128, 4, 128], BF16)
        wuvp = T(top, "wuvp", [128, 8, 128], BF16)
        lbt = T(top, "lbt", [128, 512], F32)
        omlt = T(top, "omlt", [128, 512], F32)
        lbf = T(top, "lbf", [128, 4], F32)
        omlf = T(top, "omlf", [128, 4], F32)
        nomlf = T(top, "nomlf", [128, 4], F32)
        kvg = T(top, "kvg", [128, 128], F32)
        bng = T(top, "bng", [128, 1], F32)
        skT = T(top, "skT", [128, 2, 128], BF16)
        iot = T(top, "iot", [128, 32], F32)
        thrc = T(top, "thrc", [128, 1], F32)
        wstg = [T(top, f"wstg{i}", [128, 8, 128], F32) for i in range(3)]
        wst = {"i": 0}

        with ExitStack() as st0:
            tmp1 = T(st0, "tmp1", [128, 16, 128], F32)
            tmp2 = T(st0, "tmp2", [128, 2, 512], F32)
            tmp3 = T(st0, "tmp3", [128, 2, 4], F32)
            dma("sp", "c0", idf[:], c_ident, w=["idf"])
            cp("dve", idb[:], idf[:], r=["idf"], w=["idb"])
            P.op("dve", lambda e: e.memset(onesb[:], 1.0), (), ["onesb"])
            P.op("dve", lambda e: e.memset(thrc[:], -1.0e29), (), ["thrc"])
            dma("sp", "c1", cum[:], c_cum, w=["cum"])
            dma("sp", "c2", negvis[:], c_negvis, w=["negvis"])
            dma("sp", "c4", cb[:], c_cb, w=["cb"])
            dma("sp", "c5", tmp1[:, 0:4, :], c_wukT, w=["tmp1"])
            cp("dve", wukT[:], tmp1[:, 0:4, :], r=["tmp1"], w=["wukT"])
            dma("sp", "c6", tmp1[:, 0:8, :], c_wuvp, w=["tmp1"])
            cp("dve", wuvp[:], tmp1[:, 0:8, :], r=["tmp1"], w=["wuvp"])
            dma("sp", "c7", tmp2[:], c_lbbc, w=["tmp2"])
            tt("dve", lbt[:], tmp2[:, 0, :], tmp2[:, 1, :], ALU.subtract, r=["tmp2"], w=["lbt"])
            act(lbt[:], lbt[:], AF.Sigmoid, r=["lbt"], w=["lbt"])
            ts("dve", omlt[:], lbt[:], -1.0, 1.0, ALU.mult, ALU.add, r=["lbt"], w=["omlt"])
            dma("sp", "c8", tmp3[:], c_lbfm, w=["tmp3"])
            tt("dve", lbf[:], tmp3[:, 0, :], tmp3[:, 1, :], ALU.subtract, r=["tmp3"], w=["lbf"])
            act(lbf[:], lbf[:], AF.Sigmoid, r=["lbf"], w=["lbf"])
            ts("dve", omlf[:], lbf[:], -1.0, 1.0, ALU.mult, ALU.add, r=["lbf"], w=["omlf"])
            ts("dve", nomlf[:], omlf[:], -1.0, None, ALU.mult, r=["omlf"], w=["nomlf"])
            dma("sp", "c9", kvg[:], c_kvg, w=["kvg"])
            dma("sp", "c10", bng[:], c_bng, w=["bng"])
            dma("sp", "c11", tmp1[:, 0:2, :], c_skT, w=["tmp1"])
            cp("dve", skT[:], tmp1[:, 0:2, :], r=["tmp1"], w=["skT"])
            dma("sp", "c12", iot[:], c_iota, w=["iot"])
            P.barrier()

        uv = nc.dram_tensor("uv_scratch", [16384, 2048], BF16).ap()

        def wload(dst, src2d, nkc, wd, key, eng="pool"):
            i = wst["i"] % 3
            wst["i"] += 1
            stg = wstg[i]
            dma("sp", f"wstg{i}", stg[:, 0:nkc, 0:wd], src2d.rearrange("(kc p) c -> p kc c", p=128), w=[f"wstg{i}"])
            cp(eng, dst, stg[:, 0:nkc, 0:wd], r=[f"wstg{i}"], w=[key])

        for b in range(nseq):
            t0 = b * S
            with ExitStack() as sq:
                xT = T(sq, "xT", [128, 8, S], BF16)
                oTa = T(sq, "oTa", [128, 4, S], BF16)
                oTb = T(sq, "oTb", [128, 4, S], BF16)
                wt = [T(sq, f"wt{i}", [128, 8, 128], BF16) for i in range(3)]
                wti = {"i": 0}

                def nwt():
                    i = wti["i"] % 3
                    wti["i"] += 1
                    return wt[i], f"wt{i}"

                with ExitStack() as sx:
                    xs = [T(sx, f"xs{i}", [128, D], F32) for i in range(2)]
                    xb = [T(sx, f"xb{i}", [128, D], BF16) for i in range(2)]
                    for i in range(NT):
                        p = i % 2
                        dma("sp", f"xs{p}", xs[p][:], x[t0 + 128 * i:t0 + 128 * (i + 1), :], w=[f"xs{p}"])
                        cp("dve" if i % 2 == 0 else "act", xb[p][:], xs[p][:], r=[f"xs{p}"], w=[f"xb{p}"])
                        pt, ptk = nbt()
                        for c in range(8):
                            tr(pt[:, c * 128:(c + 1) * 128], xb[p][:, c * 128:(c + 1) * 128], idb[:], r=[f"xb{p}", "idb"], w=[ptk])
                        cp("act" if i % 2 == 0 else "dve", xT[:, :, 128 * i:128 * (i + 1)], pt[:].rearrange("p (c t) -> p c t", c=8), r=[ptk], w=[f"xT{i // 4}"])
                    P.barrier()

                def proj_fm(col0, evac, wtile=None, wkey=None):
                    if wtile is None:
                        wtile, wkey = nwt()
                        wload(wtile[:], w_in[:, col0:col0 + 128], 8, 128, wkey)
                    for tb in range(4):
                        ps, psk = nb()
                        for kc in range(8):
                            mm(ps[:, :], wtile[:, kc, :], xT[:, kc, tb * 512:(tb + 1) * 512], start=(kc == 0), stop=(kc == 7), r=[wkey, f"xT{tb}"], w=[psk])
                        evac(tb, ps, psk)

                with ExitStack() as sa:
                    qlat = T(sa, "qlat", [128, 8, S], BF16)
                    qiT = T(sa, "qiT", [128, 4, S], BF16)
                    qaT = [T(sa, f"qaT{i}", [128, S], BF16) for i in range(2)]
                    kiT = T(sa, "kiT", [128, S], BF16)
                    ckv = T(sa, "ckv", [128, NT, 128], BF16)
                    ckvT = T(sa, "ckvT", [128, S], BF16)
                    wia = T(sa, "wia", [128, NT, 8], F32)
                    wck = T(sa, "wck", [128, 8, 136], BF16)
                    sc = T(sa, "sc", [128, S], F32)
                    rl = [T(sa, f"rl{i}", [128, 512], F32) for i in range(2)]
                    nm = [T(sa, f"nm{i}", [128, S], BF16) for i in range(2)]
                    pT = [T(sa, f"pT{i}", [128, S], BF16) for i in range(2)]
                    olat = [T(sa, f"olat{i}", [128, 8, 128], BF16) for i in range(2)]
                    sm = T(sa, "sm", [128, 16], F32)
                    rzs = [T(sa, f"rzs{i}", [128, 128], F32) for i in range(2)]
                    ssq = T(sa, "ssq", [128, 2], F32)
                    cjunk = T(sa, "cjunk", [128, 128], F32)
                    P.op("dve", lambda e: e.memset(ssq[:], 0.0), (), ["ssq0", "ssq1"])
                    P.op("dve", lambda e: e.memset(sm[:], 0.0), (), ["sm0", "sm1", "sm2", "sm3", "sm4", "sm5"])
                    bpp = T(sa, "bpp", [128, 16, 128], F32)
                    dma("sp", "c3", bpp[:], c_bnear, w=["bpp"])
                    for dh in range(16):
                        ts("dve", bpp[:, dh, :], bpp[:, dh, :], cb[:, dh % 8:dh % 8 + 1], 8.0, ALU.subtract, ALU.mult, r=["bpp", "cb"], w=["bpp"])

                    for m in range(4):
                        qa_t = qaT[m % 2]
                        qk = f"qaT{m % 2}"

                        def ev_qa(tb, ps, psk, qa_t=qa_t, qk=qk):
                            cp("act" if tb % 2 == 0 else "dve", qa_t[:, tb * 512:(tb + 1) * 512], ps[:, :], r=[psk], w=[qk])
                        proj_fm(QA0 + m * 128, ev_qa)
                        for hl in range(2):
                            h = 2 * m + hl
                            for tb in range(4):
                                ps, psk = nb()
                                mm(ps[:, :], wukT[64 * hl:64 * hl + 64, m, :], qa_t[64 * hl:64 * hl + 64, tb * 512:(tb + 1) * 512], r=["wukT", qk], w=[psk])
                                cp("act" if tb % 2 == 1 else "dve", qlat[:, h, tb * 512:(tb + 1) * 512], ps[:, :], r=[psk], w=[f"qlat{h}"])
                    for m in range(4):
                        def ev_qi(tb, ps, psk, m=m):
                            cp("act" if tb % 2 == 0 else "dve", qiT[:, m, tb * 512:(tb + 1) * 512], ps[:, :], r=[psk], w=[f"qiT{m}"])
                        proj_fm(QI0 + m * 128, ev_qi)
                    wk_t, wk_k = nwt()
                    wload(wk_t[:, :, 0:64], w_in[:, KI0:KI0 + 64], 8, 64, wk_k)
                    wload(wk_t[:, :, 64:128], w_in[:, KI0:KI0 + 64], 8, 64, wk_k)

                    def ev_ki(tb, ps, psk):
                        cp("act" if tb % 2 == 0 else "dve", kiT[:, tb * 512:(tb + 1) * 512], ps[:, :], r=[psk], w=["kiT"])
                    proj_fm(None, ev_ki, wtile=wk_t, wkey=wk_k)
                    wload(wck[:, :, 0:128], w_in[:, CKV0:CKV0 + 128], 8, 128, "wck")
                    wload(wck[:, :, 128:136], w_in[:, WI0:WI0 + 8], 8, 8, "wck")
                    for i in range(NT):
                        ps, psk = nb()
                        for kc in range(8):
                            mm(ps[:, 0:136], xT[:, kc, 128 * i:128 * (i + 1)], wck[:, kc, :], start=(kc == 0), stop=(kc == 7), r=["wck", f"xT{i // 4}"], w=[psk])
                        act(cjunk[:], ps[:, 0:128], AF.Square, r=[psk], w=["ssq0"], accum=ssq[:, 0:1])
                        act(ssq[:, 1:2], ssq[:, 0:1], AF.Ln, r=["ssq0"], w=["ssq1"], bias=RMS_EPS, scale=1.0 / 128.0)
                        act(ssq[:, 1:2], ssq[:, 1:2], AF.Exp, r=["ssq1"], w=["ssq1"], scale=-0.5)
                        stt(ckv[:, i, :], ps[:, 0:128], ssq[:, 1:2], kvg[:], ALU.mult, ALU.mult, r=[psk, "ssq1", "kvg"], w=[f"ckv{i}"])
                        cp("act", wia[:, i, :], ps[:, 128:136], r=[psk], w=[f"wia{i}"])
                        pt, ptk = nbt()
                        tr(pt[:, 0:128], ckv[:, i, :], idb[:], r=[f"ckv{i}", "idb"], w=[ptk])
                        cp("dve", ckvT[:, 128 * i:128 * (i + 1)], pt[:, 0:128], r=[ptk], w=[f"ckvT{i}"])

                    sck = [f"sc{kb}" for kb in range(4)]

                    def IDX(j):
                        nv = 128 * (j + 1)
                        nlo = 128 * j + 64
                        for h in range(8):
                            hl, m = h % 2, h // 2
                            for kb in range((nv + 511) // 512):
                                c0 = kb * 512
                                cw = min(512, nv - c0)
                                ps, psk = nb()
                                mm(ps[:, 0:cw], qiT[64 * hl:64 * hl + 64, m, 128 * j:128 * (j + 1)], kiT[64 * hl:64 * hl + 64, c0:c0 + cw], r=[f"qiT{m}", "kiT"], w=[psk])
                                ri = (h * 4 + kb) % 2
                                act(rl[ri][:, 0:cw], ps[:, 0:cw], AF.Relu, r=[psk], w=[f"rl{ri}"])
                                if h == 0:
                                    ts("dve", sc[:, c0:c0 + cw], rl[ri][:, 0:cw], wia[:, j, 0:1], None, ALU.mult, r=[f"rl{ri}", f"wia{j}"], w=[f"sc{kb}"])
                                else:
                                    stt(sc[:, c0:c0 + cw], rl[ri][:, 0:cw], wia[:, j, h:h + 1], sc[:, c0:c0 + cw], ALU.mult, ALU.add, r=[f"rl{ri}", f"wia{j}", f"sc{kb}"], w=[f"sc{kb}"])
                        tt("dve", sc[:, 128 * j:128 * (j + 1)], sc[:, 128 * j:128 * (j + 1)], negvis[:], ALU.add, r=sck + ["negvis"], w=sck)
                        if j >= 2:
                            red(sm[:, 0:1], sc[:, 0:nlo], ALU.min, r=sck, w=["sm0"])
                            red(sm[:, 1:2], sc[:, 0:nv], ALU.max, r=sck, w=["sm1"])
                            tt("dve", sm[:, 2:3], sm[:, 1:2], sm[:, 0:1], ALU.subtract, r=["sm0", "sm1"], w=["sm2"])
                            stt(sm[:, 3:4], sm[:, 2:3], -0.5, sm[:, 0:1], ALU.mult, ALU.subtract, r=["sm2", "sm0"], w=["sm3"])

                    def BIS(j, k):
                        nv = 128 * (j + 1)
                        f = 2.0 ** -(k + 1)
                        nmn, nmnk = nm[j % 2], f"nm{j % 2}"
                        act(nmn[:, 0:nv], sc[:, 0:nv], AF.Sign, r=sck + ["sm3"], w=[nmnk, "sm4"], bias=sm[:, 3:4], accum=sm[:, 4:5])
                        stt(sm[:, 5:6], sm[:, 4:5], 510.5 - nv, sm[:, 2:3], ALU.is_ge, ALU.mult, r=["sm4", "sm2"], w=["sm5"])
                        stt(sm[:, 0:1], sm[:, 5:6], f, sm[:, 0:1], ALU.mult, ALU.add, r=["sm5", "sm0"], w=["sm0"])
                        if k + 1 < NBIS:
                            stt(sm[:, 3:4], sm[:, 2:3], -0.5 * f, sm[:, 0:1], ALU.mult, ALU.subtract, r=["sm2", "sm0"], w=["sm3"])

                    def NM(j):
                        nv = 128 * (j + 1)
                        thr, thrk = (sm[:, 0:1], "sm0") if j >= 2 else (thrc[:, 0:1], "thrc")
                        ts("dve", nm[j % 2][:, 0:nv], sc[:, 0:nv], thr, -32768.0, ALU.is_lt, ALU.mult, r=sck + [thrk], w=[f"nm{j % 2}"])

                    def ATT_qk(j, h):
                        nmj, nmk = nm[j % 2], f"nm{j % 2}"
                        pj = pT[h % 2]
                        pk = f"pT{h % 2}"
                        for g in range((j + 4) // 4):
                            ps, psk = nb()
                            tiles = list(range(4 * g, min(4 * g + 4, j + 1)))
                            for ii_, i in enumerate(tiles):
                                o_ = ps[:, ii_ * 128:(ii_ + 1) * 128]
                                near = (i >= j - 1)
                                mm(o_, ckvT[:, 128 * i:128 * (i + 1)], qlat[:, h, 128 * j:128 * (j + 1)], start=True, stop=False, r=[f"ckvT{i}", f"qlat{h}"], w=[psk])
                                mm(o_, nmj[:, 128 * i:128 * (i + 1)], idb[:], start=False, stop=(not near), r=[nmk, "idb"], w=[psk])
                                if near:
                                    mm(o_, idf[:], bpp[:, (j - i) * 8 + h, :], start=False, stop=True, r=["idf", "bpp"], w=[psk])
                            ncol = 128 * len(tiles)
                            act(pj[:, 512 * g:512 * g + ncol], ps[:, 0:ncol], AF.Exp, r=[psk, "cb"], w=[pk], bias=cb[:, h:h + 1], scale=0.125)

                    def ATT_pv(j, h):
                        ol, olk = olat[j % 2], f"olat{j % 2}"
                        pj = pT[h % 2]
                        pk = f"pT{h % 2}"
                        ps, psk = nb()
                        for i in range(j + 1):
                            mm(ps[:, 0:128], ckv[:, i, :], pj[:, 128 * i:128 * (i + 1)], start=(i == 0), stop=(i == j), r=[f"ckv{i}", pk], w=[psk])
                        for i in range(j + 1):
                            mm(ps[:, 128:256], onesb[:], pj[:, 128 * i:128 * (i + 1)], start=(i == 0), stop=(i == j), r=["onesb", pk], w=[psk])
                        rz = rzs[h % 2]
                        rzk = f"rzs{h % 2}"
                        P.op("dve", lambda e, rz=rz, ps=ps: e.reciprocal(out=rz[:], in_=ps[:, 128:256]), [psk], [rzk])
                        tt("dve", ol[:, h, :], ps[:, 0:128], rz[:], ALU.mult, r=[psk, rzk], w=[olk])

                    def ATT_tail(j):
                        ol, olk = olat[j % 2], f"olat{j % 2}"
                        for m in range(4):
                            ps, psk = nb()
                            mm(ps[:, 0:128], wuvp[:, 2 * m, :], ol[:, 2 * m, :], start=True, stop=False, r=["wuvp", olk], w=[psk])
                            mm(ps[:, 0:128], wuvp[:, 2 * m + 1, :], ol[:, 2 * m + 1, :], start=False, stop=True, r=["wuvp", olk], w=[psk])
                            cp("act", oTa[:, m, 128 * j:128 * (j + 1)], ps[:, 0:128], r=[psk], w=[f"oTa{j // 4}"])

                    if b == 0:
                        for r_ in range(16):
                            rs_ = slice(1024 * r_, 1024 * (r_ + 1))
                            P.dma("pool", "uvtab", lambda e, rs_=rs_: e.dma_start(out=uv[rs_, 0:1024], in_=u_table[rs_, :]), (), ["uvtab"])
                            P.dma("pool", "uvtab", lambda e, rs_=rs_: e.dma_start(out=uv[rs_, 1024:2048], in_=v_table[rs_, :]), (), ["uvtab"])
                    IDX(0)
                    NM(0)
                    for j in range(NT):
                        nxt = j + 1
                        if nxt < NT:
                            IDX(nxt)
                        ATT_qk(j, 0)
                        for h in range(8):
                            if h + 1 < 8:
                                ATT_qk(j, h + 1)
                            ATT_pv(j, h)
                            if nxt < NT and nxt >= 2:
                                for k in range(NBIS * h // 8, NBIS * (h + 1) // 8):
                                    BIS(nxt, k)
                        if nxt < NT:
                            NM(nxt)
                        ATT_tail(j)
                    P.barrier()

                if dbg == "dsa":
                    with ExitStack() as sd:
                        dtmp = T(sd, "dtmp", [128, 4, S], F32)
                        cp("dve", dtmp[:], oTa[:], w=["dtmp"])
                        dma("sp", "dbg", dbg_out.rearrange("(m p) t -> p m t", p=128), dtmp[:], r=["dtmp"])
                        P.barrier()
                    break

                with ExitStack() as sb:
                    qbT = T(sb, "qbT", [128, 2, S], BF16)
                    kT = T(sb, "kT", [128, 2, S], BF16)
                    sgT = T(sb, "sgT", [128, 2, S], BF16)
                    logf = T(sb, "logf", [128, NT, 256], F32)
                    kk = T(sb, "kk", [128, NT, 256], BF16)
                    iib = T(sb, "iib", [128, NT, 256], BF16)
                    w5 = [T(sb, f"w5{i}", [128, 8, 256], BF16) for i in range(2)]
                    sgtmp = [T(sb, f"sgtmp{i}", [128, 512], F32) for i in range(2)]
                    ftmp = [T(sb, f"ftmp{i}", [128, 256], F32) for i in range(2)]
                    e14 = [T(sb, f"e14{i}", [128, 256], F32) for i in range(4)]
                    e2 = [T(sb, f"e2{i}", [128, 128], F32) for i in range(4)]
                    e3 = [T(sb, f"e3{i}", [128, 128], F32) for i in range(4)]
                    dec = [T(sb, f"dec{i}", [128, 2], F32) for i in range(4)]
                    qin = [T(sb, f"qin{i}", [128, 128], BF16) for i in range(4)]
                    q2 = [T(sb, f"q2{i}", [128, 128], F32) for i in range(4)]
                    kin = [T(sb, f"kin{i}", [128, 128], BF16) for i in range(4)]
                    k3 = [T(sb, f"k3{i}", [128, 128], BF16) for i in range(4)]
                    am = [T(sb, f"am{i}", [128, 128], BF16) for i in range(4)]
                    s32 = [[T(sb, f"s32{i}{k}", [128, 128], F32) for k in range(2)] for i in range(2)]
                    sbf = [[T(sb, f"sbf{i}{k}", [128, 128], BF16) for k in range(2)] for i in range(2)]
                    sqb = [T(sb, f"sq{i}", [128, 128], BF16) for i in range(2)]
                    sd_ = [T(sb, f"sd{i}", [128, 128], F32) for i in range(2)]
                    o1 = [T(sb, f"o1{i}", [128, 128], F32) for i in range(2)]
                    for hp in range(2):
                        for hl in range(2):
                            h = 2 * hp + hl

                            def ev_q(tb, ps, psk, hl=hl):
                                cp("act" if tb % 2 == 0 else "dve", qbT[:, hl, tb * 512:(tb + 1) * 512], ps[:, :], r=[psk], w=[f"qbT{hl}"])
                            proj_fm(QB0 + h * 128, ev_q)

                            def ev_f(tb, ps, psk, hl=hl, h=h):
                                sg = sgtmp[tb % 2]
                                sgk = f"sgtmp{tb % 2}"
                                act(sg[:], ps[:, :], AF.Sigmoid, r=[psk], w=[sgk])
                                ts("dve", kT[:, hl, tb * 512:(tb + 1) * 512], sg[:], nomlf[:, h:h + 1], omlf[:, h:h + 1], ALU.mult, ALU.add, r=[sgk], w=[f"kT{hl}"])
                            proj_fm(FB0 + h * 128, ev_f)

                            def ev_g(tb, ps, psk, hl=hl):
                                act(sgT[:, hl, tb * 512:(tb + 1) * 512], ps[:, :], AF.Silu, r=[psk], w=[f"sgT{hl}"])
                            proj_fm(GB0 + h * 128, ev_g)
                        c2 = 2 * hp * 128
                        wload(w5[0][:, :, 0:128], w_in[:, FB0 + c2:FB0 + c2 + 128], 8, 128, "w50")
                        wload(w5[0][:, :, 128:256], w_in[:, FB0 + c2 + 128:FB0 + c2 + 256], 8, 128, "w50")
                        wload(w5[1][:, :, 0:128], w_in[:, IB0 + c2:IB0 + c2 + 128], 8, 128, "w51")
                        wload(w5[1][:, :, 128:256], w_in[:, IB0 + c2 + 128:IB0 + c2 + 256], 8, 128, "w51")
                        for i in range(NT):
                            ps, psk = nb()
                            for kc in range(8):
                                mm(ps[:, 0:256], xT[:, kc, 128 * i:128 * (i + 1)], w5[0][:, kc, :], start=(kc == 0), stop=(kc == 7), r=["w50", f"xT{i // 4}"], w=[psk])
                            for kc in range(8):
                                mm(ps[:, 256:512], xT[:, kc, 128 * i:128 * (i + 1)], w5[1][:, kc, :], start=(kc == 0), stop=(kc == 7), r=["w51", f"xT{i // 4}"], w=[psk])
                            ft = ftmp[i % 2]
                            fk = f"ftmp{i % 2}"
                            act(ft[:], ps[:, 0:256], AF.Sigmoid, r=[psk], w=[fk])
                            tt("dve", ft[:], ft[:], omlt[:, c2:c2 + 256], ALU.mult, r=[fk, "omlt"], w=[fk])
                            tt("dve", logf[:, i, :], ft[:], lbt[:, c2:c2 + 256], ALU.add, r=[fk, "lbt"], w=[f"logf{i}"])
                            ts("dve", kk[:, i, :], logf[:, i, :], -1.0, 1.0, ALU.mult, ALU.add, r=[f"logf{i}"], w=[f"kk{i}"])
                            cp("act", iib[:, i, :], ps[:, 256:512], r=[psk], w=[f"iib{i}"])
                        lfk = [f"logf{i}" for i in range(NT)]
                        for q4 in range(4):
                            act(logf[:, 4 * q4:4 * q4 + 4, :], logf[:, 4 * q4:4 * q4 + 4, :], AF.Ln, r=lfk[4 * q4:4 * q4 + 4], w=lfk[4 * q4:4 * q4 + 4])
                        for hl in range(2):
                            P.op("dve", lambda e, hl=hl: e.memset(s32[hl][0][:], 0.0), (), [f"s32{hl}0"])
                        rot["pool"] = [0, 1, 2, 3]
                        pob = {(0, 0): (psb[4], "ps4"), (0, 1): (psb[5], "ps5"),
                               (1, 0): (pst[0][:].bitcast(F32), "pst0"), (1, 1): (pst[1][:].bitcast(F32), "pst1")}

                        def HA(i, hl):
                            q = 2 * hl + (i % 2)
                            sfx = f"{hl}{i % 2}"
                            tsl = slice(128 * i, 128 * (i + 1))
                            lf = logf[:, i, hl * 128:(hl + 1) * 128]
                            pe_, pek = nb()
                            mm(pe_[:, 0:128], lf, cum[:, 0:128], r=[f"logf{i}", "cum"], w=[pek])
                            mm(pe_[:, 128:256], lf, cum[:, 128:256], r=[f"logf{i}", "cum"], w=[pek])
                            mm(pe_[:, 256:258], lf, cum[:, 256:258], r=[f"logf{i}", "cum"], w=[pek])
                            mm(pe_[:, 384:512], cum[:, 258:386], lf, r=[f"logf{i}", "cum"], w=[pek])
                            act(e14[q][:], pe_[:, 0:256], AF.Exp, r=[pek], w=[f"e14{sfx}"])
                            act(e2[q][:], pe_[:, 0:128], AF.Exp, r=[pek], w=[f"e2{sfx}"], scale=-1.0)
                            act(dec[q][:], pe_[:, 256:258], AF.Exp, r=[pek], w=[f"dec{sfx}"])
                            act(e3[q][:], pe_[:, 384:512], AF.Exp, r=[pek], w=[f"e3{sfx}"])
                            tt("dve", qin[q][:], qbT[:, hl, tsl], e14[q][:, 0:128], ALU.mult, r=[f"qbT{hl}", f"e14{sfx}"], w=[f"qin{sfx}"])
                            tt("dve", q2[q][:], qbT[:, hl, tsl], e14[q][:, 128:256], ALU.mult, r=[f"qbT{hl}", f"e14{sfx}"], w=[f"q2{sfx}"])
                            tt("dve", kin[q][:], kT[:, hl, tsl], e2[q][:], ALU.mult, r=[f"kT{hl}", f"e2{sfx}"], w=[f"kin{sfx}"])
                            tt("dve", k3[q][:], kk[:, i, hl * 128:(hl + 1) * 128], e3[q][:], ALU.mult, r=[f"kk{i}", f"e3{sfx}"], w=[f"k3{sfx}"])
                            pa, pak = nb()
                            mm(pa[:, 0:128], kin[q][:], qin[q][:], r=[f"kin{sfx}", f"qin{sfx}"], w=[pak])
                            tt("dve", am[q][:], pa[:, 0:128], cum[:, 128:256], ALU.mult, r=[pak, "cum"], w=[f"am{sfx}"])
                            po, pok = pob[(hl, i % 2)]
                            iv = iib[:, i, hl * 128:(hl + 1) * 128]
                            mm(po[:, 0:128], iv, am[q][:], start=True, stop=False, r=[f"iib{i}", f"am{sfx}"], w=[pok])

                        def HB(i):
                            tsl = slice(128 * i, 128 * (i + 1))
                            for c in range(2):
                                cur, nxt = c, 1 - c
                                for hl in range(2):
                                    q = 2 * hl + (i % 2)
                                    sfx = f"{hl}{i % 2}"
                                    po, pok = pob[(hl, i % 2)]
                                    mm(po[:, 64 * c:64 * c + 64], s32[hl][cur][:], q2[q][:, 64 * c:64 * c + 64], start=False, stop=(c == 1), r=[f"s32{hl}{cur}", f"q2{sfx}"], w=[pok])
                                    pss, pssk = nb()
                                    mm(pss[:, 0:128], k3[q][64 * c:64 * c + 64, :], iib[64 * c:64 * c + 64, i, hl * 128:(hl + 1) * 128], r=[f"k3{sfx}", f"iib{i}"], w=[pssk])
                                    stt(s32[hl][nxt][:], s32[hl][cur][:], dec[q][:, c:c + 1], pss[:, 0:128], ALU.mult, ALU.add, r=[f"s32{hl}{cur}", f"dec{sfx}", pssk], w=[f"s32{hl}{nxt}"])
                            for hl in range(2):
                                h = 2 * hp + hl
                                po, pok = pob[(hl, i % 2)]
                                act(sqb[hl][:], po[:, 0:128], AF.Square, r=[pok], w=[f"sq{hl}"])
                                mm(po[:, 128:256], onesb[:], sqb[hl][:], r=["onesb", f"sq{hl}"], w=[pok])
                                act(sd_[hl][:], po[:, 128:256], AF.Ln, r=[pok], w=[f"sd{hl}"], bias=RMS_EPS, scale=1.0 / 128.0)
                                act(sd_[hl][:], sd_[hl][:], AF.Exp, r=[f"sd{hl}"], w=[f"sd{hl}"], scale=-0.5)
                                tt("dve", o1[hl][:], po[:, 0:128], sd_[hl][:], ALU.mult, r=[pok, f"sd{hl}"], w=[f"o1{hl}"])
                                stt(oTb[:, h, tsl], o1[hl][:], bng[:, 0:1], sgT[:, hl, tsl], ALU.mult, ALU.mult, r=[f"o1{hl}", "bng", f"sgT{hl}"], w=[f"oTb{i // 4}"])

                        HA(0, 0)
                        HA(0, 1)
                        for i in range(NT):
                            if i + 1 < NT:
                                HA(i + 1, 0)
                                HA(i + 1, 1)
                            HB(i)
                        rot["pool"] = [0, 1, 2, 3, 4, 5]
                        P.barrier()

                if dbg == "hgrn":
                    with ExitStack() as sd:
                        dtmp = T(sd, "dtmp", [128, 4, S], F32)
                        cp("dve", dtmp[:], oTb[:], w=["dtmp"])
                        dma("sp", "dbg", dbg_out.rearrange("(m p) t -> p m t", p=128), dtmp[:], r=["dtmp"])
                        P.barrier()
                    break

                mgT = T(sq, "mgT", [128, 8, S], BF16)
                wpq = T(sq, "wpq", [128, 8, 2048], BF16)
                wo = T(sq, "wo", [128, 8, D], BF16)
                pre = [("wo", wo, w_o, c) for c in range(8)] + [("wpq", wpq, w_pq, c) for c in range(16)]
                with ExitStack() as sc_:
                    wbr = [T(sc_, f"wbr{i}", [128, 4, 128], BF16) for i in range(2)]
                    sga = [T(sc_, f"sga{i}", [128, 512], F32) for i in range(2)]
                    t1 = [T(sc_, f"t1{i}", [128, 512], F32) for i in range(2)]
                    for m in range(8):
                        wa, wak = nwt()
                        wload(wa[:], w_in[:, GA0 + m * 128:GA0 + (m + 1) * 128], 8, 128, wak)
                        wg, wgk = nwt()
                        wload(wg[:], w_in[:, GG0 + m * 128:GG0 + (m + 1) * 128], 8, 128, wgk)
                        wload(wbr[0][:], w_br_a[:, m * 128:(m + 1) * 128], 4, 128, "wbr0")
                        wload(wbr[1][:], w_br_b[:, m * 128:(m + 1) * 128], 4, 128, "wbr1")
                        for tb in range(4):
                            cs = slice(tb * 512, (tb + 1) * 512)
                            for br in range(2):
                                wgt, wgtk = (wa, wak) if br == 0 else (wg, wgk)
                                src = oTa if br == 0 else oTb
                                srck = f"oTa{tb}" if br == 0 else f"oTb{tb}"
                                ps, psk = nb()
                                for kc in range(8):
                                    mm(ps[:, :], wgt[:, kc, :], xT[:, kc, cs], start=(kc == 0), stop=(kc == 7), r=[wgtk, f"xT{tb}"], w=[psk])
                                act(sga[br][:], ps[:, :], AF.Sigmoid, r=[psk], w=[f"sga{br}"])
                                ps2, ps2k = nb()
                                for kc in range(4):
                                    mm(ps2[:, :], wbr[br][:, kc, :], src[:, kc, cs], start=(kc == 0), stop=(kc == 3), r=[f"wbr{br}", srck], w=[ps2k])
                                tt("dve", t1[br][:], ps2[:, :], sga[br][:], ALU.mult, r=[ps2k, f"sga{br}"], w=[f"t1{br}"])
                            tt("dve", mgT[:, m, cs], t1[0][:], t1[1][:], ALU.add, r=["t10", "t11"], w=[f"mgT{tb}"])
                        for _ in range(3):
                            key_, dst_, src_, c_ = pre.pop(0)
                            wload(dst_[:, :, c_ * 128:(c_ + 1) * 128], src_[:, c_ * 128:(c_ + 1) * 128], 8, 128, key_, eng=("act", "dve")[len(pre) % 2])
                    P.barrier()

                with ExitStack() as sd:
                    lnp = oTa[:].bitcast(F32)
                    xr, xrk = wstg[0][:].rearrange("p a b -> p (a b)"), "wstg0"
                    ot, otk = wstg[1][:].rearrange("p a b -> p (a b)"), "wstg1"
                    rr2, rr2k = wstg[2][:].rearrange("p a b -> p (a b)"), "wstg2"
                    pjunk = wt[0][:].rearrange("p a b -> p (a b)")
                    h1T, h1Tk = wt[1], "wt1"
                    h1 = [xT[:, 6 + i, :].bitcast(F32) for i in range(2)]
                    h1b = [T(sd, "h1b0", [128, D], BF16)[:], wt[2][:].rearrange("p a b -> p (a b)")]
                    h1bk = ["h1b0", "wt2"]
                    ssb = T(sd, "ssb", [128, 16, 128], F32)
                    rr = ssb[:, 0:8, :].rearrange("p a b -> p (a b)")
                    RRK = ["ssb0", "ssb1"]
                    m8 = T(sd, "m8", [128, 16, 16], F32)
                    ix = T(sd, "ix", [128, 16, 16], U32)
                    ixf = T(sd, "ixf", [128, 16, 16], F32)
                    cand = T(sd, "cand", [128, 8, 256], F32)
                    qpT = cand[:].bitcast(BF16)[:, 0:4, :].rearrange("p a (c t) -> p (a c) t", t=128)
                    CANDK = [f"cand{h}" for h in range(8)]
                    big = ssb[:].rearrange("p (h a) b -> p h (a b)", h=8)
                    BIGK = ["ssb0", "ssb1", "ssb2", "ssb3"]
                    t8 = T(sd, "t8", [128, 8, 16], F32)
                    px = T(sd, "px", [128, 8, 16], U32)
                    pf = T(sd, "pf", [128, 8, 16], F32)
                    pa_ = T(sd, "pa", [128, 8, 16], F32)
                    pb_ = T(sd, "pb", [128, 8, 16], F32)
                    i1s = T(sd, "i1s", [128, 8, 16], F32)
                    i2s = T(sd, "i2s", [128, 8, 16], F32)
                    idsf = T(sd, "idsf", [128, 128], F32)
                    ids = [T(sd, f"ids{i}", [128, 128], I32) for i in range(2)]
                    gate = [T(sd, f"gate{i}", [128, 8, 16], F32) for i in range(2)]
                    zz = T(sd, "zz", [128, 8], F32)
                    hd = T(sd, "hd", [128, 128], F32)
                    gh = T(sd, "gh", [128, 128], F32)
                    st6 = T(sd, "st6", [128, 2, 12], F32)
                    mv = T(sd, "mv", [128, 2, 4], F32)
                    gg = T(sd, "gg", [128, 128], F32)
                    uvb = [oTb[:, s_, :] for s_ in range(4)] + [xT[:, s_, :] for s_ in range(6)]
                    NSL = len(uvb)
                    dk = [T(sd, f"dk{i}", [128, 128], BF16) for i in range(4)]
                    P.op("dve", lambda e: e.memset(hd[:], 0.0), (), ["hdinit"])
                    P.op("dve", lambda e: e.memset(mv[:], 0.0), (), ["mv0", "mv20", "mv30", "mv1", "mv21", "mv31"])
                    dma("sp", "lnp", lnp[:], c_ln, w=["lnp"])
                    P.barrier()
                    rot["pool"] = [0, 1, 2, 3]
                    pf0, pf1 = psb[4], psb[5]

                    def layer_norm(dst, src, srck, gi, dstk):
                        q_ = gi // 2
                        s6, m4 = st6[:, q_, :], mv[:, q_, :]
                        ka, kb_, km, km2, km3 = f"st6a{q_}", f"st6b{q_}", f"mv{q_}", f"mv2{q_}", f"mv3{q_}"
                        srcl = list(srck) if isinstance(srck, (list, tuple)) else [srck]
                        P.op("dve", lambda e: e.bn_stats(out=s6[:, 0:6], in_=src[:, 0:512]), srcl, [ka])
                        P.op("dve", lambda e: e.bn_stats(out=s6[:, 6:12], in_=src[:, 512:1024]), srcl, [kb_])
                        P.op("dve", lambda e: e.bn_aggr(out=m4[:, 0:2], in_=s6), [ka, kb_], [km])
                        act(m4[:, 2:3], m4[:, 1:2], AF.Sqrt, r=[km], w=[km2], bias=LN_EPS)
                        P.op("dve", lambda e: e.reciprocal(out=m4[:, 3:4], in_=m4[:, 2:3]), [km2], [km3])
                        ts("dve", dst, src, m4[:, 0:1], m4[:, 3:4], ALU.subtract, ALU.mult, r=srcl + [km, km3], w=[dstk])
                        tt("dve", dst, dst, lnp[:, gi, :], ALU.mult, r=[dstk, "lnp"], w=[dstk])
                        tt("dve", dst, dst, lnp[:, gi + 1, :], ALU.add, r=[dstk, "lnp"], w=[dstk])

                    def front(i):
                        p = i % 2
                        tsl = slice(128 * i, 128 * (i + 1))
                        hk, hbk, idk, gk = f"h1{p}", h1bk[p], f"ids{p}", f"gate{p}"
                        dma("sp", xrk, xr, x[t0 + 128 * i:t0 + 128 * (i + 1), :], w=[xrk])
                        for nbk in range(2):
                            ps, psk = nb()
                            for kc in range(8):
                                mm(ps[:, :], mgT[:, kc, tsl], wo[:, kc, nbk * 512:(nbk + 1) * 512], start=(kc == 0), stop=(kc == 7), r=[f"mgT{i // 4}", "wo"], w=[psk])
                            stt(rr[:, nbk * 512:(nbk + 1) * 512], xr[:, nbk * 512:(nbk + 1) * 512], ALPHA, ps[:, :], ALU.mult, ALU.add, r=[xrk, psk], w=RRK)
                        layer_norm(h1[p], rr, RRK, 0, hk)
                        if dbg == "h1":
                            dma("sp", "dbgh1", dbg_out[128 * i:128 * (i + 1), :], h1[p], r=[hk])
                        cp("act", h1b[p], h1[p], r=[hk], w=[hbk])
                        pt, ptk = nbt()
                        for c in range(8):
                            tr(pt[:, c * 128:(c + 1) * 128], h1b[p][:, c * 128:(c + 1) * 128], idb[:], r=[hbk, "idb"], w=[ptk])
                        cp("act", h1T[:], pt[:].rearrange("p (c t) -> p c t", c=8), r=[ptk], w=[h1Tk])
                        for g in range(4):
                            ps, psk = nb()
                            for q_ in range(4):
                                ct = 4 * g + q_
                                for kc in range(8):
                                    mm(ps[:, q_ * 128:(q_ + 1) * 128], wpq[:, kc, ct * 128:(ct + 1) * 128], h1T[:, kc, :], start=(kc == 0), stop=(kc == 7), r=["wpq", h1Tk], w=[psk])
                            cp("act", qpT[:, 4 * g:4 * g + 4, :], ps[:, :].rearrange("p (c t) -> p c t", c=4), r=[psk], w=CANDK)
                        for g in range(4):
                            ps, psk = nb()
                            for q_ in range(4):
                                ct = 4 * g + q_
                                mm(ps[:, q_ * 128:(q_ + 1) * 128], qpT[:, ct, :], skT[:, ct % 2, :], r=CANDK + ["skT"], w=[psk])
                            cp("act", ssb[:, 4 * g:4 * g + 4, :], ps[:, :].rearrange("p (c t) -> p c t", c=4), r=[psk], w=[f"ssb{g}"])
                        for ph in range(5):
                            for ct in range(16):
                                g = ct // 4
                                if ph == 0:
                                    P.op("dve", lambda e, ct=ct: e.max(out=m8[:, ct, 0:8], in_=ssb[:, ct, :]), [f"ssb{g}"], [f"m8a{ct}"])
                                elif ph == 1:
                                    P.op("dve", lambda e, ct=ct: e.max_index(out=ix[:, ct, 0:8], in_max=m8[:, ct, 0:8], in_values=ssb[:, ct, :]), [f"ssb{g}", f"m8a{ct}"], [f"ixa{ct}"])
                                elif ph == 2:
                                    P.op("dve", lambda e, ct=ct: e.match_replace(out=ssb[:, ct, :], in_to_replace=m8[:, ct, 0:8], in_values=ssb[:, ct, :], imm_value=NEG), [f"ssb{g}", f"m8a{ct}", f"ixa{ct}"], [f"ssb{g}"])
                                elif ph == 3:
                                    P.op("dve", lambda e, ct=ct: e.max(out=m8[:, ct, 8:16], in_=ssb[:, ct, :]), [f"ssb{g}"], [f"m8b{ct}"])
                                else:
                                    P.op("dve", lambda e, ct=ct: e.max_index(out=ix[:, ct, 8:16], in_max=m8[:, ct, 8:16], in_values=ssb[:, ct, :]), [f"ssb{g}", f"m8b{ct}"], [f"ixb{ct}"])
                        m8k = [f"m8a{ct}" for ct in range(16)] + [f"m8b{ct}" for ct in range(16)]
                        ixk = [f"ixa{ct}" for ct in range(16)] + [f"ixb{ct}" for ct in range(16)]
                        cp("dve", ixf[:], ix[:], r=ixk, w=["ixf"])
                        m8v = m8[:].rearrange("p (h two) a -> p h two a", two=2)
                        ixv = ixf[:].rearrange("p (h two) a -> p h two a", two=2)
                        candv = cand[:].rearrange("p h (a b) -> p h a b", a=16)
                        bigv = big[:].rearrange("p h (a b) -> p h a b", a=16)
                        tt("dve", candv, m8v[:, :, 0, :].unsqueeze(3).to_broadcast([128, 8, 16, 16]), m8v[:, :, 1, :].unsqueeze(2).to_broadcast([128, 8, 16, 16]), ALU.add, r=m8k, w=[f"cand{h}" for h in range(8)])
                        for ph in range(5):
                            for h in range(8):
                                if ph == 0:
                                    P.op("dve", lambda e, h=h: e.max(out=t8[:, h, 0:8], in_=cand[:, h, :]), [f"cand{h}"], [f"t8a{h}"])
                                elif ph == 1:
                                    P.op("dve", lambda e, h=h: e.max_index(out=px[:, h, 0:8], in_max=t8[:, h, 0:8], in_values=cand[:, h, :]), [f"cand{h}", f"t8a{h}"], [f"pxa{h}"])
                                elif ph == 2:
                                    P.op("dve", lambda e, h=h: e.match_replace(out=cand[:, h, :], in_to_replace=t8[:, h, 0:8], in_values=cand[:, h, :], imm_value=NEG), [f"cand{h}", f"t8a{h}", f"pxa{h}"], [f"cand{h}"])
                                elif ph == 3:
                                    P.op("dve", lambda e, h=h: e.max(out=t8[:, h, 8:16], in_=cand[:, h, :]), [f"cand{h}"], [f"t8b{h}"])
                                else:
                                    P.op("dve", lambda e, h=h: e.max_index(out=px[:, h, 8:16], in_max=t8[:, h, 8:16], in_values=cand[:, h, :]), [f"cand{h}", f"t8b{h}"], [f"pxb{h}"])
                        t8k = [f"t8a{h}" for h in range(8)] + [f"t8b{h}" for h in range(8)]
                        pxk = [f"pxa{h}" for h in range(8)] + [f"pxb{h}" for h in range(8)]
                        cp("dve", pf[:], px[:], r=pxk, w=["pf"])
                        tt("dve", bigv, pf[:].unsqueeze(3).to_broadcast([128, 8, 16, 16]), iot[:, 16:32].unsqueeze(1).unsqueeze(1).to_broadcast([128, 8, 16, 16]), ALU.is_ge, r=["pf", "iot"], w=BIGK)
                        red(pa_[:], bigv, ALU.add, r=BIGK, w=["pa"])
                        stt(pb_[:], pa_[:], -16.0, pf[:], ALU.mult, ALU.add, r=["pa", "pf"], w=["pb"])
                        io16 = iot[:, 0:16].unsqueeze(1).unsqueeze(1).to_broadcast([128, 8, 16, 16])
                        for (src_, sel, selk, part) in ((pa_, i1s, "i1s", 0), (pb_, i2s, "i2s", 1)):
                            tt("dve", bigv, src_[:].unsqueeze(3).to_broadcast([128, 8, 16, 16]), io16, ALU.is_equal, r=["pa", "pb", "iot"], w=BIGK)
                            tt("dve", bigv, bigv, ixv[:, :, part, :].unsqueeze(2).to_broadcast([128, 8, 16, 16]), ALU.mult, r=BIGK + ["ixf"], w=BIGK)
                            red(sel[:], bigv, ALU.add, r=BIGK, w=[selk])
                        stt(idsf[:].rearrange("p (h j) -> p h j", h=8), i1s[:], 128.0, i2s[:], ALU.mult, ALU.add, r=["i1s", "i2s"], w=["idsf"])
                        ts("dve", idsf[:], idsf[:], 0.0, 16383.0, ALU.max, ALU.min, r=["idsf"], w=["idsf"])
                        cp("dve", ids[p][:], idsf[:], r=["idsf"], w=[idk])
                        tt("dve", gate[p][:], t8[:], t8[:, :, 0:1].to_broadcast([128, 8, 16]), ALU.subtract, r=t8k, w=[gk])
                        act(gate[p][:], gate[p][:], AF.Exp, r=[gk], w=[gk])
                        red(zz[:], gate[p][:], ALU.add, r=[gk], w=["zz"])
                        P.op("dve", lambda e: e.reciprocal(out=zz[:], in_=zz[:]), ["zz"], ["zz"])
                        tt("dve", gate[p][:], gate[p][:], zz[:].unsqueeze(2).to_broadcast([128, 8, 16]), ALU.mult, r=[gk, "zz"], w=[gk])

                    def back(i):
                        p = i % 2
                        hk, hbk, idk, gk = f"h1{p}", h1bk[p], f"ids{p}", f"gate{p}"
                        gatef = gate[p][:].rearrange("p h j -> p (h j)")
                        for k in range(129):
                            if k < 128:
                                s_ = k % NSL
                                gather(f"uv{s_}", uvb[s_], uv[:, :], ids[p][:, k:k + 1], r=[idk], w=[f"uv{s_}"])
                                stt(pjunk, uvb[s_][:, 0:1024], 1.0, h1b[p], ALU.mult, ALU.mult, r=[f"uv{s_}", hbk, "hdinit"], w=[f"hd{k % 8}"], accum=hd[:, k:k + 1])
                                act(gh[:, k:k + 1], hd[:, k:k + 1], AF.Gelu, r=[f"hd{k % 8}"], w=[f"gh{k % 8}"])
                            if k >= 1:
                                k1 = k - 1
                                s_ = k1 % NSL
                                di = k1 % 4
                                act(gg[:, k1:k1 + 1], gh[:, k1:k1 + 1], AF.Copy, r=[f"gh{k1 % 8}", gk], w=[f"gg{k1 % 8}"], scale=gatef[:, k1:k1 + 1])
                                act(dk[di][:], idb[:], AF.Copy, r=["idb", f"gg{k1 % 8}"], w=[f"dk{di}"], scale=gg[:, k1:k1 + 1])
                                mm(pf0[:, :], dk[di][:], uvb[s_][:, 1024:1536], start=(k1 == 0), stop=(k1 == 127), r=[f"dk{di}", f"uv{s_}"], w=["pf0"])
                                mm(pf1[:, :], dk[di][:], uvb[s_][:, 1536:2048], start=(k1 == 0), stop=(k1 == 127), r=[f"dk{di}", f"uv{s_}"], w=["pf1"])
                        stt(rr2[:, 0:512], h1[p][:, 0:512], ALPHA, pf0[:, :], ALU.mult, ALU.add, r=[hk, "pf0"], w=[rr2k])
                        stt(rr2[:, 512:1024], h1[p][:, 512:1024], ALPHA, pf1[:, :], ALU.mult, ALU.add, r=[hk, "pf1"], w=[rr2k])
                        layer_norm(ot, rr2, rr2k, 2, otk)
                        dma("sp", otk, y[t0 + 128 * i:t0 + 128 * (i + 1), :], ot, r=[otk])

                    def capture(fn_, *a):
                        P.cap = []
                        fn_(*a)
                        lst = P.cap
                        P.cap = None
                        return lst

                    P.replay(capture(front, 0))
                    for i in range(NT):
                        bl = capture(back, i)
                        fl = capture(front, i + 1) if i + 1 < NT else []
                        merged = []
                        fi = 0
                        nb_ = max(1, len(bl) - 60)
                        for bi, it in enumerate(bl):
                            merged.append(it)
                            want = min(len(fl), (len(fl) * (bi + 1)) // nb_)
                            while fi < want:
                                merged.append(fl[fi])
                                fi += 1
                        merged.extend(fl[fi:])
                        P.replay(merged)
                    rot["pool"] = [0, 1, 2, 3, 4, 5]
                    P.barrier()
        P.barrier()
        P.emit()
    return nc


def _t5_bucket_np(rel):
    half = 16
    max_exact = 8
    rel = jnp.asarray(rel, jnp.int32)
    base = jnp.where(rel > 0, half, 0)
    n = jnp.abs(rel)
    nf = jnp.maximum(n, 1).astype(jnp.float32)
    large = max_exact + (jnp.log(nf / max_exact) / math.log(128 / max_exact) * (half - max_exact)).astype(jnp.int32)
    large = jnp.minimum(large, half - 1)
    return np.asarray(base + jnp.where(n < max_exact, n, large))


def host_consts(inp):
    f32 = np.float32
    c = {}
    c["c_ident"] = np.eye(128, dtype=f32)
    s = np.arange(128)[:, None]
    t = np.arange(128)[None, :]
    same = (s // 64) == (t // 64)
    a3 = (same & (s <= t)).astype(f32)
    ref = (same & ((s % 64) <= 31)).astype(f32)
    a1 = a3 - ref
    a2 = same.astype(f32) - a3
    ind = np.stack([(np.arange(128) // 64 == 0), (np.arange(128) // 64 == 1)], axis=1).astype(f32)
    c["c_cum"] = np.concatenate([a1, a3, ind, a2], axis=1).astype(f32)
    tq = np.arange(128)[:, None]
    sk = np.arange(128)[None, :]
    c["c_negvis"] = np.where((tq < 64) & (sk >= 64), NEG, 0.0).astype(f32)
    with jax.default_device(jax.devices("cpu")[0]):
        kk_ = np.arange(128)[:, None]
        tt_ = np.arange(128)[None, :]
        bk = [_t5_bucket_np(kk_ - tt_ - 128 * d) for d in range(2)]
    rb = np.asarray(inp["rel_bias"], f32)
    bn = np.zeros((128, 16, 128), f32)
    for d in range(2):
        g = rb[bk[d]]
        bn[:, d * 8:(d + 1) * 8, :] = np.transpose(g, (0, 2, 1))
    c["c_bnear"] = bn
    c["c_cb"] = np.broadcast_to(rb[15][None, :], (128, 8)).astype(f32).copy()
    wuk = np.asarray(inp["w_uk"][0], f32)
    c["c_wukT"] = np.ascontiguousarray(np.transpose(wuk, (1, 2, 0)).reshape(4, 128, 128).transpose(1, 0, 2))
    wuv = np.asarray(inp["w_uv"][0], f32)
    wp = np.zeros((128, 8, 128), f32)
    for h in range(8):
        wp[:, h, (h % 2) * 64:(h % 2) * 64 + 64] = wuv[:, h, :]
    c["c_wuvp"] = wp
    lbp = np.asarray(inp["lb_params"], f32)
    c["c_lbbc"] = np.broadcast_to(lbp[None], (128, 2, 512)).astype(f32).copy()
    c["c_lbfm"] = np.ascontiguousarray(lbp.reshape(2, 4, 128).transpose(2, 0, 1))
    c["c_kvg"] = np.broadcast_to(np.asarray(inp["kv_norm_g"][0], f32)[None, :], (128, 128)).copy()
    c["c_bng"] = np.asarray(inp["b_norm_g"][0], f32).reshape(128, 1).copy()
    ln = np.stack([inp["ln1_g"][0], inp["ln1_b"][0], inp["ln2_g"][0], inp["ln2_b"][0]], axis=0).astype(f32)
    c["c_ln"] = np.broadcast_to(ln[None], (128, 4, D)).copy()
    c["c_skT"] = np.ascontiguousarray(np.stack([np.asarray(inp["sub_keys1"][0], f32).T, np.asarray(inp["sub_keys2"][0], f32).T], axis=1))
    io = np.concatenate([np.arange(16), 16 * (np.arange(16) + 1)]).astype(f32)
    c["c_iota"] = np.broadcast_to(io[None, :], (128, 32)).copy()
    return c


def make_in_maps(inp, ncores, nseq):
    c = host_consts(inp)
    shared = dict(c)
    shared["w_in"] = np.ascontiguousarray(inp["w_in"][0], dtype=np.float32)
    shared["w_br_a"] = np.ascontiguousarray(inp["w_br_a"][0], dtype=np.float32)
    shared["w_br_b"] = np.ascontiguousarray(inp["w_br_b"][0], dtype=np.float32)
    shared["w_o"] = np.ascontiguousarray(inp["w_o"][0], dtype=np.float32)
    shared["w_pq"] = np.ascontiguousarray(inp["w_pq"][0], dtype=np.float32)
    shared["u_table"] = np.ascontiguousarray(inp["u_table"][0], dtype=np.float32)
    shared["v_table"] = np.ascontiguousarray(inp["v_table"][0], dtype=np.float32)
    maps = []
    xx = np.asarray(inp["x"], dtype=np.float32)
    for ci in range(ncores):
        m = dict(shared)
        m["x"] = np.ascontiguousarray(xx[ci * nseq:(ci + 1) * nseq].reshape(nseq * S, D))
        maps.append(m)
    return maps


def kernel(**inputs):
    nseq = 32 // NCORES
    nc = build(nseq)
    maps = make_in_maps(inputs, NCORES, nseq)
    res = run_bass_kernel_spmd(nc, maps, core_ids=list(range(NCORES)))
    out = np.concatenate([r["y"].reshape(nseq, S, D) for r in res.results], axis=0)
    return out.astype(np.float32)
```

```python
import math
from contextlib import ExitStack

import numpy as np
import jax
import jax.numpy as jnp

import concourse.bass as bass
import concourse.mybir as mybir
from concourse.bass_utils import run_bass_kernel_spmd

F32 = mybir.dt.float32
BF16 = mybir.dt.bfloat16
I32 = mybir.dt.int32
U32 = mybir.dt.uint32
AF = mybir.ActivationFunctionType
ALU = mybir.AluOpType
AX = mybir.AxisListType

NCORES = 8
D = 1024
S = 2048
NT = S // 128
QA0, CKV0, QI0, KI0, WI0, QB0, FB0, IB0, GB0, GA0, GG0 = 0, 512, 640, 1152, 1216, 1224, 1736, 2248, 2760, 3272, 4296
INCOLS = 5320
ALPHA = 2.0 ** 0.25
LN_EPS = 1e-5
RMS_EPS = 1e-6
NEG = -1.0e30
NBIS = 16
KC = 2


class Prog:
    ENG = ("pe", "act", "dve", "pool", "sp")
    CAP = 30000

    def __init__(self, nc, stack):
        self.nc = nc
        self.stack = stack
        self.q = {e: [] for e in self.ENG}
        self.cur = {e: None for e in self.ENG}
        self.cnt = {e: 0 for e in self.ENG}
        self.seen = {e: {} for e in self.ENG}
        self.lastw = {}
        self.readers = {}
        self.dsem = {}
        self.nsem = 0
        self.sems = {}
        self.cap = None

    def _newsem(self, name):
        s = self.stack.enter_context(self.nc.semaphore(f"s{self.nsem}_{name}"))
        self.nsem += 1
        self.sems[id(s)] = s
        return s

    def _deps(self, eng, reads, writes):
        toks = []
        for r in reads:
            t = self.lastw.get(r)
            if t is not None:
                toks.append(t)
        for w in writes:
            t = self.lastw.get(w)
            if t is not None:
                toks.append(t)
            toks.extend(self.readers.get(w, ()))
        waits = {}
        seen = self.seen[eng]
        for (sem, val, src) in toks:
            if src == "pe" and eng == "pe":
                continue
            k = id(sem)
            if seen.get(k, 0) >= val:
                continue
            if waits.get(k, 0) < val:
                waits[k] = val
        for k, v in waits.items():
            seen[k] = v
        return [(self.sems[k], v) for k, v in waits.items()]

    def _commit(self, tok, reads, writes):
        for w in writes:
            self.lastw[w] = tok
            self.readers[w] = []
        for r in reads:
            self.readers.setdefault(r, []).append(tok)

    def replay(self, items):
        for it in items:
            if it[0] == "op":
                self.op(*it[1:])
            else:
                self.dma(*it[1:])

    def op(self, eng, fn, reads=(), writes=()):
        if self.cap is not None:
            self.cap.append(("op", eng, fn, tuple(reads), tuple(writes)))
            return None
        waits = self._deps(eng, reads, writes)
        if self.cur[eng] is None or self.cnt[eng] >= self.CAP:
            self.cur[eng] = self._newsem(eng)
            self.cnt[eng] = 0
        self.cnt[eng] += 1
        tok = (self.cur[eng], self.cnt[eng], eng)
        self.q[eng].append((fn, waits, tok[0], 1))
        self._commit(tok, reads, writes)
        return tok

    def dma(self, queue, slot, fn, reads=(), writes=()):
        if self.cap is not None:
            self.cap.append(("dma", queue, slot, fn, tuple(reads), tuple(writes)))
            return None
        waits = self._deps(queue, reads, writes)
        if slot not in self.dsem:
            self.dsem[slot] = [self._newsem("d" + str(slot)), 0]
        ent = self.dsem[slot]
        ent[1] += 16
        tok = (ent[0], ent[1], "dma")
        self.q[queue].append((fn, waits, tok[0], 16))
        self._commit(tok, reads, writes)
        return tok

    def barrier(self):
        toks = []
        for e in self.ENG:
            if self.cur[e] is not None:
                toks.append((self.cur[e], self.cnt[e], e))
        for slot, ent in self.dsem.items():
            toks.append((ent[0], ent[1], "dma"))
        for e in self.ENG:
            waits = {}
            seen = self.seen[e]
            for (sem, val, src) in toks:
                k = id(sem)
                if seen.get(k, 0) >= val:
                    continue
                waits[k] = val
                seen[k] = val
            self.q[e].append((None, [(self.sems[k], v) for k, v in waits.items()], None, 0))
        self.lastw.clear()
        self.readers.clear()

    def emit(self):
        nc = self.nc
        with nc.Block() as block:
            def run(engname):
                def f(engine):
                    for (fn, waits, sem, inc) in self.q[engname]:
                        for (s, v) in waits:
                            engine.wait_ge(s, v)
                        if fn is not None:
                            fn(engine).then_inc(sem, inc)
                return f
            block.tensor(run("pe"))
            block.scalar(run("act"))
            block.vector(run("dve"))
            block.gpsimd(run("pool"))
            block.sync(run("sp"))


def build(nseq, dbg=None):
    nc = bass.Bass("TRN2", target_bir_lowering=False)
    ntok = nseq * S

    def din(name, shape, dt=F32):
        return nc.dram_tensor(name, list(shape), dt, kind="ExternalInput").ap()

    x = din("x", [ntok, D])
    w_in = din("w_in", [D, INCOLS])
    w_br_a = din("w_br_a", [512, D])
    w_br_b = din("w_br_b", [512, D])
    w_o = din("w_o", [D, D])
    w_pq = din("w_pq", [D, 2048])
    u_table = din("u_table", [16384, D])
    v_table = din("v_table", [16384, D])
    c_ident = din("c_ident", [128, 128])
    c_cum = din("c_cum", [128, 386])
    c_negvis = din("c_negvis", [128, 128])
    c_bnear = din("c_bnear", [128, 16, 128])
    c_cb = din("c_cb", [128, 8])
    c_wukT = din("c_wukT", [128, 4, 128])
    c_wuvp = din("c_wuvp", [128, 8, 128])
    c_lbbc = din("c_lbbc", [128, 2, 512])
    c_lbfm = din("c_lbfm", [128, 2, 4])
    c_kvg = din("c_kvg", [128, 128])
    c_bng = din("c_bng", [128, 1])
    c_ln = din("c_ln", [128, 4, D])
    c_skT = din("c_skT", [128, 2, 128])
    c_iota = din("c_iota", [128, 32])
    y = nc.dram_tensor("y", [ntok, D], F32, kind="ExternalOutput").ap()
    dbg_out = None
    if dbg == "dsa" or dbg == "hgrn":
        dbg_out = nc.dram_tensor("dbg", [512, S], F32, kind="ExternalOutput").ap()
    if dbg == "h1":
        dbg_out = nc.dram_tensor("dbg", [S, D], F32, kind="ExternalOutput").ap()

    with ExitStack() as top:
        P = Prog(nc, top)

        tcount = [0]

        def T(st, name, shape, dt):
            tcount[0] += 1
            return st.enter_context(nc.sbuf_tensor(f"{name}_{tcount[0]}", list(shape), dt))

        def mm(out, lhsT, rhs, start=True, stop=True, r=(), w=()):
            P.op("pe", lambda e: e.matmul(out, lhsT=lhsT, rhs=rhs, start=start, stop=stop), r, w)

        def tr(out, in_, ident, r=(), w=()):
            P.op("pe", lambda e: e.transpose(out=out, in_=in_, identity=ident), r, w)

        def act(out, in_, func, r=(), w=(), bias=None, scale=None, accum=None):
            kw = {}
            if bias is not None:
                kw["bias"] = bias
            if scale is not None:
                kw["scale"] = scale
            if accum is not None:
                kw["accum_out"] = accum
            P.op("act", lambda e: e.activation(out=out, in_=in_, func=func, **kw), r, w)

        def ts(eng, out, in0, s1, s2, op0, op1=None, r=(), w=(), accum=None):
            kw = {}
            if op1 is not None:
                kw["op1"] = op1
            if accum is not None:
                kw["accum_out"] = accum
            P.op(eng, lambda e: e.tensor_scalar(out, in0, s1, s2, op0, **kw), r, w)

        def tt(eng, out, in0, in1, op, r=(), w=()):
            P.op(eng, lambda e: e.tensor_tensor(out, in0, in1, op), r, w)

        def stt(out, in0, scalar, in1, op0, op1, r=(), w=(), accum=None):
            kw = {}
            if accum is not None:
                kw["accum_out"] = accum
            P.op("dve", lambda e: e.scalar_tensor_tensor(out=out, in0=in0, scalar=scalar, in1=in1, op0=op0, op1=op1, **kw), r, w)

        def cp(eng, out, in_, r=(), w=()):
            if eng == "act":
                P.op("act", lambda e: e.activation(out=out, in_=in_, func=AF.Copy), r, w)
            else:
                P.op(eng, lambda e: e.tensor_copy(out=out, in_=in_), r, w)

        def opk(eng, meth, r, w, **kw):
            P.op(eng, lambda e: getattr(e, meth)(**kw), r, w)

        def red(out, in_, op, r=(), w=()):
            P.op("dve", lambda e: e.tensor_reduce(out=out, in_=in_, axis=AX.X, op=op), r, w)

        def dma(queue, slot, out, in_, r=(), w=()):
            P.dma(queue, slot, lambda e: e.dma_start(out=out, in_=in_), r, w)

        def gather(slot, out, table, idx_ap, r=(), w=()):
            P.dma("pool", slot, lambda e: e.indirect_dma_start(
                out=out, out_offset=None, in_=table,
                in_offset=bass.IndirectOffsetOnAxis(ap=idx_ap, axis=0)), r, w)

        psb = [top.enter_context(nc.psum_tensor(f"psb{i}", [128, 512], F32)) for i in range(6)]
        pst = [top.enter_context(nc.psum_tensor(f"pst{i}", [128, 1024], BF16)) for i in range(2)]
        rot = {"pool": [0, 1, 2, 3, 4, 5], "i": 0, "t": 0}

        def nb():
            b = rot["pool"][rot["i"] % len(rot["pool"])]
            rot["i"] += 1
            return psb[b], f"ps{b}"

        def nbt():
            b = rot["t"] % 2
            rot["t"] += 1
            return pst[b], f"pst{b}"

        idf = T(top, "idf", [128, 128], F32)
        idb = T(top, "idb", [128, 128], BF16)
        onesb = T(top, "onesb", [128, 128], BF16)
        cum = T(top, "cum", [128, 386], F32)
        negvis = T(top, "negvis", [128, 128], F32)
        cb = T(top, "cb", [128, 8], F32)
        wukT = T(top, "wukT", [128, 4, 128], BF16)
        wuvp = T(top, "wuvp", [128, 8, 128], BF16)
        lbt = T(top, "lbt", [128, 512], F32)
        omlt = T(top, "omlt", [128, 512], F32)
        lbf = T(top, "lbf", [128, 4], F32)
        omlf = T(top, "omlf", [128, 4], F32)
        nomlf = T(top, "nomlf", [128, 4], F32)
        kvg = T(top, "kvg", [128, 128], F32)
        bng = T(top, "bng", [128, 1], F32)
        skT = T(top, "skT", [128, 2, 128], BF16)
        iot = T(top, "iot", [128, 32], F32)
        thrc = T(top, "thrc", [128, 1], F32)
        wstg = [T(top, f"wstg{i}", [128, 8, 128], F32) for i in range(3)]
        wst = {"i": 0}

        with ExitStack() as st0:
            tmp1 = T(st0, "tmp1", [128, 16, 128], F32)
            tmp2 = T(st0, "tmp2", [128, 2, 512], F32)
            tmp3 = T(st0, "tmp3", [128, 2, 4], F32)
            dma("sp", "c0", idf[:], c_ident, w=["idf"])
            cp("dve", idb[:], idf[:], r=["idf"], w=["idb"])
            P.op("dve", lambda e: e.memset(onesb[:], 1.0), (), ["onesb"])
            P.op("dve", lambda e: e.memset(thrc[:], -1.0e29), (), ["thrc"])
            dma("sp", "c1", cum[:], c_cum, w=["cum"])
            dma("sp", "c2", negvis[:], c_negvis, w=["negvis"])
            dma("sp", "c4", cb[:], c_cb, w=["cb"])
            dma("sp", "c5", tmp1[:, 0:4, :], c_wukT, w=["tmp1"])
            cp("dve", wukT[:], tmp1[:, 0:4, :], r=["tmp1"], w=["wukT"])
            dma("sp", "c6", tmp1[:, 0:8, :], c_wuvp, w=["tmp1"])
            cp("dve", wuvp[:], tmp1[:, 0:8, :], r=["tmp1"], w=["wuvp"])
            dma("sp", "c7", tmp2[:], c_lbbc, w=["tmp2"])
            tt("dve", lbt[:], tmp2[:, 0, :], tmp2[:, 1, :], ALU.subtract, r=["tmp2"], w=["lbt"])
            act(lbt[:], lbt[:], AF.Sigmoid, r=["lbt"], w=["lbt"])
            ts("dve", omlt[:], lbt[:], -1.0, 1.0, ALU.mult, ALU.add, r=["lbt"], w=["omlt"])
            dma("sp", "c8", tmp3[:], c_lbfm, w=["tmp3"])
            tt("dve", lbf[:], tmp3[:, 0, :], tmp3[:, 1, :], ALU.subtract, r=["tmp3"], w=["lbf"])
            act(lbf[:], lbf[:], AF.Sigmoid, r=["lbf"], w=["lbf"])
            ts("dve", omlf[:], lbf[:], -1.0, 1.0, ALU.mult, ALU.add, r=["lbf"], w=["omlf"])
            ts("dve", nomlf[:], omlf[:], -1.0, None, ALU.mult, r=["omlf"], w=["nomlf"])
            dma("sp", "c9", kvg[:], c_kvg, w=["kvg"])
            dma("sp", "c10", bng[:], c_bng, w=["bng"])
            dma("sp", "c11", tmp1[:, 0:2, :], c_skT, w=["tmp1"])
            cp("dve", skT[:], tmp1[:, 0:2, :], r=["tmp1"], w=["skT"])
            dma("sp", "c12", iot[:], c_iota, w=["iot"])
            P.barrier()

        uv = nc.dram_tensor("uv_scratch", [16384, 2048], BF16).ap()

        def wload(dst, src2d, nkc, wd, key, eng="pool"):
            i = wst["i"] % 3
            wst["i"] += 1
            stg = wstg[i]
            dma("sp", f"wstg{i}", stg[:, 0:nkc, 0:wd], src2d.rearrange("(kc p) c -> p kc c", p=128), w=[f"wstg{i}"])
            cp(eng, dst, stg[:, 0:nkc, 0:wd], r=[f"wstg{i}"], w=[key])

        for b in range(nseq):
            t0 = b * S
            with ExitStack() as sq:
                xT = T(sq, "xT", [128, 8, S], BF16)
                oTa = T(sq, "oTa", [128, 4, S], BF16)
                oTb = T(sq, "oTb", [128, 4, S], BF16)
                wt = [T(sq, f"wt{i}", [128, 8, 128], BF16) for i in range(3)]
                wti = {"i": 0}

                def nwt():
                    i = wti["i"] % 3
                    wti["i"] += 1
                    return wt[i], f"wt{i}"

                with ExitStack() as sx:
                    xs = [T(sx, f"xs{i}", [128, D], F32) for i in range(2)]
                    xb = [T(sx, f"xb{i}", [128, D], BF16) for i in range(2)]
                    for i in range(NT):
                        p = i % 2
                        dma("sp", f"xs{p}", xs[p][:], x[t0 + 128 * i:t0 + 128 * (i + 1), :], w=[f"xs{p}"])
                        cp("dve" if i % 2 == 0 else "act", xb[p][:], xs[p][:], r=[f"xs{p}"], w=[f"xb{p}"])
                        pt, ptk = nbt()
                        for c in range(8):
                            tr(pt[:, c * 128:(c + 1) * 128], xb[p][:, c * 128:(c + 1) * 128], idb[:], r=[f"xb{p}", "idb"], w=[ptk])
                        cp("act" if i % 2 == 0 else "dve", xT[:, :, 128 * i:128 * (i + 1)], pt[:].rearrange("p (c t) -> p c t", c=8), r=[ptk], w=[f"xT{i // 4}"])
                    P.barrier()

                def proj_fm(col0, evac, wtile=None, wkey=None):
                    if wtile is None:
                        wtile, wkey = nwt()
                        wload(wtile[:], w_in[:, col0:col0 + 128], 8, 128, wkey)
                    for tb in range(4):
                        ps, psk = nb()
                        for kc in range(8):
                            mm(ps[:, :], wtile[:, kc, :], xT[:, kc, tb * 512:(tb + 1) * 512], start=(kc == 0), stop=(kc == 7), r=[wkey, f"xT{tb}"], w=[psk])
                        evac(tb, ps, psk)

                with ExitStack() as sa:
                    qlat = T(sa, "qlat", [128, 8, S], BF16)
                    qiT = T(sa, "qiT", [128, 4, S], BF16)
                    qaT = [T(sa, f"qaT{i}", [128, S], BF16) for i in range(2)]
                    kiT = T(sa, "kiT", [128, S], BF16)
                    ckv = T(sa, "ckv", [128, NT, 128], BF16)
                    ckvT = T(sa, "ckvT", [128, S], BF16)
                    wia = T(sa, "wia", [128, NT, 8], F32)
                    wck = T(sa, "wck", [128, 8, 136], BF16)
                    sc = T(sa, "sc", [128, S], F32)
                    rl = [T(sa, f"rl{i}", [128, 512], F32) for i in range(2)]
                    nm = [T(sa, f"nm{i}", [128, S], BF16) for i in range(2)]
                    pT = [T(sa, f"pT{i}", [128, S], BF16) for i in range(2)]
                    olat = [T(sa, f"olat{i}", [128, 8, 128], BF16) for i in range(2)]
                    sm = T(sa, "sm", [128, 16], F32)
                    rzs = [T(sa, f"rzs{i}", [128, 128], F32) for i in range(2)]
                    ssq = T(sa, "ssq", [128, 2], F32)
                    cjunk = T(sa, "cjunk", [128, 128], F32)
                    opk("dve", "memset", (), ["ssq0", "ssq1"], ap=ssq[:], constant=0.0)
                    opk("dve", "memset", (), ["sm0", "sm1", "sm2", "sm3", "sm4", "sm5"], ap=sm[:], constant=0.0)
                    bpp = T(sa, "bpp", [128, 16, 128], F32)
                    dma("sp", "c3", bpp[:], c_bnear, w=["bpp"])
                    for dh in range(16):
                        ts("dve", bpp[:, dh, :], bpp[:, dh, :], cb[:, dh % 8:dh % 8 + 1], 8.0, ALU.subtract, ALU.mult, r=["bpp", "cb"], w=["bpp"])

                    for m in range(4):
                        qa_t = qaT[m % 2]
                        qk = f"qaT{m % 2}"

                        def ev_qa(tb, ps, psk, qa_t=qa_t, qk=qk):
                            cp("act" if tb % 2 == 0 else "dve", qa_t[:, tb * 512:(tb + 1) * 512], ps[:, :], r=[psk], w=[qk])
                        proj_fm(QA0 + m * 128, ev_qa)
                        for hl in range(2):
                            h = 2 * m + hl
                            for tb in range(4):
                                ps, psk = nb()
                                mm(ps[:, :], wukT[64 * hl:64 * hl + 64, m, :], qa_t[64 * hl:64 * hl + 64, tb * 512:(tb + 1) * 512], r=["wukT", qk], w=[psk])
                                cp("act" if tb % 2 == 1 else "dve", qlat[:, h, tb * 512:(tb + 1) * 512], ps[:, :], r=[psk], w=[f"qlat{h}"])
                    for m in range(4):
                        def ev_qi(tb, ps, psk, m=m):
                            cp("act" if tb % 2 == 0 else "dve", qiT[:, m, tb * 512:(tb + 1) * 512], ps[:, :], r=[psk], w=[f"qiT{m}"])
                        proj_fm(QI0 + m * 128, ev_qi)
                    wk_t, wk_k = nwt()
                    wload(wk_t[:, :, 0:64], w_in[:, KI0:KI0 + 64], 8, 64, wk_k)
                    wload(wk_t[:, :, 64:128], w_in[:, KI0:KI0 + 64], 8, 64, wk_k)

                    def ev_ki(tb, ps, psk):
                        cp("act" if tb % 2 == 0 else "dve", kiT[:, tb * 512:(tb + 1) * 512], ps[:, :], r=[psk], w=["kiT"])
                    proj_fm(None, ev_ki, wtile=wk_t, wkey=wk_k)
                    wload(wck[:, :, 0:128], w_in[:, CKV0:CKV0 + 128], 8, 128, "wck")
                    wload(wck[:, :, 128:136], w_in[:, WI0:WI0 + 8], 8, 8, "wck")
                    for i in range(NT):
                        ps, psk = nb()
                        for kc in range(8):
                            mm(ps[:, 0:136], xT[:, kc, 128 * i:128 * (i + 1)], wck[:, kc, :], start=(kc == 0), stop=(kc == 7), r=["wck", f"xT{i // 4}"], w=[psk])
                        act(cjunk[:], ps[:, 0:128], AF.Square, r=[psk], w=["ssq0"], accum=ssq[:, 0:1])
                        act(ssq[:, 1:2], ssq[:, 0:1], AF.Ln, r=["ssq0"], w=["ssq1"], bias=RMS_EPS, scale=1.0 / 128.0)
                        act(ssq[:, 1:2], ssq[:, 1:2], AF.Exp, r=["ssq1"], w=["ssq1"], scale=-0.5)
                        stt(ckv[:, i, :], ps[:, 0:128], ssq[:, 1:2], kvg[:], ALU.mult, ALU.mult, r=[psk, "ssq1", "kvg"], w=[f"ckv{i}"])
                        cp("act", wia[:, i, :], ps[:, 128:136], r=[psk], w=[f"wia{i}"])
                        pt, ptk = nbt()
                        tr(pt[:, 0:128], ckv[:, i, :], idb[:], r=[f"ckv{i}", "idb"], w=[ptk])
                        cp("dve", ckvT[:, 128 * i:128 * (i + 1)], pt[:, 0:128], r=[ptk], w=[f"ckvT{i}"])

                    sck = [f"sc{kb}" for kb in range(4)]

                    def IDX(j):
                        nv = 128 * (j + 1)
                        nlo = 128 * j + 64
                        for h in range(8):
                            hl, m = h % 2, h // 2
                            for kb in range((nv + 511) // 512):
                                c0 = kb * 512
                                cw = min(512, nv - c0)
                                ps, psk = nb()
                                mm(ps[:, 0:cw], qiT[64 * hl:64 * hl + 64, m, 128 * j:128 * (j + 1)], kiT[64 * hl:64 * hl + 64, c0:c0 + cw], r=[f"qiT{m}", "kiT"], w=[psk])
                                ri = (h * 4 + kb) % 2
                                act(rl[ri][:, 0:cw], ps[:, 0:cw], AF.Relu, r=[psk], w=[f"rl{ri}"])
                                if h == 0:
                                    ts("dve", sc[:, c0:c0 + cw], rl[ri][:, 0:cw], wia[:, j, 0:1], None, ALU.mult, r=[f"rl{ri}", f"wia{j}"], w=[f"sc{kb}"])
                                else:
                                    stt(sc[:, c0:c0 + cw], rl[ri][:, 0:cw], wia[:, j, h:h + 1], sc[:, c0:c0 + cw], ALU.mult, ALU.add, r=[f"rl{ri}", f"wia{j}", f"sc{kb}"], w=[f"sc{kb}"])
                        tt("dve", sc[:, 128 * j:128 * (j + 1)], sc[:, 128 * j:128 * (j + 1)], negvis[:], ALU.add, r=sck + ["negvis"], w=sck)
                        if j >= 2:
                            red(sm[:, 0:1], sc[:, 0:nlo], ALU.min, r=sck, w=["sm0"])
                            red(sm[:, 1:2], sc[:, 0:nv], ALU.max, r=sck, w=["sm1"])
                            tt("dve", sm[:, 2:3], sm[:, 1:2], sm[:, 0:1], ALU.subtract, r=["sm0", "sm1"], w=["sm2"])
                            stt(sm[:, 3:4], sm[:, 2:3], -0.5, sm[:, 0:1], ALU.mult, ALU.subtract, r=["sm2", "sm0"], w=["sm3"])

                    def BIS(j, k):
                        nv = 128 * (j + 1)
                        f = 2.0 ** -(k + 1)
                        nmn, nmnk = nm[j % 2], f"nm{j % 2}"
                        act(nmn[:, 0:nv], sc[:, 0:nv], AF.Sign, r=sck + ["sm3"], w=[nmnk, "sm4"], bias=sm[:, 3:4], accum=sm[:, 4:5])
                        stt(sm[:, 5:6], sm[:, 4:5], 510.5 - nv, sm[:, 2:3], ALU.is_ge, ALU.mult, r=["sm4", "sm2"], w=["sm5"])
                        stt(sm[:, 0:1], sm[:, 5:6], f, sm[:, 0:1], ALU.mult, ALU.add, r=["sm5", "sm0"], w=["sm0"])
                        if k + 1 < NBIS:
                            stt(sm[:, 3:4], sm[:, 2:3], -0.5 * f, sm[:, 0:1], ALU.mult, ALU.subtract, r=["sm2", "sm0"], w=["sm3"])

                    def NM(j):
                        nv = 128 * (j + 1)
                        thr, thrk = (sm[:, 0:1], "sm0") if j >= 2 else (thrc[:, 0:1], "thrc")
                        ts("dve", nm[j % 2][:, 0:nv], sc[:, 0:nv], thr, -32768.0, ALU.is_lt, ALU.mult, r=sck + [thrk], w=[f"nm{j % 2}"])

                    def ATT_qk(j, h):
                        nmj, nmk = nm[j % 2], f"nm{j % 2}"
                        pj = pT[h % 2]
                        pk = f"pT{h % 2}"
                        for g in range((j + 4) // 4):
                            ps, psk = nb()
                            tiles = list(range(4 * g, min(4 * g + 4, j + 1)))
                            for ii_, i in enumerate(tiles):
                                o_ = ps[:, ii_ * 128:(ii_ + 1) * 128]
                                near = (i >= j - 1)
                                mm(o_, ckvT[:, 128 * i:128 * (i + 1)], qlat[:, h, 128 * j:128 * (j + 1)], start=True, stop=False, r=[f"ckvT{i}", f"qlat{h}"], w=[psk])
                                mm(o_, nmj[:, 128 * i:128 * (i + 1)], idb[:], start=False, stop=(not near), r=[nmk, "idb"], w=[psk])
                                if near:
                                    mm(o_, idf[:], bpp[:, (j - i) * 8 + h, :], start=False, stop=True, r=["idf", "bpp"], w=[psk])
                            ncol = 128 * len(tiles)
                            act(pj[:, 512 * g:512 * g + ncol], ps[:, 0:ncol], AF.Exp, r=[psk, "cb"], w=[pk], bias=cb[:, h:h + 1], scale=0.125)

                    def ATT_pv(j, h):
                        ol, olk = olat[j % 2], f"olat{j % 2}"
                        pj = pT[h % 2]
                        pk = f"pT{h % 2}"
                        ps, psk = nb()
                        for i in range(j + 1):
                            mm(ps[:, 0:128], ckv[:, i, :], pj[:, 128 * i:128 * (i + 1)], start=(i == 0), stop=(i == j), r=[f"ckv{i}", pk], w=[psk])
                        for i in range(j + 1):
                            mm(ps[:, 128:256], onesb[:], pj[:, 128 * i:128 * (i + 1)], start=(i == 0), stop=(i == j), r=["onesb", pk], w=[psk])
                        rz = rzs[h % 2]
                        rzk = f"rzs{h % 2}"
                        P.op("dve", lambda e, rz=rz, ps=ps: e.reciprocal(out=rz[:], in_=ps[:, 128:256]), [psk], [rzk])
                        tt("dve", ol[:, h, :], ps[:, 0:128], rz[:], ALU.mult, r=[psk, rzk], w=[olk])

                    def ATT_tail(j):
                        ol, olk = olat[j % 2], f"olat{j % 2}"
                        for m in range(4):
                            ps, psk = nb()
                            mm(ps[:, 0:128], wuvp[:, 2 * m, :], ol[:, 2 * m, :], start=True, stop=False, r=["wuvp", olk], w=[psk])
                            mm(ps[:, 0:128], wuvp[:, 2 * m + 1, :], ol[:, 2 * m + 1, :], start=False, stop=True, r=["wuvp", olk], w=[psk])
                            cp("act", oTa[:, m, 128 * j:128 * (j + 1)], ps[:, 0:128], r=[psk], w=[f"oTa{j // 4}"])

                    if b == 0:
                        for r_ in range(16):
                            rs_ = slice(1024 * r_, 1024 * (r_ + 1))
                            P.dma("pool", "uvtab", lambda e, rs_=rs_: e.dma_start(out=uv[rs_, 0:1024], in_=u_table[rs_, :]), (), ["uvtab"])
                            P.dma("pool", "uvtab", lambda e, rs_=rs_: e.dma_start(out=uv[rs_, 1024:2048], in_=v_table[rs_, :]), (), ["uvtab"])
                    IDX(0)
                    NM(0)
                    for j in range(NT):
                        nxt = j + 1
                        if nxt < NT:
                            IDX(nxt)
                        ATT_qk(j, 0)
                        for h in range(8):
                            if h + 1 < 8:
                                ATT_qk(j, h + 1)
                            ATT_pv(j, h)
                            if nxt < NT and nxt >= 2:
                                for k in range(NBIS * h // 8, NBIS * (h + 1) // 8):
                                    BIS(nxt, k)
                        if nxt < NT:
                            NM(nxt)
                        ATT_tail(j)
                    P.barrier()

                if dbg == "dsa":
                    with ExitStack() as sd:
                        dtmp = T(sd, "dtmp", [128, 4, S], F32)
                        cp("dve", dtmp[:], oTa[:], w=["dtmp"])
                        dma("sp", "dbg", dbg_out.rearrange("(m p) t -> p m t", p=128), dtmp[:], r=["dtmp"])
                        P.barrier()
                    break

                with ExitStack() as sb:
                    qbT = T(sb, "qbT", [128, 2, S], BF16)
                    kT = T(sb, "kT", [128, 2, S], BF16)
                    sgT = T(sb, "sgT", [128, 2, S], BF16)
                    logf = T(sb, "logf", [128, NT, 256], F32)
                    kk = T(sb, "kk", [128, NT, 256], BF16)
                    iib = T(sb, "iib", [128, NT, 256], BF16)
                    w5 = [T(sb, f"w5{i}", [128, 8, 256], BF16) for i in range(2)]
                    sgtmp = [T(sb, f"sgtmp{i}", [128, 512], F32) for i in range(2)]
                    ftmp = [T(sb, f"ftmp{i}", [128, 256], F32) for i in range(2)]
                    e14 = [T(sb, f"e14{i}", [128, 256], F32) for i in range(4)]
                    e2 = [T(sb, f"e2{i}", [128, 128], F32) for i in range(4)]
                    e3 = [T(sb, f"e3{i}", [128, 128], F32) for i in range(4)]
                    dec = [T(sb, f"dec{i}", [128, 2], F32) for i in range(4)]
                    qin = [T(sb, f"qin{i}", [128, 128], BF16) for i in range(4)]
                    q2 = [T(sb, f"q2{i}", [128, 128], F32) for i in range(4)]
                    kin = [T(sb, f"kin{i}", [128, 128], BF16) for i in range(4)]
                    k3 = [T(sb, f"k3{i}", [128, 128], BF16) for i in range(4)]
                    am = [T(sb, f"am{i}", [128, 128], BF16) for i in range(4)]
                    s32 = [[T(sb, f"s32{i}{k}", [128, 128], F32) for k in range(2)] for i in range(2)]
                    sbf = [[T(sb, f"sbf{i}{k}", [128, 128], BF16) for k in range(2)] for i in range(2)]
                    sqb = [T(sb, f"sq{i}", [128, 128], BF16) for i in range(2)]
                    sd_ = [T(sb, f"sd{i}", [128, 128], F32) for i in range(2)]
                    o1 = [T(sb, f"o1{i}", [128, 128], F32) for i in range(2)]
                    for hp in range(2):
                        for hl in range(2):
                            h = 2 * hp + hl

                            def ev_q(tb, ps, psk, hl=hl):
                                cp("act" if tb % 2 == 0 else "dve", qbT[:, hl, tb * 512:(tb + 1) * 512], ps[:, :], r=[psk], w=[f"qbT{hl}"])
                            proj_fm(QB0 + h * 128, ev_q)

                            def ev_f(tb, ps, psk, hl=hl, h=h):
                                sg = sgtmp[tb % 2]
                                sgk = f"sgtmp{tb % 2}"
                                act(sg[:], ps[:, :], AF.Sigmoid, r=[psk], w=[sgk])
                                ts("dve", kT[:, hl, tb * 512:(tb + 1) * 512], sg[:], nomlf[:, h:h + 1], omlf[:, h:h + 1], ALU.mult, ALU.add, r=[sgk], w=[f"kT{hl}"])
                            proj_fm(FB0 + h * 128, ev_f)

                            def ev_g(tb, ps, psk, hl=hl):
                                act(sgT[:, hl, tb * 512:(tb + 1) * 512], ps[:, :], AF.Silu, r=[psk], w=[f"sgT{hl}"])
                            proj_fm(GB0 + h * 128, ev_g)
                        c2 = 2 * hp * 128
                        wload(w5[0][:, :, 0:128], w_in[:, FB0 + c2:FB0 + c2 + 128], 8, 128, "w50")
                        wload(w5[0][:, :, 128:256], w_in[:, FB0 + c2 + 128:FB0 + c2 + 256], 8, 128, "w50")
                        wload(w5[1][:, :, 0:128], w_in[:, IB0 + c2:IB0 + c2 + 128], 8, 128, "w51")
                        wload(w5[1][:, :, 128:256], w_in[:, IB0 + c2 + 128:IB0 + c2 + 256], 8, 128, "w51")
                        for i in range(NT):
                            ps, psk = nb()
                            for kc in range(8):
                                mm(ps[:, 0:256], xT[:, kc, 128 * i:128 * (i + 1)], w5[0][:, kc, :], start=(kc == 0), stop=(kc == 7), r=["w50", f"xT{i // 4}"], w=[psk])
                            for kc in range(8):
                                mm(ps[:, 256:512], xT[:, kc, 128 * i:128 * (i + 1)], w5[1][:, kc, :], start=(kc == 0), stop=(kc == 7), r=["w51", f"xT{i // 4}"], w=[psk])
                            ft = ftmp[i % 2]
                            fk = f"ftmp{i % 2}"
                            act(ft[:], ps[:, 0:256], AF.Sigmoid, r=[psk], w=[fk])
                            tt("dve", ft[:], ft[:], omlt[:, c2:c2 + 256], ALU.mult, r=[fk, "omlt"], w=[fk])
                            tt("dve", logf[:, i, :], ft[:], lbt[:, c2:c2 + 256], ALU.add, r=[fk, "lbt"], w=[f"logf{i}"])
                            ts("dve", kk[:, i, :], logf[:, i, :], -1.0, 1.0, ALU.mult, ALU.add, r=[f"logf{i}"], w=[f"kk{i}"])
                            cp("act", iib[:, i, :], ps[:, 256:512], r=[psk], w=[f"iib{i}"])
                        lfk = [f"logf{i}" for i in range(NT)]
                        for q4 in range(4):
                            act(logf[:, 4 * q4:4 * q4 + 4, :], logf[:, 4 * q4:4 * q4 + 4, :], AF.Ln, r=lfk[4 * q4:4 * q4 + 4], w=lfk[4 * q4:4 * q4 + 4])
                        for hl in range(2):
                            opk("dve", "memset", (), [f"s32{hl}0"], ap=s32[hl][0][:], constant=0.0)
                        rot["pool"] = [0, 1, 2, 3]
                        pob = {(0, 0): (psb[4], "ps4"), (0, 1): (psb[5], "ps5"),
                               (1, 0): (pst[0][:].bitcast(F32), "pst0"), (1, 1): (pst[1][:].bitcast(F32), "pst1")}

                        def HA(i, hl):
                            q = 2 * hl + (i % 2)
                            sfx = f"{hl}{i % 2}"
                            tsl = slice(128 * i, 128 * (i + 1))
                            lf = logf[:, i, hl * 128:(hl + 1) * 128]
                            pe_, pek = nb()
                            mm(pe_[:, 0:128], lf, cum[:, 0:128], r=[f"logf{i}", "cum"], w=[pek])
                            mm(pe_[:, 128:256], lf, cum[:, 128:256], r=[f"logf{i}", "cum"], w=[pek])
                            mm(pe_[:, 256:258], lf, cum[:, 256:258], r=[f"logf{i}", "cum"], w=[pek])
                            mm(pe_[:, 384:512], cum[:, 258:386], lf, r=[f"logf{i}", "cum"], w=[pek])
                            act(e14[q][:], pe_[:, 0:256], AF.Exp, r=[pek], w=[f"e14{sfx}"])
                            act(e2[q][:], pe_[:, 0:128], AF.Exp, r=[pek], w=[f"e2{sfx}"], scale=-1.0)
                            act(dec[q][:], pe_[:, 256:258], AF.Exp, r=[pek], w=[f"dec{sfx}"])
                            act(e3[q][:], pe_[:, 384:512], AF.Exp, r=[pek], w=[f"e3{sfx}"])
                            tt("dve", qin[q][:], qbT[:, hl, tsl], e14[q][:, 0:128], ALU.mult, r=[f"qbT{hl}", f"e14{sfx}"], w=[f"qin{sfx}"])
                            tt("dve", q2[q][:], qbT[:, hl, tsl], e14[q][:, 128:256], ALU.mult, r=[f"qbT{hl}", f"e14{sfx}"], w=[f"q2{sfx}"])
                            tt("dve", kin[q][:], kT[:, hl, tsl], e2[q][:], ALU.mult, r=[f"kT{hl}", f"e2{sfx}"], w=[f"kin{sfx}"])
                            tt("dve", k3[q][:], kk[:, i, hl * 128:(hl + 1) * 128], e3[q][:], ALU.mult, r=[f"kk{i}", f"e3{sfx}"], w=[f"k3{sfx}"])
                            pa, pak = nb()
                            mm(pa[:, 0:128], kin[q][:], qin[q][:], r=[f"kin{sfx}", f"qin{sfx}"], w=[pak])
                            tt("dve", am[q][:], pa[:, 0:128], cum[:, 128:256], ALU.mult, r=[pak, "cum"], w=[f"am{sfx}"])
                            po, pok = pob[(hl, i % 2)]
                            iv = iib[:, i, hl * 128:(hl + 1) * 128]
                            mm(po[:, 0:128], iv, am[q][:], start=True, stop=False, r=[f"iib{i}", f"am{sfx}"], w=[pok])

                        def HB(i):
                            tsl = slice(128 * i, 128 * (i + 1))
                            for c in range(2):
                                cur, nxt = c, 1 - c
                                for hl in range(2):
                                    q = 2 * hl + (i % 2)
                                    sfx = f"{hl}{i % 2}"
                                    po, pok = pob[(hl, i % 2)]
                                    mm(po[:, 64 * c:64 * c + 64], s32[hl][cur][:], q2[q][:, 64 * c:64 * c + 64], start=False, stop=(c == 1), r=[f"s32{hl}{cur}", f"q2{sfx}"], w=[pok])
                                    pss, pssk = nb()
                                    mm(pss[:, 0:128], k3[q][64 * c:64 * c + 64, :], iib[64 * c:64 * c + 64, i, hl * 128:(hl + 1) * 128], r=[f"k3{sfx}", f"iib{i}"], w=[pssk])
                                    stt(s32[hl][nxt][:], s32[hl][cur][:], dec[q][:, c:c + 1], pss[:, 0:128], ALU.mult, ALU.add, r=[f"s32{hl}{cur}", f"dec{sfx}", pssk], w=[f"s32{hl}{nxt}"])
                            for hl in range(2):
                                h = 2 * hp + hl
                                po, pok = pob[(hl, i % 2)]
                                act(sqb[hl][:], po[:, 0:128], AF.Square, r=[pok], w=[f"sq{hl}"])
                                mm(po[:, 128:256], onesb[:], sqb[hl][:], r=["onesb", f"sq{hl}"], w=[pok])
                                act(sd_[hl][:], po[:, 128:256], AF.Ln, r=[pok], w=[f"sd{hl}"], bias=RMS_EPS, scale=1.0 / 128.0)
                                act(sd_[hl][:], sd_[hl][:], AF.Exp, r=[f"sd{hl}"], w=[f"sd{hl}"], scale=-0.5)
                                tt("dve", o1[hl][:], po[:, 0:128], sd_[hl][:], ALU.mult, r=[pok, f"sd{hl}"], w=[f"o1{hl}"])
                                stt(oTb[:, h, tsl], o1[hl][:], bng[:, 0:1], sgT[:, hl, tsl], ALU.mult, ALU.mult, r=[f"o1{hl}", "bng", f"sgT{hl}"], w=[f"oTb{i // 4}"])

                        HA(0, 0)
                        HA(0, 1)
                        for i in range(NT):
                            if i + 1 < NT:
                                HA(i + 1, 0)
                                HA(i + 1, 1)
                            HB(i)
                        rot["pool"] = [0, 1, 2, 3, 4, 5]
                        P.barrier()

                if dbg == "hgrn":
                    with ExitStack() as sd:
                        dtmp = T(sd, "dtmp", [128, 4, S], F32)
                        cp("dve", dtmp[:], oTb[:], w=["dtmp"])
                        dma("sp", "dbg", dbg_out.rearrange("(m p) t -> p m t", p=128), dtmp[:], r=["dtmp"])
                        P.barrier()
                    break

                mgT = T(sq, "mgT", [128, 8, S], BF16)
                wpq = T(sq, "wpq", [128, 8, 2048], BF16)
                wo = T(sq, "wo", [128, 8, D], BF16)
                pre = [("wo", wo, w_o, c) for c in range(8)] + [("wpq", wpq, w_pq, c) for c in range(16)]
                with ExitStack() as sc_:
                    wbr = [T(sc_, f"wbr{i}", [128, 4, 128], BF16) for i in range(2)]
                    sga = [T(sc_, f"sga{i}", [128, 512], F32) for i in range(2)]
                    t1 = [T(sc_, f"t1{i}", [128, 512], F32) for i in range(2)]
                    for m in range(8):
                        wa, wak = nwt()
                        wload(wa[:], w_in[:, GA0 + m * 128:GA0 + (m + 1) * 128], 8, 128, wak)
                        wg, wgk = nwt()
                        wload(wg[:], w_in[:, GG0 + m * 128:GG0 + (m + 1) * 128], 8, 128, wgk)
                        wload(wbr[0][:], w_br_a[:, m * 128:(m + 1) * 128], 4, 128, "wbr0")
                        wload(wbr[1][:], w_br_b[:, m * 128:(m + 1) * 128], 4, 128, "wbr1")
                        for tb in range(4):
                            cs = slice(tb * 512, (tb + 1) * 512)
                            for br in range(2):
                                wgt, wgtk = (wa, wak) if br == 0 else (wg, wgk)
                                src = oTa if br == 0 else oTb
                                srck = f"oTa{tb}" if br == 0 else f"oTb{tb}"
                                ps, psk = nb()
                                for kc in range(8):
                                    mm(ps[:, :], wgt[:, kc, :], xT[:, kc, cs], start=(kc == 0), stop=(kc == 7), r=[wgtk, f"xT{tb}"], w=[psk])
                                act(sga[br][:], ps[:, :], AF.Sigmoid, r=[psk], w=[f"sga{br}"])
                                ps2, ps2k = nb()
                                for kc in range(4):
                                    mm(ps2[:, :], wbr[br][:, kc, :], src[:, kc, cs], start=(kc == 0), stop=(kc == 3), r=[f"wbr{br}", srck], w=[ps2k])
                                tt("dve", t1[br][:], ps2[:, :], sga[br][:], ALU.mult, r=[ps2k, f"sga{br}"], w=[f"t1{br}"])
                            tt("dve", mgT[:, m, cs], t1[0][:], t1[1][:], ALU.add, r=["t10", "t11"], w=[f"mgT{tb}"])
                        for _ in range(3):
                            key_, dst_, src_, c_ = pre.pop(0)
                            wload(dst_[:, :, c_ * 128:(c_ + 1) * 128], src_[:, c_ * 128:(c_ + 1) * 128], 8, 128, key_, eng=("act", "dve")[len(pre) % 2])
                    P.barrier()

                with ExitStack() as sd:
                    lnp = oTa[:].bitcast(F32)
                    xr, xrk = wstg[0][:].rearrange("p a b -> p (a b)"), "wstg0"
                    ot, otk = wstg[1][:].rearrange("p a b -> p (a b)"), "wstg1"
                    rr2, rr2k = wstg[2][:].rearrange("p a b -> p (a b)"), "wstg2"
                    pjunk = wt[0][:].rearrange("p a b -> p (a b)")
                    h1T, h1Tk = wt[1], "wt1"
                    h1 = [xT[:, 6 + i, :].bitcast(F32) for i in range(2)]
                    h1b = [T(sd, "h1b0", [128, D], BF16)[:], wt[2][:].rearrange("p a b -> p (a b)")]
                    h1bk = ["h1b0", "wt2"]
                    ssb = T(sd, "ssb", [128, 16, 128], F32)
                    rr = ssb[:, 0:8, :].rearrange("p a b -> p (a b)")
                    RRK = ["ssb0", "ssb1"]
                    m8 = T(sd, "m8", [128, 16, 16], F32)
                    ix = T(sd, "ix", [128, 16, 16], U32)
                    ixf = T(sd, "ixf", [128, 16, 16], F32)
                    cand = T(sd, "cand", [128, 8, 256], F32)
                    qpT = cand[:].bitcast(BF16)[:, 0:4, :].rearrange("p a (c t) -> p (a c) t", t=128)
                    CANDK = [f"cand{h}" for h in range(8)]
                    big = ssb[:].rearrange("p (h a) b -> p h (a b)", h=8)
                    BIGK = ["ssb0", "ssb1", "ssb2", "ssb3"]
                    t8 = T(sd, "t8", [128, 8, 16], F32)
                    px = T(sd, "px", [128, 8, 16], U32)
                    pf = T(sd, "pf", [128, 8, 16], F32)
                    pa_ = T(sd, "pa", [128, 8, 16], F32)
                    pb_ = T(sd, "pb", [128, 8, 16], F32)
                    i1s = T(sd, "i1s", [128, 8, 16], F32)
                    i2s = T(sd, "i2s", [128, 8, 16], F32)
                    idsf = T(sd, "idsf", [128, 128], F32)
                    ids = [T(sd, f"ids{i}", [128, 128], I32) for i in range(2)]
                    gate = [T(sd, f"gate{i}", [128, 8, 16], F32) for i in range(2)]
                    zz = T(sd, "zz", [128, 8], F32)
                    hd = T(sd, "hd", [128, 128], F32)
                    gh = T(sd, "gh", [128, 128], F32)
                    st6 = T(sd, "st6", [128, 2, 12], F32)
                    mv = T(sd, "mv", [128, 2, 4], F32)
                    gg = T(sd, "gg", [128, 128], F32)
                    uvb = [oTb[:, s_, :] for s_ in range(4)] + [xT[:, s_, :] for s_ in range(6)]
                    NSL = len(uvb)
                    dk = [T(sd, f"dk{i}", [128, 128], BF16) for i in range(4)]
                    opk("dve", "memset", (), ["hdinit"], ap=hd[:], constant=0.0)
                    opk("dve", "memset", (), ["mv0", "mv20", "mv30", "mv1", "mv21", "mv31"], ap=mv[:], constant=0.0)
                    dma("sp", "lnp", lnp[:], c_ln, w=["lnp"])
                    P.barrier()
                    rot["pool"] = [0, 1, 2, 3]
                    pf0, pf1 = psb[4], psb[5]

                    def layer_norm(dst, src, srck, gi, dstk):
                        q_ = gi // 2
                        s6, m4 = st6[:, q_, :], mv[:, q_, :]
                        ka, kb_, km, km2, km3 = f"st6a{q_}", f"st6b{q_}", f"mv{q_}", f"mv2{q_}", f"mv3{q_}"
                        srcl = list(srck) if isinstance(srck, (list, tuple)) else [srck]
                        P.op("dve", lambda e: e.bn_stats(out=s6[:, 0:6], in_=src[:, 0:512]), srcl, [ka])
                        P.op("dve", lambda e: e.bn_stats(out=s6[:, 6:12], in_=src[:, 512:1024]), srcl, [kb_])
                        P.op("dve", lambda e: e.bn_aggr(out=m4[:, 0:2], in_=s6), [ka, kb_], [km])
                        act(m4[:, 2:3], m4[:, 1:2], AF.Sqrt, r=[km], w=[km2], bias=LN_EPS)
                        P.op("dve", lambda e: e.reciprocal(out=m4[:, 3:4], in_=m4[:, 2:3]), [km2], [km3])
                        ts("dve", dst, src, m4[:, 0:1], m4[:, 3:4], ALU.subtract, ALU.mult, r=srcl + [km, km3], w=[dstk])
                        tt("dve", dst, dst, lnp[:, gi, :], ALU.mult, r=[dstk, "lnp"], w=[dstk])
                        tt("dve", dst, dst, lnp[:, gi + 1, :], ALU.add, r=[dstk, "lnp"], w=[dstk])

                    def front(i):
                        p = i % 2
                        tsl = slice(128 * i, 128 * (i + 1))
                        hk, hbk, idk, gk = f"h1{p}", h1bk[p], f"ids{p}", f"gate{p}"
                        dma("sp", xrk, xr, x[t0 + 128 * i:t0 + 128 * (i + 1), :], w=[xrk])
                        for nbk in range(2):
                            ps, psk = nb()
                            for kc in range(8):
                                mm(ps[:, :], mgT[:, kc, tsl], wo[:, kc, nbk * 512:(nbk + 1) * 512], start=(kc == 0), stop=(kc == 7), r=[f"mgT{i // 4}", "wo"], w=[psk])
                            stt(rr[:, nbk * 512:(nbk + 1) * 512], xr[:, nbk * 512:(nbk + 1) * 512], ALPHA, ps[:, :], ALU.mult, ALU.add, r=[xrk, psk], w=RRK)
                        layer_norm(h1[p], rr, RRK, 0, hk)
                        if dbg == "h1":
                            dma("sp", "dbgh1", dbg_out[128 * i:128 * (i + 1), :], h1[p], r=[hk])
                        cp("act", h1b[p], h1[p], r=[hk], w=[hbk])
                        pt, ptk = nbt()
                        for c in range(8):
                            tr(pt[:, c * 128:(c + 1) * 128], h1b[p][:, c * 128:(c + 1) * 128], idb[:], r=[hbk, "idb"], w=[ptk])
                        cp("act", h1T[:], pt[:].rearrange("p (c t) -> p c t", c=8), r=[ptk], w=[h1Tk])
                        for g in range(4):
                            ps, psk = nb()
                            for q_ in range(4):
                                ct = 4 * g + q_
                                for kc in range(8):
                                    mm(ps[:, q_ * 128:(q_ + 1) * 128], wpq[:, kc, ct * 128:(ct + 1) * 128], h1T[:, kc, :], start=(kc == 0), stop=(kc == 7), r=["wpq", h1Tk], w=[psk])
                            cp("act", qpT[:, 4 * g:4 * g + 4, :], ps[:, :].rearrange("p (c t) -> p c t", c=4), r=[psk], w=CANDK)
                        for g in range(4):
                            ps, psk = nb()
                            for q_ in range(4):
                                ct = 4 * g + q_
                                mm(ps[:, q_ * 128:(q_ + 1) * 128], qpT[:, ct, :], skT[:, ct % 2, :], r=CANDK + ["skT"], w=[psk])
                            cp("act", ssb[:, 4 * g:4 * g + 4, :], ps[:, :].rearrange("p (c t) -> p c t", c=4), r=[psk], w=[f"ssb{g}"])
                        for ph in range(5):
                            for ct in range(16):
                                g = ct // 4
                                if ph == 0:
                                    opk("dve", "max", [f"ssb{g}"], [f"m8a{ct}"], out=m8[:, ct, 0:8], in_=ssb[:, ct, :])
                                elif ph == 1:
                                    opk("dve", "max_index", [f"ssb{g}", f"m8a{ct}"], [f"ixa{ct}"], out=ix[:, ct, 0:8], in_max=m8[:, ct, 0:8], in_values=ssb[:, ct, :])
                                elif ph == 2:
                                    opk("dve", "match_replace", [f"ssb{g}", f"m8a{ct}", f"ixa{ct}"], [f"ssb{g}"], out=ssb[:, ct, :], in_to_replace=m8[:, ct, 0:8], in_values=ssb[:, ct, :], imm_value=NEG)
                                elif ph == 3:
                                    opk("dve", "max", [f"ssb{g}"], [f"m8b{ct}"], out=m8[:, ct, 8:16], in_=ssb[:, ct, :])
                                else:
                                    opk("dve", "max_index", [f"ssb{g}", f"m8b{ct}"], [f"ixb{ct}"], out=ix[:, ct, 8:16], in_max=m8[:, ct, 8:16], in_values=ssb[:, ct, :])
                        m8k = [f"m8a{ct}" for ct in range(16)] + [f"m8b{ct}" for ct in range(16)]
                        ixk = [f"ixa{ct}" for ct in range(16)] + [f"ixb{ct}" for ct in range(16)]
                        cp("dve", ixf[:], ix[:], r=ixk, w=["ixf"])
                        m8v = m8[:].rearrange("p (h two) a -> p h two a", two=2)
                        ixv = ixf[:].rearrange("p (h two) a -> p h two a", two=2)
                        candv = cand[:].rearrange("p h (a b) -> p h a b", a=16)
                        bigv = big[:].rearrange("p h (a b) -> p h a b", a=16)
                        tt("dve", candv, m8v[:, :, 0, :].unsqueeze(3).to_broadcast([128, 8, 16, 16]), m8v[:, :, 1, :].unsqueeze(2).to_broadcast([128, 8, 16, 16]), ALU.add, r=m8k, w=[f"cand{h}" for h in range(8)])
                        for ph in range(5):
                            for h in range(8):
                                if ph == 0:
                                    opk("dve", "max", [f"cand{h}"], [f"t8a{h}"], out=t8[:, h, 0:8], in_=cand[:, h, :])
                                elif ph == 1:
                                    opk("dve", "max_index", [f"cand{h}", f"t8a{h}"], [f"pxa{h}"], out=px[:, h, 0:8], in_max=t8[:, h, 0:8], in_values=cand[:, h, :])
                                elif ph == 2:
                                    opk("dve", "match_replace", [f"cand{h}", f"t8a{h}", f"pxa{h}"], [f"cand{h}"], out=cand[:, h, :], in_to_replace=t8[:, h, 0:8], in_values=cand[:, h, :], imm_value=NEG)
                                elif ph == 3:
                                    opk("dve", "max", [f"cand{h}"], [f"t8b{h}"], out=t8[:, h, 8:16], in_=cand[:, h, :])
                                else:
                                    opk("dve", "max_index", [f"cand{h}", f"t8b{h}"], [f"pxb{h}"], out=px[:, h, 8:16], in_max=t8[:, h, 8:16], in_values=cand[:, h, :])
                        t8k = [f"t8a{h}" for h in range(8)] + [f"t8b{h}" for h in range(8)]
                        pxk = [f"pxa{h}" for h in range(8)] + [f"pxb{h}" for h in range(8)]
                        cp("dve", pf[:], px[:], r=pxk, w=["pf"])
                        tt("dve", bigv, pf[:].unsqueeze(3).to_broadcast([128, 8, 16, 16]), iot[:, 16:32].unsqueeze(1).unsqueeze(1).to_broadcast([128, 8, 16, 16]), ALU.is_ge, r=["pf", "iot"], w=BIGK)
                        red(pa_[:], bigv, ALU.add, r=BIGK, w=["pa"])
                        stt(pb_[:], pa_[:], -16.0, pf[:], ALU.mult, ALU.add, r=["pa", "pf"], w=["pb"])
                        io16 = iot[:, 0:16].unsqueeze(1).unsqueeze(1).to_broadcast([128, 8, 16, 16])
                        for (src_, sel, selk, part) in ((pa_, i1s, "i1s", 0), (pb_, i2s, "i2s", 1)):
                            tt("dve", bigv, src_[:].unsqueeze(3).to_broadcast([128, 8, 16, 16]), io16, ALU.is_equal, r=["pa", "pb", "iot"], w=BIGK)
                            tt("dve", bigv, bigv, ixv[:, :, part, :].unsqueeze(2).to_broadcast([128, 8, 16, 16]), ALU.mult, r=BIGK + ["ixf"], w=BIGK)
                            red(sel[:], bigv, ALU.add, r=BIGK, w=[selk])
                        stt(idsf[:].rearrange("p (h j) -> p h j", h=8), i1s[:], 128.0, i2s[:], ALU.mult, ALU.add, r=["i1s", "i2s"], w=["idsf"])
                        ts("dve", idsf[:], idsf[:], 0.0, 16383.0, ALU.max, ALU.min, r=["idsf"], w=["idsf"])
                        cp("dve", ids[p][:], idsf[:], r=["idsf"], w=[idk])
                        tt("dve", gate[p][:], t8[:], t8[:, :, 0:1].to_broadcast([128, 8, 16]), ALU.subtract, r=t8k, w=[gk])
                        act(gate[p][:], gate[p][:], AF.Exp, r=[gk], w=[gk])
                        red(zz[:], gate[p][:], ALU.add, r=[gk], w=["zz"])
                        opk("dve", "reciprocal", ["zz"], ["zz"], out=zz[:], in_=zz[:])
                        tt("dve", gate[p][:], gate[p][:], zz[:].unsqueeze(2).to_broadcast([128, 8, 16]), ALU.mult, r=[gk, "zz"], w=[gk])

                    def back(i):
                        p = i % 2
                        hk, hbk, idk, gk = f"h1{p}", h1bk[p], f"ids{p}", f"gate{p}"
                        gatef = gate[p][:].rearrange("p h j -> p (h j)")
                        for k in range(129):
                            if k < 128:
                                s_ = k % NSL
                                gather(f"uv{s_}", uvb[s_], uv[:, :], ids[p][:, k:k + 1], r=[idk], w=[f"uv{s_}"])
                                stt(pjunk, uvb[s_][:, 0:1024], 1.0, h1b[p], ALU.mult, ALU.mult, r=[f"uv{s_}", hbk, "hdinit"], w=[f"hd{k % 8}"], accum=hd[:, k:k + 1])
                                act(gh[:, k:k + 1], hd[:, k:k + 1], AF.Gelu, r=[f"hd{k % 8}"], w=[f"gh{k % 8}"])
                            if k >= 1:
                                k1 = k - 1
                                s_ = k1 % NSL
                                di = k1 % 4
                                act(gg[:, k1:k1 + 1], gh[:, k1:k1 + 1], AF.Copy, r=[f"gh{k1 % 8}", gk], w=[f"gg{k1 % 8}"], scale=gatef[:, k1:k1 + 1])
                                act(dk[di][:], idb[:], AF.Copy, r=["idb", f"gg{k1 % 8}"], w=[f"dk{di}"], scale=gg[:, k1:k1 + 1])
                                mm(pf0[:, :], dk[di][:], uvb[s_][:, 1024:1536], start=(k1 == 0), stop=(k1 == 127), r=[f"dk{di}", f"uv{s_}"], w=["pf0"])
                                mm(pf1[:, :], dk[di][:], uvb[s_][:, 1536:2048], start=(k1 == 0), stop=(k1 == 127), r=[f"dk{di}", f"uv{s_}"], w=["pf1"])
                        stt(rr2[:, 0:512], h1[p][:, 0:512], ALPHA, pf0[:, :], ALU.mult, ALU.add, r=[hk, "pf0"], w=[rr2k])
                        stt(rr2[:, 512:1024], h1[p][:, 512:1024], ALPHA, pf1[:, :], ALU.mult, ALU.add, r=[hk, "pf1"], w=[rr2k])
                        layer_norm(ot, rr2, rr2k, 2, otk)
                        dma("sp", otk, y[t0 + 128 * i:t0 + 128 * (i + 1), :], ot, r=[otk])

                    def capture(fn_, *a):
                        P.cap = []
                        fn_(*a)
                        lst = P.cap
                        P.cap = None
                        return lst

                    P.replay(capture(front, 0))
                    for i in range(NT):
                        bl = capture(back, i)
                        fl = capture(front, i + 1) if i + 1 < NT else []
                        merged = []
                        fi = 0
                        nb_ = max(1, len(bl) - 60)
                        for bi, it in enumerate(bl):
                            merged.append(it)
                            want = min(len(fl), (len(fl) * (bi + 1)) // nb_)
                            while fi < want:
                                merged.append(fl[fi])
                                fi += 1
                        merged.extend(fl[fi:])
                        P.replay(merged)
                    rot["pool"] = [0, 1, 2, 3, 4, 5]
                    P.barrier()
        P.barrier()
        P.emit()
    return nc


def _t5_bucket_np(rel):
    half = 16
    max_exact = 8
    rel = jnp.asarray(rel, jnp.int32)
    base = jnp.where(rel > 0, half, 0)
    n = jnp.abs(rel)
    nf = jnp.maximum(n, 1).astype(jnp.float32)
    large = max_exact + (jnp.log(nf / max_exact) / math.log(128 / max_exact) * (half - max_exact)).astype(jnp.int32)
    large = jnp.minimum(large, half - 1)
    return np.asarray(base + jnp.where(n < max_exact, n, large))


def host_consts(inp):
    f32 = np.float32
    c = {}
    c["c_ident"] = np.eye(128, dtype=f32)
    s = np.arange(128)[:, None]
    t = np.arange(128)[None, :]
    same = (s // 64) == (t // 64)
    a3 = (same & (s <= t)).astype(f32)
    ref = (same & ((s % 64) <= 31)).astype(f32)
    a1 = a3 - ref
    a2 = same.astype(f32) - a3
    ind = np.stack([(np.arange(128) // 64 == 0), (np.arange(128) // 64 == 1)], axis=1).astype(f32)
    c["c_cum"] = np.concatenate([a1, a3, ind, a2], axis=1).astype(f32)
    tq = np.arange(128)[:, None]
    sk = np.arange(128)[None, :]
    c["c_negvis"] = np.where((tq < 64) & (sk >= 64), NEG, 0.0).astype(f32)
    with jax.default_device(jax.devices("cpu")[0]):
        kk_ = np.arange(128)[:, None]
        tt_ = np.arange(128)[None, :]
        bk = [_t5_bucket_np(kk_ - tt_ - 128 * d) for d in range(2)]
    rb = np.asarray(inp["rel_bias"], f32)
    bn = np.zeros((128, 16, 128), f32)
    for d in range(2):
        g = rb[bk[d]]
        bn[:, d * 8:(d + 1) * 8, :] = np.transpose(g, (0, 2, 1))
    c["c_bnear"] = bn
    c["c_cb"] = np.broadcast_to(rb[15][None, :], (128, 8)).astype(f32).copy()
    wuk = np.asarray(inp["w_uk"][0], f32)
    c["c_wukT"] = np.ascontiguousarray(np.transpose(wuk, (1, 2, 0)).reshape(4, 128, 128).transpose(1, 0, 2))
    wuv = np.asarray(inp["w_uv"][0], f32)
    wp = np.zeros((128, 8, 128), f32)
    for h in range(8):
        wp[:, h, (h % 2) * 64:(h % 2) * 64 + 64] = wuv[:, h, :]
    c["c_wuvp"] = wp
    lbp = np.asarray(inp["lb_params"], f32)
    c["c_lbbc"] = np.broadcast_to(lbp[None], (128, 2, 512)).astype(f32).copy()
    c["c_lbfm"] = np.ascontiguousarray(lbp.reshape(2, 4, 128).transpose(2, 0, 1))
    c["c_kvg"] = np.broadcast_to(np.asarray(inp["kv_norm_g"][0], f32)[None, :], (128, 128)).copy()
    c["c_bng"] = np.asarray(inp["b_norm_g"][0], f32).reshape(128, 1).copy()
    ln = np.stack([inp["ln1_g"][0], inp["ln1_b"][0], inp["ln2_g"][0], inp["ln2_b"][0]], axis=0).astype(f32)
    c["c_ln"] = np.broadcast_to(ln[None], (128, 4, D)).copy()
    c["c_skT"] = np.ascontiguousarray(np.stack([np.asarray(inp["sub_keys1"][0], f32).T, np.asarray(inp["sub_keys2"][0], f32).T], axis=1))
    io = np.concatenate([np.arange(16), 16 * (np.arange(16) + 1)]).astype(f32)
    c["c_iota"] = np.broadcast_to(io[None, :], (128, 32)).copy()
    return c


def make_in_maps(inp, ncores, nseq):
    c = host_consts(inp)
    shared = dict(c)
    shared["w_in"] = np.ascontiguousarray(inp["w_in"][0], dtype=np.float32)
    shared["w_br_a"] = np.ascontiguousarray(inp["w_br_a"][0], dtype=np.float32)
    shared["w_br_b"] = np.ascontiguousarray(inp["w_br_b"][0], dtype=np.float32)
    shared["w_o"] = np.ascontiguousarray(inp["w_o"][0], dtype=np.float32)
    shared["w_pq"] = np.ascontiguousarray(inp["w_pq"][0], dtype=np.float32)
    shared["u_table"] = np.ascontiguousarray(inp["u_table"][0], dtype=np.float32)
    shared["v_table"] = np.ascontiguousarray(inp["v_table"][0], dtype=np.float32)
    maps = []
    xx = np.asarray(inp["x"], dtype=np.float32)
    for ci in range(ncores):
        m = dict(shared)
        m["x"] = np.ascontiguousarray(xx[ci * nseq:(ci + 1) * nseq].reshape(nseq * S, D))
        maps.append(m)
    return maps


def kernel(**inputs):
    nseq = 32 // NCORES
    nc = build(nseq)
    maps = make_in_maps(inputs, NCORES, nseq)
    res = run_bass_kernel_spmd(nc, maps, core_ids=list(range(NCORES)))
    out = np.concatenate([r["y"].reshape(nseq, S, D) for r in res.results], axis=0)
    return out.astype(np.float32)
```

```python
import math
from contextlib import ExitStack

import numpy as np
import jax
import jax.numpy as jnp

import concourse.bass as bass
import concourse.mybir as mybir
from concourse.bass_utils import run_bass_kernel_spmd

F32 = mybir.dt.float32
BF16 = mybir.dt.bfloat16
I32 = mybir.dt.int32
U32 = mybir.dt.uint32
AF = mybir.ActivationFunctionType
ALU = mybir.AluOpType
AX = mybir.AxisListType

NCORES = 8
D = 1024
S = 2048
NT = S // 128
QA0, CKV0, QI0, KI0, WI0, QB0, FB0, IB0, GB0, GA0, GG0 = 0, 512, 640, 1152, 1216, 1224, 1736, 2248, 2760, 3272, 4296
INCOLS = 5320
ALPHA = 2.0 ** 0.25
LN_EPS = 1e-5
RMS_EPS = 1e-6
NEG = -1.0e30
NBIS = 16
KC = 2


class Prog:
    ENG = ("pe", "act", "dve", "pool", "sp")
    CAP = 30000

    def __init__(self, nc, stack):
        self.nc = nc
        self.stack = stack
        self.q = {e: [] for e in self.ENG}
        self.cur = {e: None for e in self.ENG}
        self.cnt = {e: 0 for e in self.ENG}
        self.seen = {e: {} for e in self.ENG}
        self.lastw = {}
        self.readers = {}
        self.dsem = {}
        self.nsem = 0
        self.sems = {}
        self.cap = None

    def _newsem(self, name):
        s = self.stack.enter_context(self.nc.semaphore(f"s{self.nsem}_{name}"))
        self.nsem += 1
        self.sems[id(s)] = s
        return s

    def _deps(self, eng, reads, writes):
        toks = []
        for r in reads:
            t = self.lastw.get(r)
            if t is not None:
                toks.append(t)
        for w in writes:
            t = self.lastw.get(w)
            if t is not None:
                toks.append(t)
            toks.extend(self.readers.get(w, ()))
        waits = {}
        seen = self.seen[eng]
        for (sem, val, src) in toks:
            if src == "pe" and eng == "pe":
                continue
            k = id(sem)
            if seen.get(k, 0) >= val:
                continue
            if waits.get(k, 0) < val:
                waits[k] = val
        for k, v in waits.items():
            seen[k] = v
        return [(self.sems[k], v) for k, v in waits.items()]

    def _commit(self, tok, reads, writes):
        for w in writes:
            self.lastw[w] = tok
            self.readers[w] = []
        for r in reads:
            self.readers.setdefault(r, []).append(tok)

    def replay(self, items):
        for it in items:
            if it[0] == "op":
                self.op(*it[1:])
            else:
                self.dma(*it[1:])

    def op(self, eng, fn, reads=(), writes=()):
        if self.cap is not None:
            self.cap.append(("op", eng, fn, tuple(reads), tuple(writes)))
            return None
        waits = self._deps(eng, reads, writes)
        if self.cur[eng] is None or self.cnt[eng] >= self.CAP:
            self.cur[eng] = self._newsem(eng)
            self.cnt[eng] = 0
        self.cnt[eng] += 1
        tok = (self.cur[eng], self.cnt[eng], eng)
        self.q[eng].append((fn, waits, tok[0], 1))
        self._commit(tok, reads, writes)
        return tok

    def dma(self, queue, slot, fn, reads=(), writes=()):
        if self.cap is not None:
            self.cap.append(("dma", queue, slot, fn, tuple(reads), tuple(writes)))
            return None
        waits = self._deps(queue, reads, writes)
        if slot not in self.dsem:
            self.dsem[slot] = [self._newsem("d" + str(slot)), 0]
        ent = self.dsem[slot]
        ent[1] += 16
        tok = (ent[0], ent[1], "dma")
        self.q[queue].append((fn, waits, tok[0], 16))
        self._commit(tok, reads, writes)
        return tok

    def barrier(self):
        toks = []
        for e in self.ENG:
            if self.cur[e] is not None:
                toks.append((self.cur[e], self.cnt[e], e))
        for slot, ent in self.dsem.items():
            toks.append((ent[0], ent[1], "dma"))
        for e in self.ENG:
            waits = {}
            seen = self.seen[e]
            for (sem, val, src) in toks:
                k = id(sem)
                if seen.get(k, 0) >= val:
                    continue
                waits[k] = val
                seen[k] = val
            self.q[e].append((None, [(self.sems[k], v) for k, v in waits.items()], None, 0))
        self.lastw.clear()
        self.readers.clear()

    def emit(self):
        nc = self.nc
        with nc.Block() as block:
            def run(engname):
                def f(engine):
                    for (fn, waits, sem, inc) in self.q[engname]:
                        for (s, v) in waits:
                            engine.wait_ge(s, v)
                        if fn is not None:
                            fn(engine).then_inc(sem, inc)
                return f
            block.tensor(run("pe"))
            block.scalar(run("act"))
            block.vector(run("dve"))
            block.gpsimd(run("pool"))
            block.sync(run("sp"))


def build(nseq, dbg=None):
    nc = bass.Bass("TRN2", target_bir_lowering=False)
    ntok = nseq * S

    def din(name, shape, dt=F32):
        return nc.dram_tensor(name, list(shape), dt, kind="ExternalInput").ap()

    x = din("x", [ntok, D])
    w_in = din("w_in", [D, INCOLS])
    w_br_a = din("w_br_a", [512, D])
    w_br_b = din("w_br_b", [512, D])
    w_o = din("w_o", [D, D])
    w_pq = din("w_pq", [D, 2048])
    u_table = din("u_table", [16384, D])
    v_table = din("v_table", [16384, D])
    c_ident = din("c_ident", [128, 128])
    c_cum = din("c_cum", [128, 386])
    c_negvis = din("c_negvis", [128, 128])
    c_bnear = din("c_bnear", [128, 16, 128])
    c_cb = din("c_cb", [128, 8])
    c_wukT = din("c_wukT", [128, 4, 128])
    c_wuvp = din("c_wuvp", [128, 8, 128])
    c_lbbc = din("c_lbbc", [128, 2, 512])
    c_lbfm = din("c_lbfm", [128, 2, 4])
    c_kvg = din("c_kvg", [128, 128])
    c_bng = din("c_bng", [128, 1])
    c_ln = din("c_ln", [128, 4, D])
    c_skT = din("c_skT", [128, 2, 128])
    c_iota = din("c_iota", [128, 32])
    y = nc.dram_tensor("y", [ntok, D], F32, kind="ExternalOutput").ap()
    dbg_out = None
    if dbg == "dsa" or dbg == "hgrn":
        dbg_out = nc.dram_tensor("dbg", [512, S], F32, kind="ExternalOutput").ap()
    if dbg == "h1":
        dbg_out = nc.dram_tensor("dbg", [S, D], F32, kind="ExternalOutput").ap()

    with ExitStack() as top:
        P = Prog(nc, top)

        tcount = [0]

        def T(st, name, shape, dt):
            tcount[0] += 1
            return st.enter_context(nc.sbuf_tensor(f"{name}_{tcount[0]}", list(shape), dt))

        def mm(out, lhsT, rhs, start=True, stop=True, r=(), w=()):
            P.op("pe", lambda e: e.matmul(out, lhsT=lhsT, rhs=rhs, start=start, stop=stop), r, w)

        def tr(out, in_, ident, r=(), w=()):
            P.op("pe", lambda e: e.transpose(out=out, in_=in_, identity=ident), r, w)

        def act(out, in_, func, r=(), w=(), bias=None, scale=None, accum=None):
            kw = {}
            if bias is not None:
                kw["bias"] = bias
            if scale is not None:
                kw["scale"] = scale
            if accum is not None:
                kw["accum_out"] = accum
            P.op("act", lambda e: e.activation(out=out, in_=in_, func=func, **kw), r, w)

        def ts(eng, out, in0, s1, s2, op0, op1=None, r=(), w=(), accum=None):
            kw = {}
            if op1 is not None:
                kw["op1"] = op1
            if accum is not None:
                kw["accum_out"] = accum
            P.op(eng, lambda e: e.tensor_scalar(out, in0, s1, s2, op0, **kw), r, w)

        def tt(eng, out, in0, in1, op, r=(), w=()):
            P.op(eng, lambda e: e.tensor_tensor(out, in0, in1, op), r, w)

        def stt(out, in0, scalar, in1, op0, op1, r=(), w=(), accum=None):
            kw = {}
            if accum is not None:
                kw["accum_out"] = accum
            P.op("dve", lambda e: e.scalar_tensor_tensor(out=out, in0=in0, scalar=scalar, in1=in1, op0=op0, op1=op1, **kw), r, w)

        def cp(eng, out, in_, r=(), w=()):
            if eng == "act":
                P.op("act", lambda e: e.activation(out=out, in_=in_, func=AF.Copy), r, w)
            else:
                P.op(eng, lambda e: e.tensor_copy(out=out, in_=in_), r, w)

        def opk(eng, meth, r, w, **kw):
            P.op(eng, lambda e: getattr(e, meth)(**kw), r, w)

        def red(out, in_, op, r=(), w=()):
            P.op("dve", lambda e: e.tensor_reduce(out=out, in_=in_, axis=AX.X, op=op), r, w)

        def dma(queue, slot, out, in_, r=(), w=()):
            P.dma(queue, slot, lambda e: e.dma_start(out=out, in_=in_), r, w)

        def gather(slot, out, table, idx_ap, r=(), w=()):
            P.dma("pool", slot, lambda e: e.indirect_dma_start(
                out=out, out_offset=None, in_=table,
                in_offset=bass.IndirectOffsetOnAxis(ap=idx_ap, axis=0)), r, w)

        psb = [top.enter_context(nc.psum_tensor(f"psb{i}", [128, 512], F32)) for i in range(6)]
        pst = [top.enter_context(nc.psum_tensor(f"pst{i}", [128, 1024], BF16)) for i in range(2)]
        rot = {"pool": [0, 1, 2, 3, 4, 5], "i": 0, "t": 0}

        def nb():
            b = rot["pool"][rot["i"] % len(rot["pool"])]
            rot["i"] += 1
            return psb[b], f"ps{b}"

        def nbt():
            b = rot["t"] % 2
            rot["t"] += 1
            return pst[b], f"pst{b}"

        idf = T(top, "idf", [128, 128], F32)
        idb = T(top, "idb", [128, 128], BF16)
        onesb = T(top, "onesb", [128, 128], BF16)
        cum = T(top, "cum", [128, 386], F32)
        negvis = T(top, "negvis", [128, 128], F32)
        cb = T(top, "cb", [128, 8], F32)
        wukT = T(top, "wukT", [128, 4, 128], BF16)
        wuvp = T(top, "wuvp", [128, 8, 128], BF16)
        lbt = T(top, "lbt", [128, 512], F32)
        omlt = T(top, "omlt", [128, 512], F32)
        lbf = T(top, "lbf", [128, 4], F32)
        omlf = T(top, "omlf", [128, 4], F32)
        nomlf = T(top, "nomlf", [128, 4], F32)
        kvg = T(top, "kvg", [128, 128], F32)
        bng = T(top, "bng", [128, 1], F32)
        skT = T(top, "skT", [128, 2, 128], BF16)
        iot = T(top, "iot", [128, 32], F32)
        thrc = T(top, "thrc", [128, 1], F32)
        wstg = [T(top, f"wstg{i}", [128, 8, 128], F32) for i in range(3)]
        wst = {"i": 0}

        with ExitStack() as st0:
            tmp1 = T(st0, "tmp1", [128, 16, 128], F32)
            tmp2 = T(st0, "tmp2", [128, 2, 512], F32)
            tmp3 = T(st0, "tmp3", [128, 2, 4], F32)
            dma("sp", "c0", idf[:], c_ident, w=["idf"])
            cp("dve", idb[:], idf[:], r=["idf"], w=["idb"])
            P.op("dve", lambda e: e.memset(onesb[:], 1.0), (), ["onesb"])
            P.op("dve", lambda e: e.memset(thrc[:], -1.0e29), (), ["thrc"])
            dma("sp", "c1", cum[:], c_cum, w=["cum"])
            dma("sp", "c2", negvis[:], c_negvis, w=["negvis"])
            dma("sp", "c4", cb[:], c_cb, w=["cb"])
            dma("sp", "c5", tmp1[:, 0:4, :], c_wukT, w=["tmp1"])
            cp("dve", wukT[:], tmp1[:, 0:4, :], r=["tmp1"], w=["wukT"])
            dma("sp", "c6", tmp1[:, 0:8, :], c_wuvp, w=["tmp1"])
            cp("dve", wuvp[:], tmp1[:, 0:8, :], r=["tmp1"], w=["wuvp"])
            dma("sp", "c7", tmp2[:], c_lbbc, w=["tmp2"])
            tt("dve", lbt[:], tmp2[:, 0, :], tmp2[:, 1, :], ALU.subtract, r=["tmp2"], w=["lbt"])
            act(lbt[:], lbt[:], AF.Sigmoid, r=["lbt"], w=["lbt"])
            ts("dve", omlt[:], lbt[:], -1.0, 1.0, ALU.mult, ALU.add, r=["lbt"], w=["omlt"])
            dma("sp", "c8", tmp3[:], c_lbfm, w=["tmp3"])
            tt("dve", lbf[:], tmp3[:, 0, :], tmp3[:, 1, :], ALU.subtract, r=["tmp3"], w=["lbf"])
            act(lbf[:], lbf[:], AF.Sigmoid, r=["lbf"], w=["lbf"])
            ts("dve", omlf[:], lbf[:], -1.0, 1.0, ALU.mult, ALU.add, r=["lbf"], w=["omlf"])
            ts("dve", nomlf[:], omlf[:], -1.0, None, ALU.mult, r=["omlf"], w=["nomlf"])
            dma("sp", "c9", kvg[:], c_kvg, w=["kvg"])
            dma("sp", "c10", bng[:], c_bng, w=["bng"])
            dma("sp", "c11", tmp1[:, 0:2, :], c_skT, w=["tmp1"])
            cp("dve", skT[:], tmp1[:, 0:2, :], r=["tmp1"], w=["skT"])
            dma("sp", "c12", iot[:], c_iota, w=["iot"])
            P.barrier()

        uv = nc.dram_tensor("uv_scratch", [16384, 2048], BF16).ap()

        def wload(dst, src2d, nkc, wd, key, eng="pool"):
            i = wst["i"] % 3
            wst["i"] += 1
            stg = wstg[i]
            dma("sp", f"wstg{i}", stg[:, 0:nkc, 0:wd], src2d.rearrange("(kc p) c -> p kc c", p=128), w=[f"wstg{i}"])
            cp(eng, dst, stg[:, 0:nkc, 0:wd], r=[f"wstg{i}"], w=[key])

        for b in range(nseq):
            t0 = b * S
            with ExitStack() as sq:
                xT = T(sq, "xT", [128, 8, S], BF16)
                oTa = T(sq, "oTa", [128, 4, S], BF16)
                oTb = T(sq, "oTb", [128, 4, S], BF16)
                wt = [T(sq, f"wt{i}", [128, 8, 128], BF16) for i in range(3)]
                wti = {"i": 0}

                def nwt():
                    i = wti["i"] % 3
                    wti["i"] += 1
                    return wt[i], f"wt{i}"

                with ExitStack() as sx:
                    xs = [T(sx, f"xs{i}", [128, D], F32) for i in range(2)]
                    xb = [T(sx, f"xb{i}", [128, D], BF16) for i in range(2)]
                    for i in range(NT):
                        p = i % 2
                        dma("sp", f"xs{p}", xs[p][:], x[t0 + 128 * i:t0 + 128 * (i + 1), :], w=[f"xs{p}"])
                        cp("dve" if i % 2 == 0 else "act", xb[p][:], xs[p][:], r=[f"xs{p}"], w=[f"xb{p}"])
                        pt, ptk = nbt()
                        for c in range(8):
                            tr(pt[:, c * 128:(c + 1) * 128], xb[p][:, c * 128:(c + 1) * 128], idb[:], r=[f"xb{p}", "idb"], w=[ptk])
                        cp("act" if i % 2 == 0 else "dve", xT[:, :, 128 * i:128 * (i + 1)], pt[:].rearrange("p (c t) -> p c t", c=8), r=[ptk], w=[f"xT{i // 4}"])
                    P.barrier()

                def proj_fm(col0, evac, wtile=None, wkey=None):
                    if wtile is None:
                        wtile, wkey = nwt()
                        wload(wtile[:], w_in[:, col0:col0 + 128], 8, 128, wkey)
                    for tb in range(4):
                        ps, psk = nb()
                        for kc in range(8):
                            mm(ps[:, :], wtile[:, kc, :], xT[:, kc, tb * 512:(tb + 1) * 512], start=(kc == 0), stop=(kc == 7), r=[wkey, f"xT{tb}"], w=[psk])
                        evac(tb, ps, psk)

                with ExitStack() as sa:
                    qlat = T(sa, "qlat", [128, 8, S], BF16)
                    qiT = T(sa, "qiT", [128, 4, S], BF16)
                    qaT = [T(sa, f"qaT{i}", [128, S], BF16) for i in range(2)]
                    kiT = T(sa, "kiT", [128, S], BF16)
                    ckv = T(sa, "ckv", [128, NT, 128], BF16)
                    ckvT = T(sa, "ckvT", [128, S], BF16)
                    wia = T(sa, "wia", [128, NT, 8], F32)
                    wck = T(sa, "wck", [128, 8, 136], BF16)
                    sc = T(sa, "sc", [128, S], F32)
                    rl = [T(sa, f"rl{i}", [128, 512], F32) for i in range(2)]
                    nm = [T(sa, f"nm{i}", [128, S], BF16) for i in range(2)]
                    pT = [T(sa, f"pT{i}", [128, S], BF16) for i in range(2)]
                    olat = [T(sa, f"olat{i}", [128, 8, 128], BF16) for i in range(2)]
                    sm = T(sa, "sm", [128, 16], F32)
                    rzs = [T(sa, f"rzs{i}", [128, 128], F32) for i in range(2)]
                    ssq = T(sa, "ssq", [128, 2], F32)
                    cjunk = T(sa, "cjunk", [128, 128], F32)
                    opk("dve", "memset", (), ["ssq0", "ssq1"], ap=ssq[:], constant=0.0)
                    opk("dve", "memset", (), ["sm0", "sm1", "sm2", "sm3", "sm4", "sm5"], ap=sm[:], constant=0.0)
                    bpp = T(sa, "bpp", [128, 16, 128], F32)
                    dma("sp", "c3", bpp[:], c_bnear, w=["bpp"])
                    for dh in range(16):
                        ts("dve", bpp[:, dh, :], bpp[:, dh, :], cb[:, dh % 8:dh % 8 + 1], 8.0, ALU.subtract, ALU.mult, r=["bpp", "cb"], w=["bpp"])

                    for m in range(4):
                        qa_t = qaT[m % 2]
                        qk = f"qaT{m % 2}"

                        def ev_qa(tb, ps, psk, qa_t=qa_t, qk=qk):
                            cp("act" if tb % 2 == 0 else "dve", qa_t[:, tb * 512:(tb + 1) * 512], ps[:, :], r=[psk], w=[qk])
                        proj_fm(QA0 + m * 128, ev_qa)
                        for hl in range(2):
                            h = 2 * m + hl
                            for tb in range(4):
                                ps, psk = nb()
                                mm(ps[:, :], wukT[64 * hl:64 * hl + 64, m, :], qa_t[64 * hl:64 * hl + 64, tb * 512:(tb + 1) * 512], r=["wukT", qk], w=[psk])
                                cp("act" if tb % 2 == 1 else "dve", qlat[:, h, tb * 512:(tb + 1) * 512], ps[:, :], r=[psk], w=[f"qlat{h}"])
                    for m in range(4):
                        def ev_qi(tb, ps, psk, m=m):
                            cp("act" if tb % 2 == 0 else "dve", qiT[:, m, tb * 512:(tb + 1) * 512], ps[:, :], r=[psk], w=[f"qiT{m}"])
                        proj_fm(QI0 + m * 128, ev_qi)
                    wk_t, wk_k = nwt()
                    wload(wk_t[:, :, 0:64], w_in[:, KI0:KI0 + 64], 8, 64, wk_k)
                    wload(wk_t[:, :, 64:128], w_in[:, KI0:KI0 + 64], 8, 64, wk_k)

                    def ev_ki(tb, ps, psk):
                        cp("act" if tb % 2 == 0 else "dve", kiT[:, tb * 512:(tb + 1) * 512], ps[:, :], r=[psk], w=["kiT"])
                    proj_fm(None, ev_ki, wtile=wk_t, wkey=wk_k)
                    wload(wck[:, :, 0:128], w_in[:, CKV0:CKV0 + 128], 8, 128, "wck")
                    wload(wck[:, :, 128:136], w_in[:, WI0:WI0 + 8], 8, 8, "wck")
                    for i in range(NT):
                        ps, psk = nb()
                        for kc in range(8):
                            mm(ps[:, 0:136], xT[:, kc, 128 * i:128 * (i + 1)], wck[:, kc, :], start=(kc == 0), stop=(kc == 7), r=["wck", f"xT{i // 4}"], w=[psk])
                        act(cjunk[:], ps[:, 0:128], AF.Square, r=[psk], w=["ssq0"], accum=ssq[:, 0:1])
                        act(ssq[:, 1:2], ssq[:, 0:1], AF.Ln, r=["ssq0"], w=["ssq1"], bias=RMS_EPS, scale=1.0 / 128.0)
                        act(ssq[:, 1:2], ssq[:, 1:2], AF.Exp, r=["ssq1"], w=["ssq1"], scale=-0.5)
                        stt(ckv[:, i, :], ps[:, 0:128], ssq[:, 1:2], kvg[:], ALU.mult, ALU.mult, r=[psk, "ssq1", "kvg"], w=[f"ckv{i}"])
                        cp("act", wia[:, i, :], ps[:, 128:136], r=[psk], w=[f"wia{i}"])
                        pt, ptk = nbt()
                        tr(pt[:, 0:128], ckv[:, i, :], idb[:], r=[f"ckv{i}", "idb"], w=[ptk])
                        cp("dve", ckvT[:, 128 * i:128 * (i + 1)], pt[:, 0:128], r=[ptk], w=[f"ckvT{i}"])

                    sck = [f"sc{kb}" for kb in range(4)]

                    def IDX(j):
                        nv = 128 * (j + 1)
                        nlo = 128 * j + 64
                        for h in range(8):
                            hl, m = h % 2, h // 2
                            for kb in range((nv + 511) // 512):
                                c0 = kb * 512
                                cw = min(512, nv - c0)
                                ps, psk = nb()
                                mm(ps[:, 0:cw], qiT[64 * hl:64 * hl + 64, m, 128 * j:128 * (j + 1)], kiT[64 * hl:64 * hl + 64, c0:c0 + cw], r=[f"qiT{m}", "kiT"], w=[psk])
                                ri = (h * 4 + kb) % 2
                                act(rl[ri][:, 0:cw], ps[:, 0:cw], AF.Relu, r=[psk], w=[f"rl{ri}"])
                                if h == 0:
                                    ts("dve", sc[:, c0:c0 + cw], rl[ri][:, 0:cw], wia[:, j, 0:1], None, ALU.mult, r=[f"rl{ri}", f"wia{j}"], w=[f"sc{kb}"])
                                else:
                                    stt(sc[:, c0:c0 + cw], rl[ri][:, 0:cw], wia[:, j, h:h + 1], sc[:, c0:c0 + cw], ALU.mult, ALU.add, r=[f"rl{ri}", f"wia{j}", f"sc{kb}"], w=[f"sc{kb}"])
                        tt("dve", sc[:, 128 * j:128 * (j + 1)], sc[:, 128 * j:128 * (j + 1)], negvis[:], ALU.add, r=sck + ["negvis"], w=sck)
                        if j >= 2:
                            red(sm[:, 0:1], sc[:, 0:nlo], ALU.min, r=sck, w=["sm0"])
                            red(sm[:, 1:2], sc[:, 0:nv], ALU.max, r=sck, w=["sm1"])
                            tt("dve", sm[:, 2:3], sm[:, 1:2], sm[:, 0:1], ALU.subtract, r=["sm0", "sm1"], w=["sm2"])
                            stt(sm[:, 3:4], sm[:, 2:3], -0.5, sm[:, 0:1], ALU.mult, ALU.subtract, r=["sm2", "sm0"], w=["sm3"])

                    def BIS(j, k):
                        nv = 128 * (j + 1)
                        f = 2.0 ** -(k + 1)
                        nmn, nmnk = nm[j % 2], f"nm{j % 2}"
                        if k % 2 == 0:
                            act(nmn[:, 0:nv], sc[:, 0:nv], AF.Sign, r=sck + ["sm3"], w=[nmnk, "sm4"], bias=sm[:, 3:4], accum=sm[:, 4:5])
                            cthr = 510.5 - nv
                        else:
                            ts("dve", nmn[:, 0:nv], sc[:, 0:nv], sm[:, 3:4], None, ALU.is_ge, ALU.add, r=sck + ["sm3"], w=[nmnk, "sm4"], accum=sm[:, 4:5])
                            cthr = 255.5
                        stt(sm[:, 5:6], sm[:, 4:5], cthr, sm[:, 2:3], ALU.is_ge, ALU.mult, r=["sm4", "sm2"], w=["sm5"])
                        stt(sm[:, 0:1], sm[:, 5:6], f, sm[:, 0:1], ALU.mult, ALU.add, r=["sm5", "sm0"], w=["sm0"])
                        if k + 1 < NBIS:
                            if (k + 1) % 2 == 0:
                                stt(sm[:, 3:4], sm[:, 2:3], -0.5 * f, sm[:, 0:1], ALU.mult, ALU.subtract, r=["sm2", "sm0"], w=["sm3"])
                            else:
                                stt(sm[:, 3:4], sm[:, 2:3], 0.5 * f, sm[:, 0:1], ALU.mult, ALU.add, r=["sm2", "sm0"], w=["sm3"])

                    def NM(j):
                        nv = 128 * (j + 1)
                        thr, thrk = (sm[:, 0:1], "sm0") if j >= 2 else (thrc[:, 0:1], "thrc")
                        ts("dve", nm[j % 2][:, 0:nv], sc[:, 0:nv], thr, -32768.0, ALU.is_lt, ALU.mult, r=sck + [thrk], w=[f"nm{j % 2}"])

                    def ATT_qk(j, h):
                        nmj, nmk = nm[j % 2], f"nm{j % 2}"
                        pj = pT[h % 2]
                        pk = f"pT{h % 2}"
                        for g in range((j + 4) // 4):
                            ps, psk = nb()
                            tiles = list(range(4 * g, min(4 * g + 4, j + 1)))
                            for ii_, i in enumerate(tiles):
                                o_ = ps[:, ii_ * 128:(ii_ + 1) * 128]
                                near = (i >= j - 1)
                                mm(o_, ckvT[:, 128 * i:128 * (i + 1)], qlat[:, h, 128 * j:128 * (j + 1)], start=True, stop=False, r=[f"ckvT{i}", f"qlat{h}"], w=[psk])
                                mm(o_, nmj[:, 128 * i:128 * (i + 1)], idb[:], start=False, stop=(not near), r=[nmk, "idb"], w=[psk])
                                if near:
                                    mm(o_, idf[:], bpp[:, (j - i) * 8 + h, :], start=False, stop=True, r=["idf", "bpp"], w=[psk])
                            ncol = 128 * len(tiles)
                            act(pj[:, 512 * g:512 * g + ncol], ps[:, 0:ncol], AF.Exp, r=[psk, "cb"], w=[pk], bias=cb[:, h:h + 1], scale=0.125)

                    def ATT_pv(j, h):
                        ol, olk = olat[j % 2], f"olat{j % 2}"
                        pj = pT[h % 2]
                        pk = f"pT{h % 2}"
                        ps, psk = nb()
                        for i in range(j + 1):
                            mm(ps[:, 0:128], ckv[:, i, :], pj[:, 128 * i:128 * (i + 1)], start=(i == 0), stop=(i == j), r=[f"ckv{i}", pk], w=[psk])
                        for i in range(j + 1):
                            mm(ps[:, 128:256], onesb[:], pj[:, 128 * i:128 * (i + 1)], start=(i == 0), stop=(i == j), r=["onesb", pk], w=[psk])
                        rz = rzs[h % 2]
                        rzk = f"rzs{h % 2}"
                        P.op("dve", lambda e, rz=rz, ps=ps: e.reciprocal(out=rz[:], in_=ps[:, 128:256]), [psk], [rzk])
                        tt("dve", ol[:, h, :], ps[:, 0:128], rz[:], ALU.mult, r=[psk, rzk], w=[olk])

                    def ATT_tail(j):
                        ol, olk = olat[j % 2], f"olat{j % 2}"
                        for m in range(4):
                            ps, psk = nb()
                            mm(ps[:, 0:128], wuvp[:, 2 * m, :], ol[:, 2 * m, :], start=True, stop=False, r=["wuvp", olk], w=[psk])
                            mm(ps[:, 0:128], wuvp[:, 2 * m + 1, :], ol[:, 2 * m + 1, :], start=False, stop=True, r=["wuvp", olk], w=[psk])
                            cp("act", oTa[:, m, 128 * j:128 * (j + 1)], ps[:, 0:128], r=[psk], w=[f"oTa{j // 4}"])

                    if b == 0:
                        for r_ in range(16):
                            rs_ = slice(1024 * r_, 1024 * (r_ + 1))
                            P.dma("pool", "uvtab", lambda e, rs_=rs_: e.dma_start(out=uv[rs_, 0:1024], in_=u_table[rs_, :]), (), ["uvtab"])
                            P.dma("pool", "uvtab", lambda e, rs_=rs_: e.dma_start(out=uv[rs_, 1024:2048], in_=v_table[rs_, :]), (), ["uvtab"])
                    IDX(0)
                    NM(0)
                    for j in range(NT):
                        nxt = j + 1
                        if nxt < NT:
                            IDX(nxt)
                        ATT_qk(j, 0)
                        for h in range(8):
                            if h + 1 < 8:
                                ATT_qk(j, h + 1)
                            ATT_pv(j, h)
                            if nxt < NT and nxt >= 2:
                                for k in range(NBIS * h // 8, NBIS * (h + 1) // 8):
                                    BIS(nxt, k)
                        if nxt < NT:
                            NM(nxt)
                        ATT_tail(j)
                    P.barrier()

                if dbg == "dsa":
                    with ExitStack() as sd:
                        dtmp = T(sd, "dtmp", [128, 4, S], F32)
                        cp("dve", dtmp[:], oTa[:], w=["dtmp"])
                        dma("sp", "dbg", dbg_out.rearrange("(m p) t -> p m t", p=128), dtmp[:], r=["dtmp"])
                        P.barrier()
                    break

                with ExitStack() as sb:
                    qbT = T(sb, "qbT", [128, 2, S], BF16)
                    kT = T(sb, "kT", [128, 2, S], BF16)
                    sgT = T(sb, "sgT", [128, 2, S], BF16)
                    logf = T(sb, "logf", [128, NT, 256], F32)
                    kk = T(sb, "kk", [128, NT, 256], BF16)
                    iib = T(sb, "iib", [128, NT, 256], BF16)
                    w5 = [T(sb, f"w5{i}", [128, 8, 256], BF16) for i in range(2)]
                    sgtmp = [T(sb, f"sgtmp{i}", [128, 512], F32) for i in range(2)]
                    ftmp = [T(sb, f"ftmp{i}", [128, 256], F32) for i in range(2)]
                    e14 = [T(sb, f"e14{i}", [128, 256], F32) for i in range(4)]
                    e2 = [T(sb, f"e2{i}", [128, 128], F32) for i in range(4)]
                    e3 = [T(sb, f"e3{i}", [128, 128], F32) for i in range(4)]
                    dec = [T(sb, f"dec{i}", [128, 2], F32) for i in range(4)]
                    qin = [T(sb, f"qin{i}", [128, 128], BF16) for i in range(4)]
                    q2 = [T(sb, f"q2{i}", [128, 128], F32) for i in range(4)]
                    kin = [T(sb, f"kin{i}", [128, 128], BF16) for i in range(4)]
                    k3 = [T(sb, f"k3{i}", [128, 128], BF16) for i in range(4)]
                    am = [T(sb, f"am{i}", [128, 128], BF16) for i in range(4)]
                    s32 = [[T(sb, f"s32{i}{k}", [128, 128], F32) for k in range(2)] for i in range(2)]
                    sbf = [[T(sb, f"sbf{i}{k}", [128, 128], BF16) for k in range(2)] for i in range(2)]
                    sqb = [T(sb, f"sq{i}", [128, 128], BF16) for i in range(2)]
                    sd_ = [T(sb, f"sd{i}", [128, 128], F32) for i in range(2)]
                    o1 = [T(sb, f"o1{i}", [128, 128], F32) for i in range(2)]
                    for hp in range(2):
                        for hl in range(2):
                            h = 2 * hp + hl

                            def ev_q(tb, ps, psk, hl=hl):
                                cp("act" if tb % 2 == 0 else "dve", qbT[:, hl, tb * 512:(tb + 1) * 512], ps[:, :], r=[psk], w=[f"qbT{hl}"])
                            proj_fm(QB0 + h * 128, ev_q)

                            def ev_f(tb, ps, psk, hl=hl, h=h):
                                sg = sgtmp[tb % 2]
                                sgk = f"sgtmp{tb % 2}"
                                act(sg[:], ps[:, :], AF.Sigmoid, r=[psk], w=[sgk])
                                ts("dve", kT[:, hl, tb * 512:(tb + 1) * 512], sg[:], nomlf[:, h:h + 1], omlf[:, h:h + 1], ALU.mult, ALU.add, r=[sgk], w=[f"kT{hl}"])
                            proj_fm(FB0 + h * 128, ev_f)

                            def ev_g(tb, ps, psk, hl=hl):
                                act(sgT[:, hl, tb * 512:(tb + 1) * 512], ps[:, :], AF.Silu, r=[psk], w=[f"sgT{hl}"])
                            proj_fm(GB0 + h * 128, ev_g)
                        c2 = 2 * hp * 128
                        wload(w5[0][:, :, 0:128], w_in[:, FB0 + c2:FB0 + c2 + 128], 8, 128, "w50")
                        wload(w5[0][:, :, 128:256], w_in[:, FB0 + c2 + 128:FB0 + c2 + 256], 8, 128, "w50")
                        wload(w5[1][:, :, 0:128], w_in[:, IB0 + c2:IB0 + c2 + 128], 8, 128, "w51")
                        wload(w5[1][:, :, 128:256], w_in[:, IB0 + c2 + 128:IB0 + c2 + 256], 8, 128, "w51")
                        for i in range(NT):
                            ps, psk = nb()
                            for kc in range(8):
                                mm(ps[:, 0:256], xT[:, kc, 128 * i:128 * (i + 1)], w5[0][:, kc, :], start=(kc == 0), stop=(kc == 7), r=["w50", f"xT{i // 4}"], w=[psk])
                            for kc in range(8):
                                mm(ps[:, 256:512], xT[:, kc, 128 * i:128 * (i + 1)], w5[1][:, kc, :], start=(kc == 0), stop=(kc == 7), r=["w51", f"xT{i // 4}"], w=[psk])
                            ft = ftmp[i % 2]
                            fk = f"ftmp{i % 2}"
                            act(ft[:], ps[:, 0:256], AF.Sigmoid, r=[psk], w=[fk])
                            tt("dve", ft[:], ft[:], omlt[:, c2:c2 + 256], ALU.mult, r=[fk, "omlt"], w=[fk])
                            tt("dve", logf[:, i, :], ft[:], lbt[:, c2:c2 + 256], ALU.add, r=[fk, "lbt"], w=[f"logf{i}"])
                            ts("dve", kk[:, i, :], logf[:, i, :], -1.0, 1.0, ALU.mult, ALU.add, r=[f"logf{i}"], w=[f"kk{i}"])
                            cp("act", iib[:, i, :], ps[:, 256:512], r=[psk], w=[f"iib{i}"])
                        lfk = [f"logf{i}" for i in range(NT)]
                        for q4 in range(4):
                            act(logf[:, 4 * q4:4 * q4 + 4, :], logf[:, 4 * q4:4 * q4 + 4, :], AF.Ln, r=lfk[4 * q4:4 * q4 + 4], w=lfk[4 * q4:4 * q4 + 4])
                        for hl in range(2):
                            opk("dve", "memset", (), [f"s32{hl}0"], ap=s32[hl][0][:], constant=0.0)
                        rot["pool"] = [0, 1, 2, 3]
                        pob = {(0, 0): (psb[4], "ps4"), (0, 1): (psb[5], "ps5"),
                               (1, 0): (pst[0][:].bitcast(F32), "pst0"), (1, 1): (pst[1][:].bitcast(F32), "pst1")}

                        def HA(i, hl):
                            q = 2 * hl + (i % 2)
                            sfx = f"{hl}{i % 2}"
                            tsl = slice(128 * i, 128 * (i + 1))
                            lf = logf[:, i, hl * 128:(hl + 1) * 128]
                            pe_, pek = nb()
                            mm(pe_[:, 0:128], lf, cum[:, 0:128], r=[f"logf{i}", "cum"], w=[pek])
                            mm(pe_[:, 128:256], lf, cum[:, 128:256], r=[f"logf{i}", "cum"], w=[pek])
                            mm(pe_[:, 256:258], lf, cum[:, 256:258], r=[f"logf{i}", "cum"], w=[pek])
                            mm(pe_[:, 384:512], cum[:, 258:386], lf, r=[f"logf{i}", "cum"], w=[pek])
                            act(e14[q][:], pe_[:, 0:256], AF.Exp, r=[pek], w=[f"e14{sfx}"])
                            act(e2[q][:], pe_[:, 0:128], AF.Exp, r=[pek], w=[f"e2{sfx}"], scale=-1.0)
                            act(dec[q][:], pe_[:, 256:258], AF.Exp, r=[pek], w=[f"dec{sfx}"])
                            act(e3[q][:], pe_[:, 384:512], AF.Exp, r=[pek], w=[f"e3{sfx}"])
                            tt("dve", qin[q][:], qbT[:, hl, tsl], e14[q][:, 0:128], ALU.mult, r=[f"qbT{hl}", f"e14{sfx}"], w=[f"qin{sfx}"])
                            tt("dve", q2[q][:], qbT[:, hl, tsl], e14[q][:, 128:256], ALU.mult, r=[f"qbT{hl}", f"e14{sfx}"], w=[f"q2{sfx}"])
                            tt("dve", kin[q][:], kT[:, hl, tsl], e2[q][:], ALU.mult, r=[f"kT{hl}", f"e2{sfx}"], w=[f"kin{sfx}"])
                            tt("dve", k3[q][:], kk[:, i, hl * 128:(hl + 1) * 128], e3[q][:], ALU.mult, r=[f"kk{i}", f"e3{sfx}"], w=[f"k3{sfx}"])
                            pa, pak = nb()
                            mm(pa[:, 0:128], kin[q][:], qin[q][:], r=[f"kin{sfx}", f"qin{sfx}"], w=[pak])
                            tt("dve", am[q][:], pa[:, 0:128], cum[:, 128:256], ALU.mult, r=[pak, "cum"], w=[f"am{sfx}"])
                            po, pok = pob[(hl, i % 2)]
                            iv = iib[:, i, hl * 128:(hl + 1) * 128]
                            mm(po[:, 0:128], iv, am[q][:], start=True, stop=False, r=[f"iib{i}", f"am{sfx}"], w=[pok])

                        def HB(i):
                            tsl = slice(128 * i, 128 * (i + 1))
                            for c in range(2):
                                cur, nxt = c, 1 - c
                                for hl in range(2):
                                    q = 2 * hl + (i % 2)
                                    sfx = f"{hl}{i % 2}"
                                    po, pok = pob[(hl, i % 2)]
                                    mm(po[:, 64 * c:64 * c + 64], s32[hl][cur][:], q2[q][:, 64 * c:64 * c + 64], start=False, stop=(c == 1), r=[f"s32{hl}{cur}", f"q2{sfx}"], w=[pok])
                                    pss, pssk = nb()
                                    mm(pss[:, 0:128], k3[q][64 * c:64 * c + 64, :], iib[64 * c:64 * c + 64, i, hl * 128:(hl + 1) * 128], r=[f"k3{sfx}", f"iib{i}"], w=[pssk])
                                    stt(s32[hl][nxt][:], s32[hl][cur][:], dec[q][:, c:c + 1], pss[:, 0:128], ALU.mult, ALU.add, r=[f"s32{hl}{cur}", f"dec{sfx}", pssk], w=[f"s32{hl}{nxt}"])
                            for hl in range(2):
                                h = 2 * hp + hl
                                po, pok = pob[(hl, i % 2)]
                                act(sqb[hl][:], po[:, 0:128], AF.Square, r=[pok], w=[f"sq{hl}"])
                                mm(po[:, 128:256], onesb[:], sqb[hl][:], r=["onesb", f"sq{hl}"], w=[pok])
                                act(sd_[hl][:], po[:, 128:256], AF.Ln, r=[pok], w=[f"sd{hl}"], bias=RMS_EPS, scale=1.0 / 128.0)
                                act(sd_[hl][:], sd_[hl][:], AF.Exp, r=[f"sd{hl}"], w=[f"sd{hl}"], scale=-0.5)
                                tt("dve", o1[hl][:], po[:, 0:128], sd_[hl][:], ALU.mult, r=[pok, f"sd{hl}"], w=[f"o1{hl}"])
                                stt(oTb[:, h, tsl], o1[hl][:], bng[:, 0:1], sgT[:, hl, tsl], ALU.mult, ALU.mult, r=[f"o1{hl}", "bng", f"sgT{hl}"], w=[f"oTb{i // 4}"])

                        HA(0, 0)
                        HA(0, 1)
                        for i in range(NT):
                            if i + 1 < NT:
                                HA(i + 1, 0)
                                HA(i + 1, 1)
                            HB(i)
                        rot["pool"] = [0, 1, 2, 3, 4, 5]
                        P.barrier()

                if dbg == "hgrn":
                    with ExitStack() as sd:
                        dtmp = T(sd, "dtmp", [128, 4, S], F32)
                        cp("dve", dtmp[:], oTb[:], w=["dtmp"])
                        dma("sp", "dbg", dbg_out.rearrange("(m p) t -> p m t", p=128), dtmp[:], r=["dtmp"])
                        P.barrier()
                    break

                mgT = T(sq, "mgT", [128, 8, S], BF16)
                wpq = T(sq, "wpq", [128, 8, 2048], BF16)
                wo = T(sq, "wo", [128, 8, D], BF16)
                pre = [("wo", wo, w_o, c) for c in range(8)] + [("wpq", wpq, w_pq, c) for c in range(16)]
                with ExitStack() as sc_:
                    wbr = [T(sc_, f"wbr{i}", [128, 4, 128], BF16) for i in range(2)]
                    sga = [T(sc_, f"sga{i}", [128, 512], F32) for i in range(2)]
                    t1 = [T(sc_, f"t1{i}", [128, 512], F32) for i in range(2)]
                    for m in range(8):
                        wa, wak = nwt()
                        wload(wa[:], w_in[:, GA0 + m * 128:GA0 + (m + 1) * 128], 8, 128, wak)
                        wg, wgk = nwt()
                        wload(wg[:], w_in[:, GG0 + m * 128:GG0 + (m + 1) * 128], 8, 128, wgk)
                        wload(wbr[0][:], w_br_a[:, m * 128:(m + 1) * 128], 4, 128, "wbr0")
                        wload(wbr[1][:], w_br_b[:, m * 128:(m + 1) * 128], 4, 128, "wbr1")
                        for tb in range(4):
                            cs = slice(tb * 512, (tb + 1) * 512)
                            for br in range(2):
                                wgt, wgtk = (wa, wak) if br == 0 else (wg, wgk)
                                src = oTa if br == 0 else oTb
                                srck = f"oTa{tb}" if br == 0 else f"oTb{tb}"
                                ps, psk = nb()
                                for kc in range(8):
                                    mm(ps[:, :], wgt[:, kc, :], xT[:, kc, cs], start=(kc == 0), stop=(kc == 7), r=[wgtk, f"xT{tb}"], w=[psk])
                                act(sga[br][:], ps[:, :], AF.Sigmoid, r=[psk], w=[f"sga{br}"])
                                ps2, ps2k = nb()
                                for kc in range(4):
                                    mm(ps2[:, :], wbr[br][:, kc, :], src[:, kc, cs], start=(kc == 0), stop=(kc == 3), r=[f"wbr{br}", srck], w=[ps2k])
                                tt("dve", t1[br][:], ps2[:, :], sga[br][:], ALU.mult, r=[ps2k, f"sga{br}"], w=[f"t1{br}"])
                            tt("dve", mgT[:, m, cs], t1[0][:], t1[1][:], ALU.add, r=["t10", "t11"], w=[f"mgT{tb}"])
                        for _ in range(3):
                            key_, dst_, src_, c_ = pre.pop(0)
                            wload(dst_[:, :, c_ * 128:(c_ + 1) * 128], src_[:, c_ * 128:(c_ + 1) * 128], 8, 128, key_, eng=("act", "dve")[len(pre) % 2])
                    P.barrier()

                with ExitStack() as sd:
                    lnp = oTa[:].bitcast(F32)
                    xr, xrk = wstg[0][:].rearrange("p a b -> p (a b)"), "wstg0"
                    ot, otk = wstg[1][:].rearrange("p a b -> p (a b)"), "wstg1"
                    rr2, rr2k = wstg[2][:].rearrange("p a b -> p (a b)"), "wstg2"
                    pjunk = wt[0][:].rearrange("p a b -> p (a b)")
                    h1T, h1Tk = wt[1], "wt1"
                    h1 = [xT[:, 6 + i, :].bitcast(F32) for i in range(2)]
                    h1b = [T(sd, "h1b0", [128, D], BF16)[:], wt[2][:].rearrange("p a b -> p (a b)")]
                    h1bk = ["h1b0", "wt2"]
                    ssb = T(sd, "ssb", [128, 16, 128], F32)
                    rr = ssb[:, 0:8, :].rearrange("p a b -> p (a b)")
                    RRK = ["ssb0", "ssb1"]
                    m8 = T(sd, "m8", [128, 16, 16], F32)
                    ix = T(sd, "ix", [128, 16, 16], U32)
                    ixf = T(sd, "ixf", [128, 16, 16], F32)
                    cand = T(sd, "cand", [128, 8, 256], F32)
                    qpT = cand[:].bitcast(BF16)[:, 0:4, :].rearrange("p a (c t) -> p (a c) t", t=128)
                    CANDK = [f"cand{h}" for h in range(8)]
                    big = ssb[:].rearrange("p (h a) b -> p h (a b)", h=8)
                    BIGK = ["ssb0", "ssb1", "ssb2", "ssb3"]
                    t8 = T(sd, "t8", [128, 8, 16], F32)
                    px = T(sd, "px", [128, 8, 16], U32)
                    pf = T(sd, "pf", [128, 8, 16], F32)
                    pa_ = T(sd, "pa", [128, 8, 16], F32)
                    pb_ = T(sd, "pb", [128, 8, 16], F32)
                    i1s = T(sd, "i1s", [128, 8, 16], F32)
                    i2s = T(sd, "i2s", [128, 8, 16], F32)
                    idsf = T(sd, "idsf", [128, 128], F32)
                    ids = [T(sd, f"ids{i}", [128, 128], I32) for i in range(2)]
                    gate = [T(sd, f"gate{i}", [128, 8, 16], F32) for i in range(2)]
                    zz = T(sd, "zz", [128, 8], F32)
                    hd = T(sd, "hd", [128, 128], F32)
                    gh = T(sd, "gh", [128, 128], F32)
                    st6 = T(sd, "st6", [128, 2, 12], F32)
                    mv = T(sd, "mv", [128, 2, 4], F32)
                    gg = T(sd, "gg", [128, 128], F32)
                    uvb = [oTb[:, s_, :] for s_ in range(4)] + [xT[:, s_, :] for s_ in range(6)]
                    NSL = len(uvb)
                    dk = [T(sd, f"dk{i}", [128, 128], BF16) for i in range(4)]
                    opk("dve", "memset", (), ["hdinit"], ap=hd[:], constant=0.0)
                    opk("dve", "memset", (), ["mv0", "mv20", "mv30", "mv1", "mv21", "mv31"], ap=mv[:], constant=0.0)
                    dma("sp", "lnp", lnp[:], c_ln, w=["lnp"])
                    P.barrier()
                    rot["pool"] = [0, 1, 2, 3]
                    pf0, pf1 = psb[4], psb[5]

                    def layer_norm(dst, src, srck, gi, dstk):
                        q_ = gi // 2
                        s6, m4 = st6[:, q_, :], mv[:, q_, :]
                        ka, kb_, km, km2, km3 = f"st6a{q_}", f"st6b{q_}", f"mv{q_}", f"mv2{q_}", f"mv3{q_}"
                        srcl = list(srck) if isinstance(srck, (list, tuple)) else [srck]
                        P.op("dve", lambda e: e.bn_stats(out=s6[:, 0:6], in_=src[:, 0:512]), srcl, [ka])
                        P.op("dve", lambda e: e.bn_stats(out=s6[:, 6:12], in_=src[:, 512:1024]), srcl, [kb_])
                        P.op("dve", lambda e: e.bn_aggr(out=m4[:, 0:2], in_=s6), [ka, kb_], [km])
                        act(m4[:, 2:3], m4[:, 1:2], AF.Sqrt, r=[km], w=[km2], bias=LN_EPS)
                        P.op("dve", lambda e: e.reciprocal(out=m4[:, 3:4], in_=m4[:, 2:3]), [km2], [km3])
                        ts("dve", dst, src, m4[:, 0:1], m4[:, 3:4], ALU.subtract, ALU.mult, r=srcl + [km, km3], w=[dstk])
                        tt("dve", dst, dst, lnp[:, gi, :], ALU.mult, r=[dstk, "lnp"], w=[dstk])
                        tt("dve", dst, dst, lnp[:, gi + 1, :], ALU.add, r=[dstk, "lnp"], w=[dstk])

                    def front(i):
                        p = i % 2
                        tsl = slice(128 * i, 128 * (i + 1))
                        hk, hbk, idk, gk = f"h1{p}", h1bk[p], f"ids{p}", f"gate{p}"
                        dma("sp", xrk, xr, x[t0 + 128 * i:t0 + 128 * (i + 1), :], w=[xrk])
                        for nbk in range(2):
                            ps, psk = nb()
                            for kc in range(8):
                                mm(ps[:, :], mgT[:, kc, tsl], wo[:, kc, nbk * 512:(nbk + 1) * 512], start=(kc == 0), stop=(kc == 7), r=[f"mgT{i // 4}", "wo"], w=[psk])
                            stt(rr[:, nbk * 512:(nbk + 1) * 512], xr[:, nbk * 512:(nbk + 1) * 512], ALPHA, ps[:, :], ALU.mult, ALU.add, r=[xrk, psk], w=RRK)
                        layer_norm(h1[p], rr, RRK, 0, hk)
                        if dbg == "h1":
                            dma("sp", "dbgh1", dbg_out[128 * i:128 * (i + 1), :], h1[p], r=[hk])
                        cp("act", h1b[p], h1[p], r=[hk], w=[hbk])
                        pt, ptk = nbt()
                        for c in range(8):
                            tr(pt[:, c * 128:(c + 1) * 128], h1b[p][:, c * 128:(c + 1) * 128], idb[:], r=[hbk, "idb"], w=[ptk])
                        cp("act", h1T[:], pt[:].rearrange("p (c t) -> p c t", c=8), r=[ptk], w=[h1Tk])
                        for g in range(4):
                            ps, psk = nb()
                            for q_ in range(4):
                                ct = 4 * g + q_
                                for kc in range(8):
                                    mm(ps[:, q_ * 128:(q_ + 1) * 128], wpq[:, kc, ct * 128:(ct + 1) * 128], h1T[:, kc, :], start=(kc == 0), stop=(kc == 7), r=["wpq", h1Tk], w=[psk])
                            cp("act", qpT[:, 4 * g:4 * g + 4, :], ps[:, :].rearrange("p (c t) -> p c t", c=4), r=[psk], w=CANDK)
                        for g in range(4):
                            ps, psk = nb()
                            for q_ in range(4):
                                ct = 4 * g + q_
                                mm(ps[:, q_ * 128:(q_ + 1) * 128], qpT[:, ct, :], skT[:, ct % 2, :], r=CANDK + ["skT"], w=[psk])
                            cp("act", ssb[:, 4 * g:4 * g + 4, :], ps[:, :].rearrange("p (c t) -> p c t", c=4), r=[psk], w=[f"ssb{g}"])
                        for ph in range(5):
                            for ct in range(16):
                                g = ct // 4
                                if ph == 0:
                                    opk("dve", "max", [f"ssb{g}"], [f"m8a{ct}"], out=m8[:, ct, 0:8], in_=ssb[:, ct, :])
                                elif ph == 1:
                                    opk("dve", "max_index", [f"ssb{g}", f"m8a{ct}"], [f"ixa{ct}"], out=ix[:, ct, 0:8], in_max=m8[:, ct, 0:8], in_values=ssb[:, ct, :])
                                elif ph == 2:
                                    opk("dve", "match_replace", [f"ssb{g}", f"m8a{ct}", f"ixa{ct}"], [f"ssb{g}"], out=ssb[:, ct, :], in_to_replace=m8[:, ct, 0:8], in_values=ssb[:, ct, :], imm_value=NEG)
                                elif ph == 3:
                                    opk("dve", "max", [f"ssb{g}"], [f"m8b{ct}"], out=m8[:, ct, 8:16], in_=ssb[:, ct, :])
                                else:
                                    opk("dve", "max_index", [f"ssb{g}", f"m8b{ct}"], [f"ixb{ct}"], out=ix[:, ct, 8:16], in_max=m8[:, ct, 8:16], in_values=ssb[:, ct, :])
                        m8k = [f"m8a{ct}" for ct in range(16)] + [f"m8b{ct}" for ct in range(16)]
                        ixk = [f"ixa{ct}" for ct in range(16)] + [f"ixb{ct}" for ct in range(16)]
                        cp("dve", ixf[:], ix[:], r=ixk, w=["ixf"])
                        m8v = m8[:].rearrange("p (h two) a -> p h two a", two=2)
                        ixv = ixf[:].rearrange("p (h two) a -> p h two a", two=2)
                        candv = cand[:].rearrange("p h (a b) -> p h a b", a=16)
                        bigv = big[:].rearrange("p h (a b) -> p h a b", a=16)
                        tt("dve", candv, m8v[:, :, 0, :].unsqueeze(3).to_broadcast([128, 8, 16, 16]), m8v[:, :, 1, :].unsqueeze(2).to_broadcast([128, 8, 16, 16]), ALU.add, r=m8k, w=[f"cand{h}" for h in range(8)])
                        for ph in range(5):
                            for h in range(8):
                                if ph == 0:
                                    opk("dve", "max", [f"cand{h}"], [f"t8a{h}"], out=t8[:, h, 0:8], in_=cand[:, h, :])
                                elif ph == 1:
                                    opk("dve", "max_index", [f"cand{h}", f"t8a{h}"], [f"pxa{h}"], out=px[:, h, 0:8], in_max=t8[:, h, 0:8], in_values=cand[:, h, :])
                                elif ph == 2:
                                    opk("dve", "match_replace", [f"cand{h}", f"t8a{h}", f"pxa{h}"], [f"cand{h}"], out=cand[:, h, :], in_to_replace=t8[:, h, 0:8], in_values=cand[:, h, :], imm_value=NEG)
                                elif ph == 3:
                                    opk("dve", "max", [f"cand{h}"], [f"t8b{h}"], out=t8[:, h, 8:16], in_=cand[:, h, :])
                                else:
                                    opk("dve", "max_index", [f"cand{h}", f"t8b{h}"], [f"pxb{h}"], out=px[:, h, 8:16], in_max=t8[:, h, 8:16], in_values=cand[:, h, :])
                        t8k = [f"t8a{h}" for h in range(8)] + [f"t8b{h}" for h in range(8)]
                        pxk = [f"pxa{h}" for h in range(8)] + [f"pxb{h}" for h in range(8)]
                        cp("dve", pf[:], px[:], r=pxk, w=["pf"])
                        tt("dve", bigv, pf[:].unsqueeze(3).to_broadcast([128, 8, 16, 16]), iot[:, 16:32].unsqueeze(1).unsqueeze(1).to_broadcast([128, 8, 16, 16]), ALU.is_ge, r=["pf", "iot"], w=BIGK)
                        red(pa_[:], bigv, ALU.add, r=BIGK, w=["pa"])
                        stt(pb_[:], pa_[:], -16.0, pf[:], ALU.mult, ALU.add, r=["pa", "pf"], w=["pb"])
                        io16 = iot[:, 0:16].unsqueeze(1).unsqueeze(1).to_broadcast([128, 8, 16, 16])
                        for (src_, sel, selk, part) in ((pa_, i1s, "i1s", 0), (pb_, i2s, "i2s", 1)):
                            tt("dve", bigv, src_[:].unsqueeze(3).to_broadcast([128, 8, 16, 16]), io16, ALU.is_equal, r=["pa", "pb", "iot"], w=BIGK)
                            tt("dve", bigv, bigv, ixv[:, :, part, :].unsqueeze(2).to_broadcast([128, 8, 16, 16]), ALU.mult, r=BIGK + ["ixf"], w=BIGK)
                            red(sel[:], bigv, ALU.add, r=BIGK, w=[selk])
                        stt(idsf[:].rearrange("p (h j) -> p h j", h=8), i1s[:], 128.0, i2s[:], ALU.mult, ALU.add, r=["i1s", "i2s"], w=["idsf"])
                        ts("dve", idsf[:], idsf[:], 0.0, 16383.0, ALU.max, ALU.min, r=["idsf"], w=["idsf"])
                        cp("dve", ids[p][:], idsf[:], r=["idsf"], w=[idk])
                        tt("dve", gate[p][:], t8[:], t8[:, :, 0:1].to_broadcast([128, 8, 16]), ALU.subtract, r=t8k, w=[gk])
                        act(gate[p][:], gate[p][:], AF.Exp, r=[gk], w=[gk])
                        red(zz[:], gate[p][:], ALU.add, r=[gk], w=["zz"])
                        opk("dve", "reciprocal", ["zz"], ["zz"], out=zz[:], in_=zz[:])
                        tt("dve", gate[p][:], gate[p][:], zz[:].unsqueeze(2).to_broadcast([128, 8, 16]), ALU.mult, r=[gk, "zz"], w=[gk])

                    def back(i):
                        p = i % 2
                        hk, hbk, idk, gk = f"h1{p}", h1bk[p], f"ids{p}", f"gate{p}"
                        gatef = gate[p][:].rearrange("p h j -> p (h j)")
                        for k in range(129):
                            if k < 128:
                                s_ = k % NSL
                                gather(f"uv{s_}", uvb[s_], uv[:, :], ids[p][:, k:k + 1], r=[idk], w=[f"uv{s_}"])
                                stt(pjunk, uvb[s_][:, 0:1024], 1.0, h1b[p], ALU.mult, ALU.mult, r=[f"uv{s_}", hbk, "hdinit"], w=[f"hd{k % 8}"], accum=hd[:, k:k + 1])
                                act(gh[:, k:k + 1], hd[:, k:k + 1], AF.Gelu, r=[f"hd{k % 8}"], w=[f"gh{k % 8}"])
                            if k >= 1:
                                k1 = k - 1
                                s_ = k1 % NSL
                                di = k1 % 4
                                act(gg[:, k1:k1 + 1], gh[:, k1:k1 + 1], AF.Copy, r=[f"gh{k1 % 8}", gk], w=[f"gg{k1 % 8}"], scale=gatef[:, k1:k1 + 1])
                                act(dk[di][:], idb[:], AF.Copy, r=["idb", f"gg{k1 % 8}"], w=[f"dk{di}"], scale=gg[:, k1:k1 + 1])
                                mm(pf0[:, :], dk[di][:], uvb[s_][:, 1024:1536], start=(k1 == 0), stop=(k1 == 127), r=[f"dk{di}", f"uv{s_}"], w=["pf0"])
                                mm(pf1[:, :], dk[di][:], uvb[s_][:, 1536:2048], start=(k1 == 0), stop=(k1 == 127), r=[f"dk{di}", f"uv{s_}"], w=["pf1"])
                        stt(rr2[:, 0:512], h1[p][:, 0:512], ALPHA, pf0[:, :], ALU.mult, ALU.add, r=[hk, "pf0"], w=[rr2k])
                        stt(rr2[:, 512:1024], h1[p][:, 512:1024], ALPHA, pf1[:, :], ALU.mult, ALU.add, r=[hk, "pf1"], w=[rr2k])
                        layer_norm(ot, rr2, rr2k, 2, otk)
                        dma("sp", otk, y[t0 + 128 * i:t0 + 128 * (i + 1), :], ot, r=[otk])

                    def capture(fn_, *a):
                        P.cap = []
                        fn_(*a)
                        lst = P.cap
                        P.cap = None
                        return lst

                    P.replay(capture(front, 0))
                    for i in range(NT):
                        bl = capture(back, i)
                        fl = capture(front, i + 1) if i + 1 < NT else []
                        merged = []
                        fi = 0
                        nb_ = max(1, len(bl) - 60)
                        for bi, it in enumerate(bl):
                            merged.append(it)
                            want = min(len(fl), (len(fl) * (bi + 1)) // nb_)
                            while fi < want:
                                merged.append(fl[fi])
                                fi += 1
                        merged.extend(fl[fi:])
                        P.replay(merged)
                    rot["pool"] = [0, 1, 2, 3, 4, 5]
                    P.barrier()
        P.barrier()
        P.emit()
    return nc


def _t5_bucket_np(rel):
    half = 16
    max_exact = 8
    rel = jnp.asarray(rel, jnp.int32)
    base = jnp.where(rel > 0, half, 0)
    n = jnp.abs(rel)
    nf = jnp.maximum(n, 1).astype(jnp.float32)
    large = max_exact + (jnp.log(nf / max_exact) / math.log(128 / max_exact) * (half - max_exact)).astype(jnp.int32)
    large = jnp.minimum(large, half - 1)
    return np.asarray(base + jnp.where(n < max_exact, n, large))


def host_consts(inp):
    f32 = np.float32
    c = {}
    c["c_ident"] = np.eye(128, dtype=f32)
    s = np.arange(128)[:, None]
    t = np.arange(128)[None, :]
    same = (s // 64) == (t // 64)
    a3 = (same & (s <= t)).astype(f32)
    ref = (same & ((s % 64) <= 31)).astype(f32)
    a1 = a3 - ref
    a2 = same.astype(f32) - a3
    ind = np.stack([(np.arange(128) // 64 == 0), (np.arange(128) // 64 == 1)], axis=1).astype(f32)
    c["c_cum"] = np.concatenate([a1, a3, ind, a2], axis=1).astype(f32)
    tq = np.arange(128)[:, None]
    sk = np.arange(128)[None, :]
    c["c_negvis"] = np.where((tq < 64) & (sk >= 64), NEG, 0.0).astype(f32)
    with jax.default_device(jax.devices("cpu")[0]):
        kk_ = np.arange(128)[:, None]
        tt_ = np.arange(128)[None, :]
        bk = [_t5_bucket_np(kk_ - tt_ - 128 * d) for d in range(2)]
    rb = np.asarray(inp["rel_bias"], f32)
    bn = np.zeros((128, 16, 128), f32)
    for d in range(2):
        g = rb[bk[d]]
        bn[:, d * 8:(d + 1) * 8, :] = np.transpose(g, (0, 2, 1))
    c["c_bnear"] = bn
    c["c_cb"] = np.broadcast_to(rb[15][None, :], (128, 8)).astype(f32).copy()
    wuk = np.asarray(inp["w_uk"][0], f32)
    c["c_wukT"] = np.ascontiguousarray(np.transpose(wuk, (1, 2, 0)).reshape(4, 128, 128).transpose(1, 0, 2))
    wuv = np.asarray(inp["w_uv"][0], f32)
    wp = np.zeros((128, 8, 128), f32)
    for h in range(8):
        wp[:, h, (h % 2) * 64:(h % 2) * 64 + 64] = wuv[:, h, :]
    c["c_wuvp"] = wp
    lbp = np.asarray(inp["lb_params"], f32)
    c["c_lbbc"] = np.broadcast_to(lbp[None], (128, 2, 512)).astype(f32).copy()
    c["c_lbfm"] = np.ascontiguousarray(lbp.reshape(2, 4, 128).transpose(2, 0, 1))
    c["c_kvg"] = np.broadcast_to(np.asarray(inp["kv_norm_g"][0], f32)[None, :], (128, 128)).copy()
    c["c_bng"] = np.asarray(inp["b_norm_g"][0], f32).reshape(128, 1).copy()
    ln = np.stack([inp["ln1_g"][0], inp["ln1_b"][0], inp["ln2_g"][0], inp["ln2_b"][0]], axis=0).astype(f32)
    c["c_ln"] = np.broadcast_to(ln[None], (128, 4, D)).copy()
    c["c_skT"] = np.ascontiguousarray(np.stack([np.asarray(inp["sub_keys1"][0], f32).T, np.asarray(inp["sub_keys2"][0], f32).T], axis=1))
    io = np.concatenate([np.arange(16), 16 * (np.arange(16) + 1)]).astype(f32)
    c["c_iota"] = np.broadcast_to(io[None, :], (128, 32)).copy()
    return c


def make_in_maps(inp, ncores, nseq):
    c = host_consts(inp)
    shared = dict(c)
    shared["w_in"] = np.ascontiguousarray(inp["w_in"][0], dtype=np.float32)
    shared["w_br_a"] = np.ascontiguousarray(inp["w_br_a"][0], dtype=np.float32)
    shared["w_br_b"] = np.ascontiguousarray(inp["w_br_b"][0], dtype=np.float32)
    shared["w_o"] = np.ascontiguousarray(inp["w_o"][0], dtype=np.float32)
    shared["w_pq"] = np.ascontiguousarray(inp["w_pq"][0], dtype=np.float32)
    shared["u_table"] = np.ascontiguousarray(inp["u_table"][0], dtype=np.float32)
    shared["v_table"] = np.ascontiguousarray(inp["v_table"][0], dtype=np.float32)
    maps = []
    xx = np.asarray(inp["x"], dtype=np.float32)
    for ci in range(ncores):
        m = dict(shared)
        m["x"] = np.ascontiguousarray(xx[ci * nseq:(ci + 1) * nseq].reshape(nseq * S, D))
        maps.append(m)
    return maps


def kernel(**inputs):
    nseq = 32 // NCORES
    nc = build(nseq)
    maps = make_in_maps(inputs, NCORES, nseq)
    res = run_bass_kernel_spmd(nc, maps, core_ids=list(range(NCORES)))
    out = np.concatenate([r["y"].reshape(nseq, S, D) for r in res.results], axis=0)
    return out.astype(np.float32)
```
